# Optimizing a Trainium2 kernel written in Bass

```python
import math
import jax, jax.numpy as jnp
from jax import lax
import numpy as np

D_MODEL = 1024
BATCH = 4
SEQ = 4096
DEPTH = 1

CHUNK = 64
Q_BLOCK = 128
HEAD_DIM = 64
SB_HEADS = 8
SB_WIDTH = SB_HEADS * HEAD_DIM
DF_HEADS = 4
DF_QK_WIDTH = DF_HEADS * 2 * HEAD_DIM
DF_V_DIM = 2 * HEAD_DIM
DF_V_WIDTH = DF_HEADS * DF_V_DIM
IN_WIDTH = 3 * SB_WIDTH + 2 * DF_QK_WIDTH + DF_V_WIDTH
D_FF = 4 * D_MODEL
ROPE_THETA = 10000.0
EPS = 1e-6
NEG_INF = -1e30

kernel_name = "hybrid_stickbreak_diffattn_block"


def rmsnorm(x, g):
    xf = x.astype(jnp.float32)
    y = xf * lax.rsqrt(jnp.mean(xf * xf, axis=-1, keepdims=True) + EPS)
    return (y * g.astype(jnp.float32)).astype(x.dtype)


def rope(x, pos):
    d = x.shape[-1]
    inv_freq = ROPE_THETA ** (-jnp.arange(0, d, 2, dtype=jnp.float32) / d)
    ang = pos.astype(jnp.float32)[..., None] * inv_freq
    cos = jnp.cos(ang)[:, :, None, :]
    sin = jnp.sin(ang)[:, :, None, :]
    x1, x2 = x[..., : d // 2], x[..., d // 2:]
    return jnp.concatenate([x1 * cos - x2 * sin, x1 * sin + x2 * cos], axis=-1)


def stick_breaking_attention(q, k, v):
    b, s, h, d = q.shape
    nb = s // Q_BLOCK
    qf = q.astype(jnp.float32) * (d ** -0.5)
    kf = k.astype(jnp.float32)
    vf = v.astype(jnp.float32)
    k_idx = jnp.arange(s)

    def block(i):
        qb = lax.dynamic_slice_in_dim(qf, i * Q_BLOCK, Q_BLOCK, axis=1)
        q_idx = i * Q_BLOCK + jnp.arange(Q_BLOCK)
        z = jnp.einsum('bqhd,bkhd->bhqk', qb, kf)
        causal = k_idx[None, :] < q_idx[:, None]
        log_1m = jnp.where(causal, jax.nn.log_sigmoid(-z), 0.0)
        after = lax.cumsum(log_1m, axis=3, reverse=True) - log_1m
        w = jnp.where(causal, jnp.exp(jax.nn.log_sigmoid(z) + after), 0.0)
        return jnp.einsum('bhqk,bkhd->bqhd', w, vf)

    out = lax.map(block, jnp.arange(nb))
    return out.transpose(1, 0, 2, 3, 4).reshape(b, s, h, d)


def differential_attention(q1, q2, k1, k2, v, lam):
    b, s, h, d = q1.shape
    nb = s // Q_BLOCK
    scale = d ** -0.5
    vf = v.astype(jnp.float32)
    k_chunk = jnp.arange(s) // CHUNK

    def block(i):
        qb1 = lax.dynamic_slice_in_dim(q1, i * Q_BLOCK, Q_BLOCK, axis=1) * scale
        qb2 = lax.dynamic_slice_in_dim(q2, i * Q_BLOCK, Q_BLOCK, axis=1) * scale
        q_chunk = (i * Q_BLOCK + jnp.arange(Q_BLOCK)) // CHUNK
        allowed = k_chunk[None, :] <= q_chunk[:, None]
        s1 = jnp.where(allowed, jnp.einsum('bqhd,bkhd->bhqk', qb1, k1), NEG_INF)
        s2 = jnp.where(allowed, jnp.einsum('bqhd,bkhd->bhqk', qb2, k2), NEG_INF)
        a = jax.nn.softmax(s1, axis=-1) - lam * jax.nn.softmax(s2, axis=-1)
        return jnp.einsum('bhqk,bkhe->bqhe', a, vf)

    out = lax.map(block, jnp.arange(nb))
    return out.transpose(1, 0, 2, 3, 4).reshape(b, s, h, 2 * d)


def setup_inputs(seed: int = 0) -> dict:
    key = jax.random.key(seed)
    ks = jax.random.split(key, 24)
    f32 = jnp.float32

    def normal(k, shape, std):
        return jax.random.normal(k, shape, f32) * std

    def gain(k, shape):
        return 1.0 + 0.05 * jax.random.normal(k, shape, f32)

    return {
        "x": jax.random.normal(ks[0], (BATCH, SEQ, D_MODEL), f32),
        "c": jax.random.normal(ks[1], (BATCH, D_MODEL), f32),
        "positions": (jnp.arange(SEQ, dtype=jnp.int32)[None, :]
                      + jax.random.randint(ks[2], (BATCH, 1), 0, 4096, dtype=jnp.int32)),
        "w_ada": normal(ks[3], (DEPTH, D_MODEL, 6 * D_MODEL), D_MODEL ** -0.5),
        "b_ada": normal(ks[4], (DEPTH, 6 * D_MODEL), 0.02),
        "g_pre_mix": gain(ks[5], (DEPTH, D_MODEL)),
        "w_in": normal(ks[6], (DEPTH, D_MODEL, IN_WIDTH), D_MODEL ** -0.5),
        "lambda_q1": normal(ks[7], (DEPTH, HEAD_DIM), 0.1),
        "lambda_k1": normal(ks[8], (DEPTH, HEAD_DIM), 0.1),
        "lambda_q2": normal(ks[9], (DEPTH, HEAD_DIM), 0.1),
        "lambda_k2": normal(ks[10], (DEPTH, HEAD_DIM), 0.1),
        "g_subln": gain(ks[11], (DEPTH, DF_V_DIM)),
        "w_branch_sb": normal(ks[12], (DEPTH, SB_WIDTH, D_MODEL), SB_WIDTH ** -0.5),
        "w_branch_df": normal(ks[13], (DEPTH, DF_V_WIDTH, D_MODEL), DF_V_WIDTH ** -0.5),
        "w_gate": normal(ks[14], (DEPTH, D_MODEL, 2 * D_MODEL), D_MODEL ** -0.5),
        "b_gate": normal(ks[15], (DEPTH, 2 * D_MODEL), 0.02),
        "w_out": normal(ks[16], (DEPTH, D_MODEL, D_MODEL), D_MODEL ** -0.5),
        "g_post_mix": gain(ks[17], (DEPTH, D_MODEL)),
        "g_pre_ffn": gain(ks[18], (DEPTH, D_MODEL)),
        "w_ff1": normal(ks[19], (DEPTH, D_MODEL, D_FF), D_MODEL ** -0.5),
        "w_ff2": normal(ks[20], (DEPTH, D_FF, D_MODEL), D_FF ** -0.5),
        "g_post_ffn": gain(ks[21], (DEPTH, D_MODEL)),
    }


def reference(x, c, positions, w_ada, b_ada, g_pre_mix, w_in, lambda_q1, lambda_k1,
              lambda_q2, lambda_k2, g_subln, w_branch_sb, w_branch_df, w_gate, b_gate,
              w_out, g_post_mix, g_pre_ffn, w_ff1, w_ff2, g_post_ffn):
    b, s, _ = x.shape
    for l in range(DEPTH):
        lambda_init = 0.8 - 0.6 * math.exp(-0.3 * l)
        mod = jnp.einsum('bd,de->be', jax.nn.silu(c), w_ada[l]) + b_ada[l]
        sh1, sc1, gt1, sh2, sc2, gt2 = jnp.split(mod[:, None, :], 6, axis=-1)

        h = rmsnorm(x, g_pre_mix[l]) * (1.0 + sc1) + sh1
        proj = jnp.einsum('bsd,de->bse', h, w_in[l])
        q_sb, k_sb, v_sb, q_df, k_df, v_df = jnp.split(
            proj, np.cumsum([SB_WIDTH, SB_WIDTH, SB_WIDTH, DF_QK_WIDTH, DF_QK_WIDTH]).tolist(), axis=-1)

        y_sb = stick_breaking_attention(q_sb.reshape(b, s, SB_HEADS, HEAD_DIM),
                                        k_sb.reshape(b, s, SB_HEADS, HEAD_DIM),
                                        v_sb.reshape(b, s, SB_HEADS, HEAD_DIM))
        y_sb = y_sb.reshape(b, s, SB_WIDTH).astype(x.dtype)

        qd = rope(q_df.astype(jnp.float32).reshape(b, s, 2 * DF_HEADS, HEAD_DIM), positions)
        kd = rope(k_df.astype(jnp.float32).reshape(b, s, 2 * DF_HEADS, HEAD_DIM), positions)
        qd = qd.reshape(b, s, DF_HEADS, 2, HEAD_DIM)
        kd = kd.reshape(b, s, DF_HEADS, 2, HEAD_DIM)
        lam = (jnp.exp(jnp.sum(lambda_q1[l].astype(jnp.float32) * lambda_k1[l].astype(jnp.float32)))
               - jnp.exp(jnp.sum(lambda_q2[l].astype(jnp.float32) * lambda_k2[l].astype(jnp.float32)))
               + lambda_init)
        y_df = differential_attention(qd[..., 0, :], qd[..., 1, :], kd[..., 0, :], kd[..., 1, :],
                                      v_df.reshape(b, s, DF_HEADS, DF_V_DIM), lam)
        y_df = rmsnorm(y_df, g_subln[l]) * (1.0 - lambda_init)
        y_df = y_df.reshape(b, s, DF_V_WIDTH).astype(x.dtype)

        gates = jax.nn.sigmoid(jnp.einsum('bsd,de->bse', h, w_gate[l]) + b_gate[l])
        g_sb, g_df = jnp.split(gates, 2, axis=-1)
        merged = (g_sb * jnp.einsum('bse,ed->bsd', y_sb, w_branch_sb[l])
                  + g_df * jnp.einsum('bse,ed->bsd', y_df, w_branch_df[l]))
        m = jnp.einsum('bsd,de->bse', merged, w_out[l])
        x = x + gt1 * rmsnorm(m, g_post_mix[l])

        h2 = rmsnorm(x, g_pre_ffn[l]) * (1.0 + sc2) + sh2
        f = jnp.square(jax.nn.relu(jnp.einsum('bsd,df->bsf', h2, w_ff1[l])))
        f = jnp.einsum('bsf,fd->bsd', f, w_ff2[l])
        x = x + gt2 * rmsnorm(f, g_post_ffn[l])
    return x
```

```python
import math
from contextlib import ExitStack

import numpy as np
import ml_dtypes

import concourse.bass as bass
import concourse.mybir as mybir
from concourse.bass_utils import run_bass_kernel_spmd

F32 = mybir.dt.float32
BF16 = mybir.dt.bfloat16
I32 = mybir.dt.int32
AF = mybir.ActivationFunctionType
ALU = mybir.AluOpType

D = 1024
DC = 8
EPS = 1e-6
NEG = -30000.0
NDS = 24
LAMBDA_INIT = 0.8 - 0.6 * math.exp(0.0)


class Res:
    __slots__ = ("w", "r")

    def __init__(self):
        self.w = None
        self.r = {}


class Sched:
    def __init__(self, nc, es):
        self.nc = nc
        self.eng = {}
        for name, h in (("pe", nc.tensor), ("act", nc.scalar), ("dve", nc.vector),
                        ("pool", nc.gpsimd), ("sp", nc.sync)):
            sem = es.enter_context(nc.semaphore("cnt_" + name))
            self.eng[name] = dict(h=h, sem=sem, n=0, seen={})
        self.es = es
        self.nsw = 0
        self.dsems = [es.enter_context(nc.semaphore("dma%d" % i)) for i in range(NDS)]
        self.dcnt = [0] * NDS
        self.dnext = 0
        self.swd = []
        self.res = {}

    def R(self, *key):
        r = self.res.get(key)
        if r is None:
            r = self.res[key] = Res()
        return r

    def _wait(self, ename, dep):
        key, sem, val = dep
        e = self.eng[ename]
        if e["seen"].get(key, 0) >= val:
            return
        e["h"].wait_ge(sem, val)
        e["seen"][key] = val

    def _sync(self, ename, reads, writes):
        for t in reads:
            if t.w is not None and not (ename == "pe" and t.w[0] == "pe"):
                self._wait(ename, t.w)
        for t in writes:
            if t.w is not None and not (ename == "pe" and t.w[0] == "pe"):
                self._wait(ename, t.w)
            for k, dep in t.r.items():
                if k != ename:
                    self._wait(ename, dep)

    def _mark(self, me, reads, writes):
        for t in reads:
            t.r[me[0]] = me
        for t in writes:
            t.w = me
            t.r = {}

    def op(self, ename, fn, reads=(), writes=()):
        e = self.eng[ename]
        self._sync(ename, reads, writes)
        ins = fn(e["h"])
        e["n"] += 1
        ins.then_inc(e["sem"], 1)
        me = (ename, e["sem"], e["n"])
        self._mark(me, reads, writes)
        return me

    def pe(self, fns, reads=(), writes=()):
        e = self.eng["pe"]
        self._sync("pe", reads, writes)
        ins = None
        for f in fns:
            ins = f(e["h"])
        e["n"] += 1
        ins.then_inc(e["sem"], 1)
        me = ("pe", e["sem"], e["n"])
        self._mark(me, reads, writes)
        return me

    def dma(self, ename, out, in_, reads=(), writes=()):
        e = self.eng[ename]
        self._sync(ename, reads, writes)
        if ename == "pool":
            sem = self.es.enter_context(self.nc.semaphore("swd%d" % self.nsw))
            key = "swd%d" % self.nsw
            self.nsw += 1
            ins = e["h"].dma_start(out=out, in_=in_)
            ins.then_inc(sem, 16)
            me = (key, sem, 16)
            self._mark(me, reads, writes)
            self.swd.append(me)
            return me
        k = self.dnext
        self.dnext = (k + 1) % NDS
        if self.dcnt[k] > 0:
            self._wait(ename, ("dma%d" % k, self.dsems[k], self.dcnt[k]))
        ins = e["h"].dma_start(out=out, in_=in_)
        self.dcnt[k] += 16
        ins.then_inc(self.dsems[k], 16)
        me = ("dma%d" % k, self.dsems[k], self.dcnt[k])
        self._mark(me, reads, writes)
        return me

    def barrier(self):
        names = list(self.eng)
        for a in names:
            for b in names:
                if a != b and self.eng[b]["n"] > 0:
                    self._wait(a, (b, self.eng[b]["sem"], self.eng[b]["n"]))
            for k in range(NDS):
                if self.dcnt[k] > 0:
                    self._wait(a, ("dma%d" % k, self.dsems[k], self.dcnt[k]))
            for dep in self.swd:
                self._wait(a, dep)


def build(S):
    NBLK = S // 128
    NOWN = NBLK // 2
    SO = NOWN * 128
    NT_ALL = S // 512
    NT_OWN = SO // 512
    assert SO % 512 == 0

    nc = bass.Bass("TRN2", target_bir_lowering=False)

    def din(name, shape, dt=F32):
        return nc.dram_tensor(name, list(shape), dt, kind="ExternalInput").ap()

    x_p = din("x_p", [S, D])
    pos_rep = din("pos_rep", [128, S], I32)
    c_col = din("c_col", [128, 8])
    w_ada = din("w_ada", [D, 6 * D])
    bada_col = din("bada_col", [128, 48])
    bada_rep = din("bada_rep", [128, 6 * D])
    gpre_col = din("gpre_col", [128, 16])
    gpost_rep = din("gpost_rep", [128, 2 * D])
    w_in = din("w_in", [D, 3072])
    lam_rep = din("lam_rep", [128, 256])
    gsub_rep = din("gsub_rep", [128, 128])
    w_bsb = din("w_bsb", [512, D])
    w_bdf = din("w_bdf", [512, D])
    w_gate = din("w_gate", [D, 2 * D])
    bgate_col = din("bgate_col", [128, 16])
    w_out = din("w_out", [D, D])
    w_ff1 = din("w_ff1", [D, 4 * D])
    w_ff2 = din("w_ff2", [4 * D, D])
    cb = din("cb", [128, 5 * 128 + 3 * 512], BF16)
    cf = din("cf", [128, 260])
    out = nc.dram_tensor("out", [SO, D], F32, kind="ExternalOutput").ap()

    with ExitStack() as es:
        sc = Sched(nc, es)
        R = sc.R

        def sb(name, shape, dt=F32, stack=es):
            return stack.enter_context(nc.sbuf_tensor(name, list(shape), dt))

        ps = [es.enter_context(nc.psum_tensor("ps%d" % i, [128, 512], F32)) for i in range(8)]
        PS = [R("ps", i) for i in range(8)]

        cbt = sb("cbt", [128, 5 * 128 + 3 * 512], BF16)
        cft = sb("cft", [128, 260])
        ident_b = cbt[:, 0:128]
        negtri = cbt[:, 128:256]
        negones = cbt[:, 256:384]
        rot = cbt[:, 384:512]
        mk_sb = cbt[:, 640:1152]
        mk_df = cbt[:, 1152:1664]
        mk_dm = cbt[:, 1664:2176]
        ones_f = cft[:, 0:128]
        invf = cft[:, 256:257]
        negpi = cft[:, 257:258]
        neghalf = cft[:, 258:259]
        epsc = cft[:, 259:260]
        lnmul = cft[:, 128:129]
        modc = sb("modc", [128, 32])
        gtg = sb("gtg", [128, 2 * D])
        bgc = sb("bgc", [128, 16])
        gsub = sb("gsub", [128, 128])
        lamc = sb("lamc", [128, 4])
        bufX = sb("bufX", [128, DC * SO], BF16)
        bufY = sb("bufY", [128, DC * SO], BF16)
        hT_oth = bufX[:, :].rearrange("p (a s) -> p a s", a=DC)
        h2T = hT_oth
        mergedT = bufY[:, :].rearrange("p (a s) -> p a s", a=DC)
        small = sb("small", [128, 64])
        scol = sb("scol", [128, 8])
        bcol = sb("bcol", [128, 48])
        gpc = sb("gpc", [128, 16])
        esP1 = ExitStack()
        esP1.__enter__()
        hT_own = sb("hT_own", [128, DC, SO], BF16, esP1)
        ysbT = sb("ysbT", [128, 4, SO], BF16, esP1)
        ydfT = sb("ydfT", [128, 4, SO], BF16, esP1)
        RC = R("const")

        sc.dma("sp", cbt[:], cb, writes=[RC])
        sc.dma("sp", cft[:], cf, writes=[RC])
        sc.dma("sp", bgc[:], bgate_col, writes=[RC])
        sc.dma("sp", gsub[:], gsub_rep, writes=[RC])

        def hT(blk):
            if blk < NOWN:
                return hT_own, blk * 128
            return hT_oth, (blk - NOWN) * 128

        def hTtile(tt):
            if tt < NT_OWN:
                return hT_own, tt * 512
            return hT_oth, (tt - NT_OWN) * 512

        def rstd_op(rs_ap, ssq_ap, n, rres, sres, mul=1.0, on_act=True):
            if not on_act:
                sc.op("dve", lambda h: h.tensor_scalar(rs_ap, ssq_ap, 1.0 / n, EPS, ALU.mult, ALU.add),
                      reads=[sres], writes=[rres])
                sc.op("pool", lambda h: h.tensor_tensor(rs_ap, rs_ap, neghalf, ALU.pow),
                      reads=[rres, RC], writes=[rres])
                if mul != 1.0:
                    sc.op("pool", lambda h: h.tensor_scalar(rs_ap, rs_ap, mul, None, ALU.mult),
                          reads=[rres], writes=[rres])
                return
            sc.op("act", lambda h: h.activation(rs_ap, ssq_ap, AF.Ln, bias=epsc, scale=1.0 / n),
                  reads=[sres, RC], writes=[rres])
            if mul == 1.0:
                sc.op("act", lambda h: h.activation(rs_ap, rs_ap, AF.Exp, scale=-0.5),
                      reads=[rres], writes=[rres])
            else:
                sc.op("act", lambda h: h.activation(rs_ap, rs_ap, AF.Exp, bias=lnmul, scale=-0.5),
                      reads=[rres, RC], writes=[rres])

        with ExitStack() as es_att:

            with ExitStack() as es0:
                ccol = sb("ccol", [128, 8], F32, es0)
                wst = [sb("wst%d" % i, [128, 8, 512], F32, es0) for i in range(2)]
                accs = [sb("macc%d" % i, [128, 512], F32, es0) for i in range(2)]
                brow = [sb("brow%d" % i, [128, 512], F32, es0) for i in range(2)]
                gprow = [sb("gprow%d" % i, [128, 512], F32, es0) for i in range(2)]
                lamt = sb("lamt", [128, 256], F32, es0)
                lamj = sb("lamj", [128, 64], F32, es0)
                sc.dma("sp", ccol[:], c_col, writes=[R("ccol")])
                sc.dma("sp", bcol[:], bada_col, writes=[RC])
                sc.dma("sp", gpc[:], gpre_col, writes=[RC])
                sc.dma("sp", lamt[:], lam_rep, writes=[RC])
                sc.barrier()
                sc.op("act", lambda h: h.activation(scol[:], ccol[:], AF.Silu),
                      reads=[R("ccol")], writes=[R("scol")])
                for q in range(2):
                    sc.op("dve", lambda h, q=q: h.scalar_tensor_tensor(
                        lamj[:], lamt[:, q * 128:q * 128 + 64], 1.0, lamt[:, q * 128 + 64:q * 128 + 128],
                        ALU.mult, ALU.mult, accum_out=lamc[:, q:q + 1]),
                        reads=[RC], writes=[R("lamj"), R("lamc")])
                sc.op("act", lambda h: h.activation(lamc[:, 0:2], lamc[:, 0:2], AF.Exp),
                      reads=[R("lamc")], writes=[R("lamc")])
                sc.op("dve", lambda h: h.tensor_scalar(lamc[:, 3:4], lamc[:, 1:2], lamc[:, 0:1], -LAMBDA_INIT,
                                                      ALU.subtract, ALU.add),
                      reads=[R("lamc")], writes=[R("lamc")])

                for ct in range(4):
                    v, half = ct // 2, ct % 2
                    w = wst[ct % 2]
                    acc = accs[ct % 2]
                    RW, RA = R("wst", ct % 2), R("macc", ct % 2)
                    sc.dma("sp", w[:], w_ada[:, ct * 512:(ct + 1) * 512].rearrange("(kc p) n -> p kc n", p=128),
                           writes=[RW])
                    for kc in range(8):
                        if kc == 0:
                            sc.op("dve", lambda h, w=w, acc=acc: h.tensor_scalar(
                                acc[:], w[:, 0, :], scol[:, 0:1], None, ALU.mult),
                                reads=[RW, R("scol")], writes=[RA])
                        else:
                            sc.op("dve", lambda h, w=w, acc=acc, kc=kc: h.scalar_tensor_tensor(
                                acc[:], w[:, kc, :], scol[:, kc:kc + 1], acc[:], ALU.mult, ALU.add),
                                reads=[RW, R("scol"), RA], writes=[RA])
                    pz = ps[ct % 2]
                    PZ = PS[ct % 2]
                    if v in (2, 5):
                        gi = 0 if v == 2 else 1
                        sc.pe([lambda h, pz=pz, acc=acc: h.matmul(pz[:], ones_f, acc[:], start=True, stop=True)],
                              reads=[RA, RC], writes=[PZ])
                        br, gp = brow[ct % 2], gprow[ct % 2]
                        sc.dma("sp", br[:], bada_rep[:, ct * 512:(ct + 1) * 512], writes=[R("brow", ct % 2)])
                        sc.dma("sp", gp[:], gpost_rep[:, gi * D + half * 512: gi * D + (half + 1) * 512],
                               writes=[R("gprow", ct % 2)])
                        sc.op("dve", lambda h, br=br, pz=pz: h.tensor_tensor(br[:], pz[:], br[:], ALU.add),
                              reads=[PZ, R("brow", ct % 2)], writes=[R("brow", ct % 2)])
                        dst = gtg[:, gi * D + half * 512: gi * D + (half + 1) * 512]
                        sc.op("dve", lambda h, br=br, gp=gp, dst=dst: h.tensor_tensor(dst, br[:], gp[:], ALU.mult),
                              reads=[R("brow", ct % 2), R("gprow", ct % 2)], writes=[R("gtg")])
                    else:
                        fns = []
                        for j in range(4):
                            fns.append(lambda h, pz=pz, acc=acc, j=j: h.matmul(
                                pz[:, j:j + 1], acc[:, j * 128:(j + 1) * 128], ones_f[:, 0:1], start=True, stop=True))
                        sc.pe(fns, reads=[RA, RC], writes=[PZ])
                        blkbase = {1: 0, 0: 8, 4: 16, 3: 24}[v]
                        dst = modc[:, blkbase + half * 4: blkbase + half * 4 + 4]
                        bsl = bcol[:, ct * 4:(ct + 1) * 4]
                        if v in (0, 3):
                            sc.op("dve", lambda h, dst=dst, pz=pz, bsl=bsl: h.tensor_tensor(dst, pz[:, 0:4], bsl, ALU.add),
                                  reads=[PZ, RC], writes=[R("modc")])
                        else:
                            gsl = gpc[:, (0 if v == 1 else 8) + half * 4:(0 if v == 1 else 8) + half * 4 + 4]
                            sc.op("dve", lambda h, dst=dst, pz=pz, bsl=bsl: h.scalar_tensor_tensor(
                                dst, pz[:, 0:4], 1.0, bsl, ALU.add, ALU.add),
                                reads=[PZ, RC], writes=[R("modc")])
                            sc.op("dve", lambda h, dst=dst, gsl=gsl: h.tensor_tensor(dst, dst, gsl, ALU.mult),
                                  reads=[R("modc"), RC], writes=[R("modc")])
                sc.barrier()

            def nt_block(gidx, bi, xt_ap, XR, tag, scratch, banks):
                junk, xs, ssq, rs = scratch
                tps = [ps[b][:].bitcast(BF16) for b in banks]
                TP = [PS[b] for b in banks]
                k = (gidx * 4 + bi) % len(xs)
                kj = (gidx * 4 + bi) % len(junk)
                sc.op("act", lambda h: h.activation(junk[kj][:], xt_ap, AF.Square, accum_out=ssq[:, k:k + 1]),
                      reads=[XR], writes=[R(tag + "junk", kj), R(tag + "ssq", k)])
                rstd_op(rs[:, k:k + 1], ssq[:, k:k + 1], D, R(tag + "rs", k), R(tag + "ssq", k))
                sc.op("dve", lambda h: h.tensor_scalar(xs[k][:], xt_ap, rs[:, k:k + 1], None, ALU.mult),
                      reads=[XR, R(tag + "rs", k)], writes=[R(tag + "xs", k)])
                sc.pe([lambda h, j=j: h.transpose(
                    tps[j // 2][:, (j % 2) * 512 + bi * 128:(j % 2) * 512 + (bi + 1) * 128],
                    xs[k][:, j * 128:(j + 1) * 128], ident_b) for j in range(8)],
                    reads=[R(tag + "xs", k), RC], writes=TP)

            def nt_evac(dst_fn, mbase, banks):
                tps = [ps[b][:].bitcast(BF16) for b in banks]
                TP = [PS[b] for b in banks]
                for j in range(8):
                    dst, DR = dst_fn(j)
                    sc.op("dve", lambda h, j=j, dst=dst: h.tensor_scalar(
                        dst, tps[j // 2][:, (j % 2) * 512:(j % 2) * 512 + 512], modc[:, mbase + j:mbase + j + 1],
                        modc[:, mbase + 8 + j:mbase + 9 + j], ALU.mult, ALU.add),
                        reads=[TP[j // 2], R("modc")], writes=[DR])

            def norm_transpose_group(gidx, blocks, dst_fn, mbase, tag, scratch, banks):
                for bi, (xt_ap, XR) in enumerate(blocks):
                    nt_block(gidx, bi, xt_ap, XR, tag, scratch, banks)
                nt_evac(dst_fn, mbase, banks)

            with ExitStack() as esA:
                xb = [sb("xb%d" % i, [128, D], F32, esA) for i in range(9)]
                junk = [sb("junkA%d" % i, [128, D], BF16, esA) for i in range(2)]
                xs = [sb("xsA%d" % i, [128, D], BF16, esA) for i in range(8)]
                ssq = sb("ssqA", [128, 8], F32, esA)
                rs = sb("rsA", [128, 8], F32, esA)
                for g in range(NT_ALL):
                    blocks = []
                    for bi in range(4):
                        t = g * 4 + bi
                        xt = xb[t % 9]
                        XR = R("xb", t % 9)
                        sc.dma("sp", xt[:], x_p[t * 128:(t + 1) * 128, :], writes=[XR])
                        blocks.append((xt[:], XR))
                    tens, off = hTtile(g)

                    def dst_fn(j, tens=tens, off=off, g=g):
                        return tens[:, j, off:off + 512], R("hT", g)
                    norm_transpose_group(g, blocks, dst_fn, 0, "A", (junk, xs, ssq, rs),
                                         [0, 1, 2, 3] if g % 2 == 0 else [4, 5, 6, 7])
                sc.barrier()

            with ExitStack() as esB:
                KT = bufY[:, 0:2 * S].rearrange("p (a s) -> p a s", a=2)
                QM = bufY[:, 2 * S:2 * S + 4 * SO].rearrange("p (a s) -> p a s", a=4)
                Vb = sb("Vb", [128, NBLK * 260], BF16, esB)
                wsl = sb("wsl", [128, DC, 768], BF16, esB)
                Eb = [sb("Eb%d" % i, [128, 512], BF16, esB) for i in range(2)]
                spb = [sb("spb%d" % i, [128, 512], BF16, esB) for i in range(3)]
                wb = [sb("wb%d" % i, [128, 512], BF16, esB) for i in range(3)]
                Rb = [sb("Rb%d" % i, [128, 512], BF16, esB) for i in range(2)]
                ytok = [sb("ytok%d" % i, [128, 256], BF16, esB) for i in range(2)]
                dfar = sb("dfar", [128, 5120], F32, esB)
                t1b = [dfar[:, i * 512:(i + 1) * 512] for i in range(2)]
                t2b = [dfar[:, 1024 + i * 512:1024 + (i + 1) * 512] for i in range(2)]
                angb = dfar[:, 2048:2560]
                ang2 = dfar[:, 2560:3072]
                sinT = dfar[:, 3072:3584]
                cosT = dfar[:, 3584:4096]
                post = dfar[:, 4096:4608].bitcast(I32)
                raw = [dfar[:, 4608 + i * 256:4608 + (i + 1) * 256].bitcast(BF16) for i in range(2)]
                dfd = [sb("dfd%d" % i, [128, 128], F32, esB) for i in range(2)]
                dfj = sb("dfj", [128, 128], F32, esB)
                dfy = [sb("dfy%d" % i, [128, 128], BF16, esB) for i in range(2)]
                dsm = sb("dsm", [128, 16], F32, esB)
                ocp = [sb("ocp%d" % i, [128, 385], F32, esB) for i in range(2)]
                Vsb = Vb[:, :].rearrange("p (t c) -> p t c", c=260)
                wsd = [dfar[:, i * 1024:(i + 1) * 1024].rearrange("p (a n) -> p a n", a=8) for i in range(2)]
                accd = [dfar[:, 2048 + i * 128:2048 + (i + 1) * 128] for i in range(2)]
                brd = [dfar[:, 2304 + i * 128:2304 + (i + 1) * 128] for i in range(2)]
                gpd = [dfar[:, 2560 + i * 128:2560 + (i + 1) * 128] for i in range(2)]
                mod_piece = [0]
                mod_pend = []

                def emit_mod_piece():
                    while mod_pend:
                        mod_pend.pop(0)()
                    p = mod_piece[0]
                    if p >= 32:
                        return
                    mod_piece[0] += 1
                    mod_pend.append(lambda p=p: mod_part2(p))
                    ct, q = 4 + p // 4, p % 4
                    v, half = ct // 2, ct % 2
                    c0 = ct * 512 + q * 128
                    k = p % 2
                    w, acc = wsd[k], accd[k]
                    RW, RA = R("wsd", k), R("accd", k)
                    pz, PZ = ps[k], PS[k]
                    sc.dma("sp", w[:], w_ada[:, c0:c0 + 128].rearrange("(kc p) n -> p kc n", p=128), writes=[RW])
                    sc.op("dve", lambda h: h.tensor_scalar(acc[:], w[:, 0, :], scol[:, 0:1], None, ALU.mult),
                          reads=[RW, R("scol")], writes=[RA])
                    for kc in range(1, 8):
                        sc.op("dve", lambda h, kc=kc: h.scalar_tensor_tensor(
                            acc[:], w[:, kc, :], scol[:, kc:kc + 1], acc[:], ALU.mult, ALU.add),
                            reads=[RW, R("scol"), RA], writes=[RA])
                    if v in (2, 5):
                        gi = 0 if v == 2 else 1
                        goff = gi * D + half * 512 + q * 128
                        br, gp = brd[k], gpd[k]
                        sc.dma("sp", br[:], bada_rep[:, c0:c0 + 128], writes=[R("brd", k)])
                        sc.dma("sp", gp[:], gpost_rep[:, goff:goff + 128], writes=[R("gpd", k)])

                def mod_part2(p):
                    ct, q = 4 + p // 4, p % 4
                    v, half = ct // 2, ct % 2
                    c0 = ct * 512 + q * 128
                    k = p % 2
                    w, acc = wsd[k], accd[k]
                    RW, RA = R("wsd", k), R("accd", k)
                    pz, PZ = ps[k], PS[k]
                    if v in (2, 5):
                        gi = 0 if v == 2 else 1
                        goff = gi * D + half * 512 + q * 128
                        br, gp = brd[k], gpd[k]
                        sc.pe([lambda h: h.matmul(pz[:, 0:128], ones_f, acc[:], start=True, stop=True)],
                              reads=[RA, RC], writes=[PZ])
                        sc.op("dve", lambda h: h.tensor_tensor(br[:], pz[:, 0:128], br[:], ALU.add),
                              reads=[PZ, R("brd", k)], writes=[R("brd", k)])
                        sc.op("dve", lambda h: h.tensor_tensor(gtg[:, goff:goff + 128], br[:], gp[:], ALU.mult),
                              reads=[R("brd", k), R("gpd", k)], writes=[R("gtg")])
                    else:
                        sc.pe([lambda h: h.matmul(pz[:, 0:1], acc[:, 0:128], ones_f[:, 0:1], start=True, stop=True)],
                              reads=[RA, RC], writes=[PZ])
                        blkbase = {4: 16, 3: 24}[v]
                        dst = modc[:, blkbase + half * 4 + q: blkbase + half * 4 + q + 1]
                        bsl = bcol[:, ct * 4 + q:ct * 4 + q + 1]
                        if v == 3:
                            sc.op("dve", lambda h: h.tensor_tensor(dst, pz[:, 0:1], bsl, ALU.add),
                                  reads=[PZ, RC], writes=[R("modc")])
                        else:
                            gsl = gpc[:, 8 + half * 4 + q:8 + half * 4 + q + 1]
                            sc.op("dve", lambda h: h.scalar_tensor_tensor(dst, pz[:, 0:1], 1.0, bsl, ALU.add, ALU.add),
                                  reads=[PZ, RC], writes=[R("modc")])
                            sc.op("dve", lambda h: h.tensor_tensor(dst, dst, gsl, ALU.mult),
                                  reads=[R("modc"), RC], writes=[R("modc")])
                sc.op("pool", lambda h: h.memset(QM, 0.0), writes=[R("QM", t) for t in range(NT_OWN)])

                def rope_tables(tt):
                    TWO_PI = 2 * math.pi
                    C1 = 6.28125
                    C2 = TWO_PI - C1
                    sc.dma("sp", post[:], pos_rep[:, tt * 512:(tt + 1) * 512], writes=[R("post")])
                    sc.op("dve", lambda h: h.tensor_copy(angb[:], post[:]), reads=[R("post")], writes=[R("angb")])
                    sc.op("dve", lambda h: h.tensor_scalar(angb[:], angb[:], invf, None, ALU.mult),
                          reads=[R("angb"), RC], writes=[R("angb")])
                    sc.op("dve", lambda h: h.tensor_scalar(ang2[:], angb[:], 1.0 / TWO_PI, None, ALU.mult),
                          reads=[R("angb")], writes=[R("ang2")])
                    sc.op("dve", lambda h: h.tensor_copy(post[:], ang2[:]), reads=[R("ang2")], writes=[R("post")])
                    sc.op("dve", lambda h: h.tensor_copy(ang2[:], post[:]), reads=[R("post")], writes=[R("ang2")])
                    sc.op("dve", lambda h: h.scalar_tensor_tensor(angb[:], ang2[:], -C1, angb[:], ALU.mult, ALU.add),
                          reads=[R("ang2"), R("angb")], writes=[R("angb")])
                    sc.op("dve", lambda h: h.scalar_tensor_tensor(angb[:], ang2[:], -C2, angb[:], ALU.mult, ALU.add),
                          reads=[R("ang2"), R("angb")], writes=[R("angb")])

                    def fold(dst_tab, DR):
                        sc.op("dve", lambda h: h.tensor_scalar(ang2[:], angb[:], math.pi, None, ALU.is_gt),
                              reads=[R("angb")], writes=[R("ang2")])
                        sc.op("dve", lambda h: h.scalar_tensor_tensor(angb[:], ang2[:], -TWO_PI, angb[:], ALU.mult, ALU.add),
                              reads=[R("ang2"), R("angb")], writes=[R("angb")])
                        sc.op("dve", lambda h: h.tensor_scalar(ang2[:], angb[:], -math.pi, None, ALU.is_lt),
                              reads=[R("angb")], writes=[R("ang2")])
                        sc.op("dve", lambda h: h.scalar_tensor_tensor(angb[:], ang2[:], TWO_PI, angb[:], ALU.mult, ALU.add),
                              reads=[R("ang2"), R("angb")], writes=[R("angb")])
                        sc.op("dve", lambda h: h.tensor_scalar(ang2[:], angb[:], -math.pi, math.pi, ALU.max, ALU.min),
                              reads=[R("angb")], writes=[R("ang2")])
                        sc.op("act", lambda h: h.activation(dst_tab[:], ang2[:], AF.Sin),
                              reads=[R("ang2")], writes=[DR])
                    fold(sinT, R("sinT"))
                    sc.op("dve", lambda h: h.tensor_scalar(angb[:], angb[:], 0.5 * math.pi, None, ALU.add),
                          reads=[R("angb")], writes=[R("angb")])
                    fold(cosT, R("cosT"))

                pcount = [0]

                def project(col0, tt, is_df, scale, dst_fn):
                    k = pcount[0] % 2
                    pcount[0] += 1
                    pz, PZ = ps[k], PS[k]
                    rz, RZ = ps[7], PS[7]
                    tens, off = hTtile(tt)
                    sc.pe([lambda h, dc=dc: h.matmul(pz[:], wsl[:, dc, col0:col0 + 128], tens[:, dc, off:off + 512],
                                                     start=(dc == 0), stop=(dc == 7)) for dc in range(8)],
                          reads=[R("wsl"), R("hTall")], writes=[PZ])
                    if not is_df:
                        for (dst, DR, lo, hi) in dst_fn():
                            sc.op("dve", lambda h, dst=dst, lo=lo, hi=hi: h.tensor_scalar(
                                dst, pz[lo:hi, :], scale, None, ALU.mult), reads=[PZ], writes=[DR])
                        return
                    rw, RW = raw[k], R("raw", k)
                    sc.op("dve", lambda h: h.tensor_copy(rw[:], pz[:]), reads=[PZ], writes=[RW])
                    sc.pe([lambda h: h.matmul(rz[:], rot, rw[:], start=True, stop=True)],
                          reads=[RW, RC], writes=[RZ])
                    t1, t2 = t1b[k], t2b[k]
                    sc.op("dve", lambda h: h.scalar_tensor_tensor(t1[:], pz[:], scale, cosT[:], ALU.mult, ALU.mult),
                          reads=[PZ, R("cosT")], writes=[R("t1", k)])
                    sc.op("dve", lambda h: h.scalar_tensor_tensor(t2[:], rz[:], scale, sinT[:], ALU.mult, ALU.mult),
                          reads=[RZ, R("sinT")], writes=[R("t2", k)])
                    for (dst, DR, lo, hi) in dst_fn():
                        sc.op("pool", lambda h, dst=dst, lo=lo, hi=hi: h.tensor_tensor(
                            dst, t1[lo:hi, :], t2[lo:hi, :], ALU.add),
                            reads=[R("t1", k), R("t2", k)], writes=[DR])

                wsl_loaded = [False]

                def load_wsl(is_df, g):
                    if is_df:
                        qc, kc_, vc = 1536 + g * 256, 2048 + g * 256, 2560 + g * 256
                    else:
                        qc, kc_, vc = g * 256, 512 + g * 256, 1024 + g * 256
                    for n, c0 in enumerate((qc, kc_, vc)):
                        sc.dma("pool", wsl[:, :, n * 256:(n + 1) * 256],
                               w_in[:, c0:c0 + 256].rearrange("(kc p) n -> p kc n", p=128), writes=[R("wsl")])

                def run_pass(is_df, g, next_pass):
                    if not wsl_loaded[0]:
                        load_wsl(is_df, g)
                    wsl_loaded[0] = False
                    if is_df:
                        allV = [R("V", t) for t in range(NT_ALL)]
                        sc.op("pool", lambda h: h.memset(Vsb[:, :, 128:129], 1.0), writes=allV)
                        sc.op("pool", lambda h: h.memset(Vsb[:, :, 258:259], 1.0), writes=allV)

                    def vproj(t2_):
                        k = pcount[0] % 2
                        pcount[0] += 1
                        pz, PZ = ps[k], PS[k]
                        fns = []
                        for u in range(2):
                            tens, off = hT(2 * t2_ + u)
                            for dc in range(8):
                                fns.append(lambda h, u=u, dc=dc, tens=tens, off=off: h.matmul(
                                    pz[:, u * 256:(u + 1) * 256], tens[:, dc, off:off + 128], wsl[:, dc, 512:768],
                                    start=(dc == 0), stop=(dc == 7)))
                        sc.pe(fns, reads=[R("wsl"), R("hTall")], writes=[PZ])
                        pzv = pz[:, :].rearrange("p (u c) -> p u c", c=256)
                        VR = R("V", (2 * t2_) // 4)
                        if not is_df:
                            sc.op("dve", lambda h: h.tensor_copy(Vsb[:, 2 * t2_:2 * t2_ + 2, 0:256], pzv),
                                  reads=[PZ], writes=[VR])
                        else:
                            for hd in range(2):
                                sc.op("dve", lambda h, hd=hd: h.tensor_copy(
                                    Vsb[:, 2 * t2_:2 * t2_ + 2, hd * 130:hd * 130 + 128],
                                    pzv[:, :, hd * 128:(hd + 1) * 128]), reads=[PZ], writes=[VR])

                    chunks = []
                    for tp_ in range(NT_OWN):
                        lst = []
                        for tt in (tp_, NT_OWN + tp_):
                            if is_df:
                                lst.append(lambda tt=tt: rope_tables(tt))
                            for p in range(2):
                                def kd(p=p, tt=tt):
                                    return [(KT[:, p, tt * 512:(tt + 1) * 512], R("KT", tt), 0, 128)]
                                lst.append(lambda p=p, tt=tt, kd=kd: project(256 + p * 128, tt, is_df, 1.0, kd))
                            if tt < NT_OWN:
                                for p in range(2):
                                    def qd(p=p, tt=tt):
                                        return [(QM[0:64, 2 * p, tt * 512:(tt + 1) * 512], R("QM", tt), 0, 64),
                                                (QM[64:128, 2 * p + 1, tt * 512:(tt + 1) * 512], R("QM", tt), 64, 128)]
                                    lst.append(lambda p=p, tt=tt, qd=qd: project(p * 128, tt, is_df, 0.125, qd))
                            for t2_ in (2 * tt, 2 * tt + 1):
                                lst.append(lambda t2_=t2_: vproj(t2_))
                        chunks.append(lst)
                    chunk_state = [0]

                    def emit_chunk_items(nitems):
                        while nitems > 0 and chunk_state[0] < len(chunks):
                            lst = chunks[chunk_state[0]]
                            if not lst:
                                chunk_state[0] += 1
                                if chunk_state[0] == len(chunks) and next_pass is not None:
                                    load_wsl(*next_pass)
                                    wsl_loaded[0] = True
                                continue
                            lst.pop(0)()
                            nitems -= 1
                        while chunk_state[0] < len(chunks) and not chunks[chunk_state[0]]:
                            chunk_state[0] += 1
                            if chunk_state[0] == len(chunks) and next_pass is not None:
                                load_wsl(*next_pass)
                                wsl_loaded[0] = True

                    def ensure_chunks(gq):
                        while chunk_state[0] <= gq and chunk_state[0] < len(chunks):
                            emit_chunk_items(1)

                    units = []
                    for i in range(NOWN):
                        n = 2 * i + 2
                        for s in range(n):
                            if s % 2 == 0:
                                kb = i - s // 2
                            else:
                                kb = NOWN + i - s // 2
                            units.append((i, s, n, kb))
                    NU = len(units)
                    zb = [3, 4, 5]
                    yb = [6, 2]

                    if is_df:
                        ensure_chunks(len(chunks) - 1)

                    def drip(u):
                        gq = units[u][0] // 4
                        if chunk_state[0] == gq + 1 and chunk_state[0] < len(chunks):
                            emit_chunk_items(1)

                    def stA(u):
                        i, s, n, kb = units[u]
                        z, Z = ps[zb[u % 3]], PS[zb[u % 3]]
                        fns = []
                        plain = is_df and not (s == 0 or s == n - 1)
                        for hh in range(4):
                            fns.append(lambda h, hh=hh: h.matmul(
                                z[:, hh * 128:(hh + 1) * 128], KT[:, hh // 2, kb * 128:(kb + 1) * 128],
                                QM[:, hh, i * 128:(i + 1) * 128], start=(hh == 0), stop=plain, skip_group_check=True))
                        if s == 0:
                            mk = mk_df if is_df else mk_sb
                            fns.append(lambda h: h.matmul(z[:], ident_b, mk, start=False, stop=is_df,
                                                          skip_group_check=True))
                        elif s == n - 1:
                            fns.append(lambda h: h.matmul(z[:], ident_b, mk_dm, start=False, stop=is_df,
                                                          skip_group_check=True))
                        ensure_chunks(i // 4)
                        sc.pe(fns, reads=[R("KT", kb // 4), R("QM", i // 4), RC], writes=[Z])

                    def stB_sb(u):
                        z, Z = ps[zb[u % 3]], PS[zb[u % 3]]
                        E, sp = Eb[u % 2], spb[u % 3]
                        sc.op("act", lambda h: h.activation(E[:], z[:], AF.Exp), reads=[Z], writes=[R("E", u % 2)])
                        sc.op("act", lambda h: h.activation(sp[:], E[:], AF.Ln, bias=1.0),
                              reads=[R("E", u % 2)], writes=[R("sp", u % 3)])

                    def stE_sb(u):
                        i, s, n, kb = units[u]
                        if s == n - 1:
                            return
                        sp = spb[u % 3]
                        if s == 0:
                            sc.op("pool", lambda h: h.tensor_copy(Rb[1][:], sp[:]),
                                  reads=[R("sp", u % 3)], writes=[R("Rb", 1)])
                        else:
                            a, b = Rb[s % 2], Rb[(s + 1) % 2]
                            sc.op("pool", lambda h: h.tensor_tensor(b[:], a[:], sp[:], ALU.add),
                                  reads=[R("sp", u % 3), R("Rb", s % 2)], writes=[R("Rb", (s + 1) % 2)])

                    def stC_sb(u):
                        i, s, n, kb = units[u]
                        z, Z = ps[zb[u % 3]], PS[zb[u % 3]]
                        sp = spb[u % 3]
                        fns = []
                        rd = [R("sp", u % 3), RC, Z]
                        for hh in range(4):
                            fns.append(lambda h, hh=hh: h.matmul(
                                z[:, hh * 128:(hh + 1) * 128], negtri, sp[:, hh * 128:(hh + 1) * 128],
                                start=False, stop=(s == 0), skip_group_check=True))
                        if s > 0:
                            rb = Rb[s % 2]
                            rd.append(R("Rb", s % 2))
                            for hh in range(4):
                                fns.append(lambda h, hh=hh: h.matmul(
                                    z[:, hh * 128:(hh + 1) * 128], negones, rb[:, hh * 128:(hh + 1) * 128],
                                    start=False, stop=True, skip_group_check=True))
                        sc.pe(fns, reads=rd, writes=[Z])

                    def stD(u):
                        z, Z = ps[zb[u % 3]], PS[zb[u % 3]]
                        w_ = wb[u % 3]
                        sc.op("act", lambda h: h.activation(w_[:], z[:], AF.Exp), reads=[Z], writes=[R("w", u % 3)])

                    def stF(u):
                        i, s, n, kb = units[u]
                        w_ = wb[u % 3]
                        fns = []
                        if not is_df:
                            y, Y = ps[yb[i % 2]], PS[yb[i % 2]]
                            for hh in range(4):
                                fns.append(lambda h, hh=hh: h.matmul(
                                    y[:, hh * 64:(hh + 1) * 64], w_[:, hh * 128:(hh + 1) * 128],
                                    Vsb[:, kb, hh * 64:(hh + 1) * 64], start=(s == 0 and hh == 0), stop=(s == n - 1),
                                    skip_group_check=True))
                            sc.pe(fns, reads=[R("w", u % 3), R("V", kb // 4)], writes=[Y])
                        else:
                            for c in range(4):
                                y = ps[yb[c // 2]]
                                fns.append(lambda h, c=c, y=y: h.matmul(
                                    y[:, (c % 2) * 256:(c % 2) * 256 + 129], w_[:, c * 128:(c + 1) * 128],
                                    Vsb[:, kb, (c // 2) * 130:(c // 2) * 130 + 129], start=(s == 0 and c % 2 == 0), stop=(s == n - 1),
                                    skip_group_check=True))
                            sc.pe(fns, reads=[R("w", u % 3), R("V", kb // 4)], writes=[PS[yb[0]], PS[yb[1]]])
                        if s == n - 1:
                            delay = min(8, 2 * i + 3)
                            if is_df:
                                fin_df(i)
                                pending.append((u + delay, lambda i=i: fin_df2(i)))
                            else:
                                fin_sb(i)
                                pending.append((u + delay, lambda i=i: fin_sb2(i)))

                    pending = []

                    def flush(u):
                        while pending and pending[0][0] <= u:
                            pending.pop(0)[1]()

                    def fin_sb(i):
                        y, Y = ps[yb[i % 2]], PS[yb[i % 2]]
                        yt = ytok[i % 2]
                        sc.op("dve", lambda h: h.tensor_copy(yt[:], y[:, 0:256]), reads=[Y], writes=[R("ytok", i % 2)])

                    def fin_sb2(i):
                        yt = ytok[i % 2]
                        tp = ps[7][:].bitcast(BF16)
                        sc.pe([lambda h, p=p: h.transpose(tp[:, p * 128:(p + 1) * 128], yt[:, p * 128:(p + 1) * 128], ident_b)
                               for p in range(2)], reads=[R("ytok", i % 2), RC], writes=[PS[7]])
                        tpv = tp[:, 0:256].rearrange("p (a c) -> p a c", c=128)
                        sc.op("dve", lambda h: h.tensor_copy(ysbT[:, 2 * g:2 * g + 2, i * 128:(i + 1) * 128], tpv),
                              reads=[PS[7]], writes=[R("ysbT")])

                    def fin_df(i):
                        tp = ps[7][:].bitcast(BF16)
                        for hd in range(2):
                            sc.op("dve", lambda h, hd=hd: h.tensor_copy(ocp[hd][:], ps[yb[hd]][:, 0:385]),
                                  reads=[PS[yb[hd]]], writes=[R("ocp", hd)])
                        for hd in range(2):
                            y, Y = ocp[hd], R("ocp", hd)
                            dd, DDR = dfd[hd], R("dfd", hd)
                            b0 = hd * 8
                            sc.op("dve", lambda h, y=y, b0=b0: h.reciprocal(dsm[:, b0:b0 + 1], y[:, 128:129]),
                                  reads=[Y], writes=[R("dsm", hd)])
                            sc.op("dve", lambda h, y=y, b0=b0: h.reciprocal(dsm[:, b0 + 1:b0 + 2], y[:, 256 + 128:256 + 129]),
                                  reads=[Y], writes=[R("dsm", hd)])
                            sc.op("dve", lambda h, b0=b0: h.tensor_tensor(dsm[:, b0 + 1:b0 + 2], dsm[:, b0 + 1:b0 + 2],
                                                                          lamc[:, 3:4], ALU.mult),
                                  reads=[R("dsm", hd), R("lamc")], writes=[R("dsm", hd)])
                            sc.op("dve", lambda h, y=y, dd=dd, b0=b0: h.tensor_scalar(
                                dd[:], y[:, 0:128], dsm[:, b0:b0 + 1], None, ALU.mult),
                                reads=[Y, R("dsm", hd)], writes=[DDR])
                            sc.op("dve", lambda h, y=y, dd=dd, b0=b0: h.scalar_tensor_tensor(
                                dd[:], y[:, 256:384], dsm[:, b0 + 1:b0 + 2], dd[:], ALU.mult, ALU.add),
                                reads=[Y, R("dsm", hd), DDR], writes=[DDR])
                            sc.op("dve", lambda h, dd=dd, b0=b0: h.scalar_tensor_tensor(
                                dfj[:], dd[:], 1.0, dd[:], ALU.mult, ALU.mult, accum_out=dsm[:, b0 + 2:b0 + 3]),
                                reads=[DDR], writes=[R("dfj"), R("dsm", hd)])
                            rstd_op(dsm[:, b0 + 3:b0 + 4], dsm[:, b0 + 2:b0 + 3], 128, R("dsm", hd), R("dsm", hd),
                                    mul=1.0 - LAMBDA_INIT, on_act=False)
                            yy = dfy[hd]
                            sc.op("dve", lambda h, dd=dd, yy=yy, b0=b0: h.scalar_tensor_tensor(
                                yy[:], dd[:], dsm[:, b0 + 3:b0 + 4], gsub[:], ALU.mult, ALU.mult),
                                reads=[DDR, R("dsm", hd), RC], writes=[R("dfy", hd)])

                    def fin_df2(i):
                        tp = ps[7][:].bitcast(BF16)
                        sc.pe([lambda h, hd=hd: h.transpose(tp[:, hd * 128:(hd + 1) * 128], dfy[hd][:], ident_b)
                               for hd in range(2)], reads=[R("dfy", 0), R("dfy", 1), RC], writes=[PS[7]])
                        tpv = tp[:, 0:256].rearrange("p (a c) -> p a c", c=128)
                        sc.op("dve", lambda h: h.tensor_copy(ydfT[:, 2 * g:2 * g + 2, i * 128:(i + 1) * 128], tpv),
                              reads=[PS[7]], writes=[R("ydfT")])

                    if not is_df:
                        stA(0)
                        stA(1)
                        stB_sb(0)
                        for u in range(NU):
                            if g == 0 and u % 8 == 4:
                                emit_mod_piece()
                            drip(u)
                            stC_sb(u)
                            stE_sb(u)
                            if u >= 1:
                                stD(u - 1)
                                stF(u - 1)
                                flush(u - 1)
                            if u + 2 < NU:
                                stA(u + 2)
                            if u + 1 < NU:
                                stB_sb(u + 1)
                        stD(NU - 1)
                        stF(NU - 1)
                        flush(10 ** 9)
                        if g == 0:
                            while mod_piece[0] < 32 or mod_pend:
                                emit_mod_piece()
                    else:
                        stA(0)
                        stA(1)
                        for u in range(NU):
                            drip(u)
                            stD(u)
                            if u + 2 < NU:
                                stA(u + 2)
                            stF(u)
                            flush(u)
                        flush(10 ** 9)

                hall = R("hTall")
                for t in range(NBLK):
                    hall.w = R("hT", t).w if R("hT", t).w is not None else hall.w
                sc.barrier()
                run_pass(False, 0, (False, 1))
                run_pass(False, 1, (True, 0))
                sc.barrier()
                run_pass(True, 0, (True, 1))
                run_pass(True, 1, None)
                sc.barrier()
        sc.barrier()

        with ExitStack() as esCD:
            with ExitStack() as esC:
                with ExitStack() as esC1:
                    wg = sb("wg", [128, DC, 2 * D], BF16, esC1)
                    wbs = sb("wbs", [128, 4, D], BF16, esC1)
                    wbd = sb("wbd", [128, 4, D], BF16, esC1)
                    gsb_ = [sb("gsb%d" % i, [128, 512], F32, esC1) for i in range(2)]
                    gdf_ = [sb("gdf%d" % i, [128, 512], F32, esC1) for i in range(2)]
                    t1c = [sb("t1c%d" % i, [128, 512], F32, esC1) for i in range(2)]
                    t2c = [sb("t2c%d" % i, [128, 512], F32, esC1) for i in range(2)]
                    def load_wg(q):
                        for hh in range(2):
                            c0 = hh * D + q * 256
                            sc.dma("pool", wg[:, :, c0:c0 + 256],
                                   w_gate[:, c0:c0 + 256].rearrange("(kc p) n -> p kc n", p=128),
                                   writes=[R("wg", hh, q)])
                    load_wg(0)
                    sc.dma("pool", wbs[:], w_bsb.rearrange("(c p) n -> p c n", p=128), writes=[R("wbs")])
                    sc.dma("pool", wbd[:], w_bdf.rearrange("(c p) n -> p c n", p=128), writes=[R("wbd")])
                    for q in range(1, 4):
                        load_wg(q)
                    it = 0
                    for tt in range(NT_OWN):
                        tsl = slice(tt * 512, (tt + 1) * 512)
                        for dc in range(8):
                            k = it % 2
                            it += 1
                            pg, pd, pbs, pbd = ps[4 * k], ps[4 * k + 1], ps[4 * k + 2], ps[4 * k + 3]
                            PG, PD, PBS, PBD = PS[4 * k], PS[4 * k + 1], PS[4 * k + 2], PS[4 * k + 3]
                            sc.pe([lambda h, kk=kk, pg=pg, dc=dc, tsl=tsl: h.matmul(
                                pg[:], wg[:, kk, dc * 128:(dc + 1) * 128], hT_own[:, kk, tsl],
                                start=(kk == 0), stop=(kk == 7)) for kk in range(8)],
                                reads=[R("wg", 0, dc // 2)], writes=[PG])
                            sc.pe([lambda h, kk=kk, pd=pd, dc=dc, tsl=tsl: h.matmul(
                                pd[:], wg[:, kk, D + dc * 128:D + (dc + 1) * 128], hT_own[:, kk, tsl],
                                start=(kk == 0), stop=(kk == 7)) for kk in range(8)],
                                reads=[R("wg", 1, dc // 2)], writes=[PD])
                            sc.pe([lambda h, c=c, pbs=pbs, dc=dc, tsl=tsl: h.matmul(
                                pbs[:], wbs[:, c, dc * 128:(dc + 1) * 128], ysbT[:, c, tsl],
                                start=(c == 0), stop=(c == 3)) for c in range(4)],
                                reads=[R("wbs"), R("ysbT")], writes=[PBS])
                            sc.pe([lambda h, c=c, pbd=pbd, dc=dc, tsl=tsl: h.matmul(
                                pbd[:], wbd[:, c, dc * 128:(dc + 1) * 128], ydfT[:, c, tsl],
                                start=(c == 0), stop=(c == 3)) for c in range(4)],
                                reads=[R("wbd"), R("ydfT")], writes=[PBD])
                            sc.op("act", lambda h, k=k, pg=pg, dc=dc: h.activation(
                                gsb_[k][:], pg[:], AF.Sigmoid, bias=bgc[:, dc:dc + 1]),
                                reads=[PG, RC], writes=[R("gsb", k)])
                            sc.op("act", lambda h, k=k, pd=pd, dc=dc: h.activation(
                                gdf_[k][:], pd[:], AF.Sigmoid, bias=bgc[:, 8 + dc:9 + dc]),
                                reads=[PD, RC], writes=[R("gdf", k)])
                            sc.op("dve", lambda h, k=k, pbs=pbs: h.tensor_tensor(t1c[k][:], pbs[:], gsb_[k][:], ALU.mult),
                                  reads=[PBS, R("gsb", k)], writes=[R("t1c", k)])
                            sc.op("dve", lambda h, k=k, pbd=pbd: h.tensor_tensor(t2c[k][:], pbd[:], gdf_[k][:], ALU.mult),
                                  reads=[PBD, R("gdf", k)], writes=[R("t2c", k)])
                            sc.op("pool", lambda h, k=k, dc=dc, tsl=tsl: h.tensor_tensor(
                                mergedT[:, dc, tsl], t1c[k][:], t2c[k][:], ALU.add),
                                reads=[R("t1c", k), R("t2c", k)], writes=[R("mergedT")])
                    sc.barrier()
                esP1.close()

                esW = ExitStack()
                esW.__enter__()
                wf1 = [sb("wf1_%d" % i, [128, DC, 512], BF16, esW) for i in range(2)]
                wf2 = [sb("wf2_%d" % i, [128, 4, D], BF16, esW) for i in range(2)]
                with ExitStack() as esC2:
                    wo = sb("wo", [128, DC, D], BF16, esC2)
                    xb2 = [sb("xb2_%d" % i, [128, D], F32, esC2) for i in range(3)]
                    x1b = [sb("x1b%d" % i, [128, D], F32, esC2) for i in range(4)]
                    junk = [sb("junkC%d" % i, [128, D], BF16, esC2) for i in range(2)]
                    xs = [sb("xsC%d" % i, [128, D], BF16, esC2) for i in range(8)]
                    ssq = sb("ssqC", [128, 8], F32, esC2)
                    rs = sb("rsC", [128, 8], F32, esC2)
                    ssm = sb("ssm", [128, 8], F32, esC2)
                    sc.dma("pool", wo[:], w_out.rearrange("(kc p) n -> p kc n", p=128), writes=[R("wo")])
                    sc.dma("pool", wf1[0][:], w_ff1[:, 0:512].rearrange("(kc p) n -> p kc n", p=128),
                           writes=[R("wf1", 0)])
                    sc.dma("pool", wf2[0][:], w_ff2[0:512, :].rearrange("(c p) n -> p c n", p=128),
                           writes=[R("wf2", 0)])
                    scr = (junk, xs, ssq, rs)

                    def c2_s1(i):
                        k, k3 = i % 2, i % 3
                        mm = ((ps[2 * k], PS[2 * k]), (ps[2 * k + 1], PS[2 * k + 1]))
                        xt, XR = xb2[k3], R("xb2", k3)
                        sc.dma("sp", xt[:], x_p[i * 128:(i + 1) * 128, :], writes=[XR])
                        for half, (m, M) in enumerate(mm):
                            sc.pe([lambda h, dc=dc, m=m, half=half: h.matmul(
                                m[:], mergedT[:, dc, i * 128:(i + 1) * 128], wo[:, dc, half * 512:(half + 1) * 512],
                                start=(dc == 0), stop=(dc == 7)) for dc in range(8)],
                                reads=[R("mergedT"), R("wo")], writes=[M])
                            sc.op("act", lambda h, m=m, half=half: h.activation(
                                junk[k][:, half * 512:(half + 1) * 512], m[:], AF.Square,
                                accum_out=ssm[:, 4 * k + half:4 * k + half + 1]),
                                reads=[M], writes=[R("Cjunk", k), R("ssm", k)])

                    def c2_s2(i):
                        k, k3 = i % 2, i % 3
                        mm = ((ps[2 * k], PS[2 * k]), (ps[2 * k + 1], PS[2 * k + 1]))
                        xt, XR = xb2[k3], R("xb2", k3)
                        sc.op("dve", lambda h: h.tensor_tensor(
                            ssm[:, 4 * k + 2:4 * k + 3], ssm[:, 4 * k:4 * k + 1], ssm[:, 4 * k + 1:4 * k + 2], ALU.add),
                            reads=[R("ssm", k)], writes=[R("ssm", k)])
                        rstd_op(ssm[:, 4 * k + 3:4 * k + 4], ssm[:, 4 * k + 2:4 * k + 3], D, R("ssm", k), R("ssm", k))
                        x1, X1 = x1b[i % 4], R("x1b", i % 4)
                        for half, (m, M) in enumerate(mm):
                            sc.op("dve", lambda h, m=m, half=half: h.scalar_tensor_tensor(
                                x1[:, half * 512:(half + 1) * 512], m[:], ssm[:, 4 * k + 3:4 * k + 4],
                                gtg[:, half * 512:(half + 1) * 512], ALU.mult, ALU.mult),
                                reads=[M, R("ssm", k), R("gtg")], writes=[X1])
                        sc.op("dve", lambda h: h.tensor_tensor(x1[:], x1[:], xt[:], ALU.add),
                              reads=[X1, XR], writes=[X1])
                        sc.dma("sp", out[i * 128:(i + 1) * 128, :], x1[:], reads=[X1], writes=[R("out", i)])

                    def c2_s3(i):
                        g, bi = i // 4, i % 4
                        x1, X1 = x1b[i % 4], R("x1b", i % 4)
                        nt_block(g, bi, x1[:], X1, "C", scr, [4, 5, 6, 7])
                        if bi == 3:
                            def dst_fn(j, g=g):
                                return h2T[:, j, g * 512:(g + 1) * 512], R("h2T")
                            nt_evac(dst_fn, 16, [4, 5, 6, 7])

                    for n in range(NOWN + 3):
                        if n < NOWN:
                            c2_s1(n)
                        if 0 <= n - 1 < NOWN:
                            c2_s2(n - 1)
                        if 0 <= n - 3 < NOWN:
                            c2_s3(n - 3)
                    sc.barrier()
            sc.barrier()

            with ExitStack() as esD:
                facc = sb("facc", [128, NOWN, D], F32, esD)
                f1G = [bufY[:, i * 4 * SO:(i + 1) * 4 * SO].rearrange("p (a s) -> p a s", a=4) for i in range(2)]
                rl = [sb("rl%d" % i, [128, 512], F32, esD) for i in range(2)]
                xr = [sb("xr%d" % i, [128, D], F32, esD) for i in range(2)]
                junkE = sb("junkE", [128, D], BF16, esD)
                sse = sb("sse", [128, 4], F32, esD)
                NG = 8
                cnt = [0, 0]

                def final_block(i):
                    k = i % 2
                    sc.op("act", lambda h: h.activation(junkE[:], facc[:, i, :], AF.Square,
                                                        accum_out=sse[:, 2 * k:2 * k + 1]),
                          reads=[R("facc", i)], writes=[R("junkE"), R("sse", k)])
                    rstd_op(sse[:, 2 * k + 1:2 * k + 2], sse[:, 2 * k:2 * k + 1], D, R("sse", k), R("sse", k))
                    sc.dma("sp", xr[k][:], out[i * 128:(i + 1) * 128, :], reads=[R("out", i)], writes=[R("xr", k)])
                    sc.op("dve", lambda h: h.scalar_tensor_tensor(
                        facc[:, i, :], facc[:, i, :], sse[:, 2 * k + 1:2 * k + 2], gtg[:, D:2 * D], ALU.mult, ALU.mult),
                        reads=[R("facc", i), R("sse", k), R("gtg")], writes=[R("facc", i)])
                    sc.op("dve", lambda h: h.tensor_tensor(xr[k][:], xr[k][:], facc[:, i, :], ALU.add),
                          reads=[R("xr", k), R("facc", i)], writes=[R("xr", k)])
                    sc.dma("sp", out[i * 128:(i + 1) * 128, :], xr[k][:], reads=[R("xr", k)], writes=[R("out", i)])

                def ffn_load(g):
                    k = g % 2
                    sc.dma("pool", wf1[k][:], w_ff1[:, g * 512:(g + 1) * 512].rearrange("(kc p) n -> p kc n", p=128),
                           reads=[], writes=[R("wf1", k)])
                    sc.dma("pool", wf2[k][:], w_ff2[g * 512:(g + 1) * 512, :].rearrange("(c p) n -> p c n", p=128),
                           reads=[], writes=[R("wf2", k)])

                def ffn_f1(g):
                    k = g % 2
                    for c in range(4):
                        for tt in range(NT_OWN):
                            j = cnt[0] % 2
                            cnt[0] += 1
                            pz, PZ = ps[j], PS[j]
                            tsl = slice(tt * 512, (tt + 1) * 512)
                            sc.pe([lambda h, dc=dc, pz=pz, c=c, tsl=tsl, k=k: h.matmul(
                                pz[:], wf1[k][:, dc, c * 128:(c + 1) * 128], h2T[:, dc, tsl],
                                start=(dc == 0), stop=(dc == 7)) for dc in range(8)],
                                reads=[R("wf1", k), R("h2T")], writes=[PZ])
                            sc.op("act", lambda h, j=j, pz=pz: h.activation(rl[j][:], pz[:], AF.Relu),
                                  reads=[PZ], writes=[R("rl", j)])
                            sc.op("dve", lambda h, j=j, k=k, c=c, tsl=tsl: h.tensor_tensor(
                                f1G[k][:, c, tsl], rl[j][:], rl[j][:], ALU.mult),
                                reads=[R("rl", j)], writes=[R("f1G", k)])

                def ffn_f2(g):
                    k = g % 2
                    for i in range(NOWN):
                        for half in range(2):
                            j = 2 + cnt[1] % 4
                            cnt[1] += 1
                            pz, PZ = ps[j], PS[j]
                            sc.pe([lambda h, c=c, pz=pz, i=i, half=half, k=k: h.matmul(
                                pz[:], f1G[k][:, c, i * 128:(i + 1) * 128], wf2[k][:, c, half * 512:(half + 1) * 512],
                                start=(c == 0), stop=(c == 3)) for c in range(4)],
                                reads=[R("f1G", k), R("wf2", k)], writes=[PZ])
                            dst = facc[:, i, half * 512:(half + 1) * 512]
                            if g == 0:
                                sc.op("dve", lambda h, dst=dst, pz=pz: h.tensor_copy(dst, pz[:]),
                                      reads=[PZ], writes=[R("facc", i)])
                            else:
                                sc.op("dve", lambda h, dst=dst, pz=pz: h.tensor_tensor(dst, dst, pz[:], ALU.add),
                                      reads=[PZ, R("facc", i)], writes=[R("facc", i)])
                        if g == NG - 1:
                            final_block(i)

                ffn_f1(0)
                for g in range(NG):
                    if g + 1 < NG:
                        ffn_load(g + 1)
                        ffn_f1(g + 1)
                    ffn_f2(g)

                sc.barrier()
            esW.close()
    return nc


def _consts(hf):
    cbm = np.zeros((128, 5 * 128 + 3 * 512), np.float32)
    j = np.arange(128)[:, None]
    k = np.arange(128)[None, :]
    cbm[:, 0:128] = np.eye(128)
    cbm[:, 128:256] = np.where(j >= k, -1.0, 0.0)
    cbm[:, 256:384] = -1.0
    rot = np.zeros((128, 128), np.float32)
    for b0 in (0, 64):
        for d in range(32):
            rot[b0 + d + 32, b0 + d] = -1.0
            rot[b0 + d, b0 + d + 32] = 1.0
    cbm[:, 384:512] = rot
    m_sb = np.where(j < k, 0.0, NEG)
    m_df = np.where((j // 64) <= (k // 64), 0.0, NEG)
    cbm[:, 640:1152] = np.tile(m_sb, (1, 4))
    cbm[:, 1152:1664] = np.tile(m_df, (1, 4))
    cbm[:, 1664:2176] = NEG if hf == 0 else 0.0
    cfm = np.zeros((128, 260), np.float32)
    cfm[:, 0:128] = 1.0
    inv_freq = (np.float32(10000.0) ** (-(np.arange(0, 64, 2, dtype=np.float32)) / np.float32(64))).astype(np.float32)
    cfm[:, 256] = inv_freq[(np.arange(128) % 64) % 32]
    cfm[:, 257] = -math.pi
    cfm[:, 258] = -0.5
    cfm[:, 259] = EPS
    cfm[:, 128] = math.log(1.0 - LAMBDA_INIT)
    return cbm.astype(ml_dtypes.bfloat16), cfm


_NC_CACHE = {}


def kernel(x, c, positions, w_ada, b_ada, g_pre_mix, w_in, lambda_q1, lambda_k1, lambda_q2, lambda_k2,
           g_subln, w_branch_sb, w_branch_df, w_gate, b_gate, w_out, g_post_mix, g_pre_ffn, w_ff1, w_ff2,
           g_post_ffn):
    f = lambda a: np.ascontiguousarray(np.asarray(a, dtype=np.float32))
    x = f(x)
    B, S, _ = x.shape
    NBLK = S // 128
    NOWN = NBLK // 2
    positions = np.asarray(positions).astype(np.int32)
    if S not in _NC_CACHE:
        _NC_CACHE[S] = build(S)
    nc = _NC_CACHE[S]

    def col(v, n):
        return np.ascontiguousarray(f(v).reshape(n, 128).T)

    def rep(v):
        v = f(v).reshape(1, -1)
        return np.ascontiguousarray(np.broadcast_to(v, (128, v.shape[1])))

    shared = {
        "w_ada": f(w_ada[0]), "bada_col": col(b_ada[0], 48), "bada_rep": rep(b_ada[0]),
        "gpre_col": np.ascontiguousarray(np.concatenate([col(g_pre_mix[0], 8), col(g_pre_ffn[0], 8)], axis=1)),
        "gpost_rep": np.ascontiguousarray(np.concatenate([rep(g_post_mix[0]), rep(g_post_ffn[0])], axis=1)),
        "w_in": f(w_in[0]),
        "lam_rep": np.ascontiguousarray(np.concatenate(
            [rep(lambda_q1[0]), rep(lambda_k1[0]), rep(lambda_q2[0]), rep(lambda_k2[0])], axis=1)),
        "gsub_rep": rep(g_subln[0]), "w_bsb": f(w_branch_sb[0]), "w_bdf": f(w_branch_df[0]),
        "w_gate": f(w_gate[0]), "bgate_col": col(b_gate[0], 16), "w_out": f(w_out[0]),
        "w_ff1": f(w_ff1[0]), "w_ff2": f(w_ff2[0]),
    }
    in_maps = []
    owns = []
    for core in range(8):
        b, hf = core // 2, core % 2
        own = [2 * i + hf for i in range(NOWN)]
        if hf == 0:
            oth = [NBLK - 1] + [2 * o - 1 for o in range(1, NOWN)]
        else:
            oth = [2 * o for o in range(NOWN)]
        owns.append(own)
        blocks = own + oth
        tok = np.concatenate([np.arange(g * 128, (g + 1) * 128) for g in blocks])
        cbm, cfm = _consts(hf)
        m = dict(shared)
        m["x_p"] = np.ascontiguousarray(x[b][tok])
        m["pos_rep"] = np.ascontiguousarray(np.broadcast_to(positions[b][tok][None, :], (128, S))).astype(np.int32)
        m["c_col"] = col(np.asarray(c)[b], 8)
        m["cb"] = cbm
        m["cf"] = cfm
        in_maps.append(m)
    res = run_bass_kernel_spmd(nc, in_maps, core_ids=list(range(8)))
    outp = np.empty((B, S, D), np.float32)
    for core in range(8):
        b = core // 2
        o = np.asarray(res.results[core]["out"], dtype=np.float32)
        for i, gblk in enumerate(owns[core]):
            outp[b, gblk * 128:(gblk + 1) * 128, :] = o[i * 128:(i + 1) * 128, :]
    return outp
```

```python
import math
from contextlib import ExitStack

import numpy as np
import ml_dtypes

import concourse.bass as bass
import concourse.mybir as mybir
from concourse.bass_utils import run_bass_kernel_spmd

F32 = mybir.dt.float32
BF16 = mybir.dt.bfloat16
I32 = mybir.dt.int32
AF = mybir.ActivationFunctionType
ALU = mybir.AluOpType

D = 1024
DC = 8
EPS = 1e-6
NEG = -30000.0
NDS = 24
LAMBDA_INIT = 0.8 - 0.6 * math.exp(0.0)


class Res:
    __slots__ = ("w", "r")

    def __init__(self):
        self.w = None
        self.r = {}


class Sched:
    def __init__(self, nc, es):
        self.nc = nc
        self.eng = {}
        for name, h in (("pe", nc.tensor), ("act", nc.scalar), ("dve", nc.vector),
                        ("pool", nc.gpsimd), ("sp", nc.sync)):
            sem = es.enter_context(nc.semaphore("cnt_" + name))
            self.eng[name] = dict(h=h, sem=sem, n=0, seen={})
        self.es = es
        self.nsw = 0
        self.dsems = [es.enter_context(nc.semaphore("dma%d" % i)) for i in range(NDS)]
        self.dcnt = [0] * NDS
        self.dnext = 0
        self.swd = []
        self.res = {}

    def R(self, *key):
        r = self.res.get(key)
        if r is None:
            r = self.res[key] = Res()
        return r

    def _wait(self, ename, dep):
        key, sem, val = dep
        e = self.eng[ename]
        if e["seen"].get(key, 0) >= val:
            return
        e["h"].wait_ge(sem, val)
        e["seen"][key] = val

    def _sync(self, ename, reads, writes):
        for t in reads:
            if t.w is not None and not (ename == "pe" and t.w[0] == "pe"):
                self._wait(ename, t.w)
        for t in writes:
            if t.w is not None and not (ename == "pe" and t.w[0] == "pe"):
                self._wait(ename, t.w)
            for k, dep in t.r.items():
                if k != ename:
                    self._wait(ename, dep)

    def _mark(self, me, reads, writes):
        for t in reads:
            t.r[me[0]] = me
        for t in writes:
            t.w = me
            t.r = {}

    def op(self, ename, fn, reads=(), writes=()):
        e = self.eng[ename]
        self._sync(ename, reads, writes)
        ins = fn(e["h"])
        e["n"] += 1
        ins.then_inc(e["sem"], 1)
        me = (ename, e["sem"], e["n"])
        self._mark(me, reads, writes)
        return me

    def pe(self, fns, reads=(), writes=()):
        e = self.eng["pe"]
        self._sync("pe", reads, writes)
        ins = None
        for f in fns:
            ins = f(e["h"])
        e["n"] += 1
        ins.then_inc(e["sem"], 1)
        me = ("pe", e["sem"], e["n"])
        self._mark(me, reads, writes)
        return me

    def dma(self, ename, out, in_, reads=(), writes=()):
        e = self.eng[ename]
        self._sync(ename, reads, writes)
        if ename == "pool":
            sem = self.es.enter_context(self.nc.semaphore("swd%d" % self.nsw))
            key = "swd%d" % self.nsw
            self.nsw += 1
            ins = e["h"].dma_start(out=out, in_=in_)
            ins.then_inc(sem, 16)
            me = (key, sem, 16)
            self._mark(me, reads, writes)
            self.swd.append(me)
            return me
        k = self.dnext
        self.dnext = (k + 1) % NDS
        if self.dcnt[k] > 0:
            self._wait(ename, ("dma%d" % k, self.dsems[k], self.dcnt[k]))
        ins = e["h"].dma_start(out=out, in_=in_)
        self.dcnt[k] += 16
        ins.then_inc(self.dsems[k], 16)
        me = ("dma%d" % k, self.dsems[k], self.dcnt[k])
        self._mark(me, reads, writes)
        return me

    def barrier(self):
        names = list(self.eng)
        for a in names:
            for b in names:
                if a != b and self.eng[b]["n"] > 0:
                    self._wait(a, (b, self.eng[b]["sem"], self.eng[b]["n"]))
            for k in range(NDS):
                if self.dcnt[k] > 0:
                    self._wait(a, ("dma%d" % k, self.dsems[k], self.dcnt[k]))
            for dep in self.swd:
                self._wait(a, dep)


def build(S):
    NBLK = S // 128
    NOWN = NBLK // 2
    SO = NOWN * 128
    NT_ALL = S // 512
    NT_OWN = SO // 512
    assert SO % 512 == 0

    nc = bass.Bass("TRN2", target_bir_lowering=False)

    def din(name, shape, dt=F32):
        return nc.dram_tensor(name, list(shape), dt, kind="ExternalInput").ap()

    x_p = din("x_p", [S, D])
    pos_rep = din("pos_rep", [128, S], I32)
    c_col = din("c_col", [128, 8])
    w_ada = din("w_ada", [D, 6 * D])
    bada_col = din("bada_col", [128, 48])
    bada_rep = din("bada_rep", [128, 6 * D])
    gpre_col = din("gpre_col", [128, 16])
    gpost_rep = din("gpost_rep", [128, 2 * D])
    w_in = din("w_in", [D, 3072])
    lam_rep = din("lam_rep", [128, 256])
    gsub_rep = din("gsub_rep", [128, 128])
    w_bsb = din("w_bsb", [512, D])
    w_bdf = din("w_bdf", [512, D])
    w_gate = din("w_gate", [D, 2 * D])
    bgate_col = din("bgate_col", [128, 16])
    w_out = din("w_out", [D, D])
    w_ff1 = din("w_ff1", [D, 4 * D])
    w_ff2 = din("w_ff2", [4 * D, D])
    cb = din("cb", [128, 5 * 128 + 3 * 512], BF16)
    cf = din("cf", [128, 260])
    out = nc.dram_tensor("out", [SO, D], F32, kind="ExternalOutput").ap()

    with ExitStack() as es:
        sc = Sched(nc, es)
        R = sc.R

        def sb(name, shape, dt=F32, stack=es):
            return stack.enter_context(nc.sbuf_tensor(name, list(shape), dt))

        ps = [es.enter_context(nc.psum_tensor("ps%d" % i, [128, 512], F32)) for i in range(8)]
        PS = [R("ps", i) for i in range(8)]

        cbt = sb("cbt", [128, 5 * 128 + 3 * 512], BF16)
        cft = sb("cft", [128, 260])
        ident_b = cbt[:, 0:128]
        negtri = cbt[:, 128:256]
        negones = cbt[:, 256:384]
        rot = cbt[:, 384:512]
        mk_sb = cbt[:, 640:1152]
        mk_df = cbt[:, 1152:1664]
        mk_dm = cbt[:, 1664:2176]
        ones_f = cft[:, 0:128]
        invf = cft[:, 256:257]
        negpi = cft[:, 257:258]
        neghalf = cft[:, 258:259]
        epsc = cft[:, 259:260]
        lnmul = cft[:, 128:129]
        modc = sb("modc", [128, 32])
        gtg = sb("gtg", [128, 2 * D])
        bgc = sb("bgc", [128, 16])
        gsub = sb("gsub", [128, 128])
        lamc = sb("lamc", [128, 4])
        bufX = sb("bufX", [128, DC * SO], BF16)
        bufY = sb("bufY", [128, DC * SO], BF16)
        hT_oth = bufX[:, :].rearrange("p (a s) -> p a s", a=DC)
        h2T = hT_oth
        mergedT = bufY[:, :].rearrange("p (a s) -> p a s", a=DC)
        small = sb("small", [128, 64])
        scol = sb("scol", [128, 8])
        bcol = sb("bcol", [128, 48])
        gpc = sb("gpc", [128, 16])
        esP1 = ExitStack()
        esP1.__enter__()
        hT_own = sb("hT_own", [128, DC, SO], BF16, esP1)
        ysbT = sb("ysbT", [128, 4, SO], BF16, esP1)
        ydfT = sb("ydfT", [128, 4, SO], BF16, esP1)
        RC = R("const")

        sc.dma("sp", cbt[:], cb, writes=[RC])
        sc.dma("sp", cft[:], cf, writes=[RC])
        sc.dma("sp", bgc[:], bgate_col, writes=[RC])
        sc.dma("sp", gsub[:], gsub_rep, writes=[RC])

        def hT(blk):
            if blk < NOWN:
                return hT_own, blk * 128
            return hT_oth, (blk - NOWN) * 128

        def hTtile(tt):
            if tt < NT_OWN:
                return hT_own, tt * 512
            return hT_oth, (tt - NT_OWN) * 512

        def rstd_op(rs_ap, ssq_ap, n, rres, sres, mul=1.0, on_act=True):
            if not on_act:
                sc.op("dve", lambda h: h.tensor_scalar(rs_ap, ssq_ap, 1.0 / n, EPS, ALU.mult, ALU.add),
                      reads=[sres], writes=[rres])
                sc.op("pool", lambda h: h.tensor_tensor(rs_ap, rs_ap, neghalf, ALU.pow),
                      reads=[rres, RC], writes=[rres])
                if mul != 1.0:
                    sc.op("pool", lambda h: h.tensor_scalar(rs_ap, rs_ap, mul, None, ALU.mult),
                          reads=[rres], writes=[rres])
                return
            sc.op("act", lambda h: h.activation(rs_ap, ssq_ap, AF.Ln, bias=epsc, scale=1.0 / n),
                  reads=[sres, RC], writes=[rres])
            if mul == 1.0:
                sc.op("act", lambda h: h.activation(rs_ap, rs_ap, AF.Exp, scale=-0.5),
                      reads=[rres], writes=[rres])
            else:
                sc.op("act", lambda h: h.activation(rs_ap, rs_ap, AF.Exp, bias=lnmul, scale=-0.5),
                      reads=[rres, RC], writes=[rres])

        with ExitStack() as es_att:

            with ExitStack() as es0:
                ccol = sb("ccol", [128, 8], F32, es0)
                wst = [sb("wst%d" % i, [128, 8, 512], F32, es0) for i in range(2)]
                accs = [sb("macc%d" % i, [128, 512], F32, es0) for i in range(2)]
                brow = [sb("brow%d" % i, [128, 512], F32, es0) for i in range(2)]
                gprow = [sb("gprow%d" % i, [128, 512], F32, es0) for i in range(2)]
                lamt = sb("lamt", [128, 256], F32, es0)
                lamj = sb("lamj", [128, 64], F32, es0)
                sc.dma("sp", ccol[:], c_col, writes=[R("ccol")])
                sc.dma("sp", bcol[:], bada_col, writes=[RC])
                sc.dma("sp", gpc[:], gpre_col, writes=[RC])
                sc.dma("sp", lamt[:], lam_rep, writes=[RC])
                sc.barrier()
                sc.op("act", lambda h: h.activation(scol[:], ccol[:], AF.Silu),
                      reads=[R("ccol")], writes=[R("scol")])
                for q in range(2):
                    sc.op("dve", lambda h, q=q: h.scalar_tensor_tensor(
                        lamj[:], lamt[:, q * 128:q * 128 + 64], 1.0, lamt[:, q * 128 + 64:q * 128 + 128],
                        ALU.mult, ALU.mult, accum_out=lamc[:, q:q + 1]),
                        reads=[RC], writes=[R("lamj"), R("lamc")])
                sc.op("act", lambda h: h.activation(lamc[:, 0:2], lamc[:, 0:2], AF.Exp),
                      reads=[R("lamc")], writes=[R("lamc")])
                sc.op("dve", lambda h: h.tensor_scalar(lamc[:, 3:4], lamc[:, 1:2], lamc[:, 0:1], -LAMBDA_INIT,
                                                      ALU.subtract, ALU.add),
                      reads=[R("lamc")], writes=[R("lamc")])

                for ct in range(4):
                    v, half = ct // 2, ct % 2
                    w = wst[ct % 2]
                    acc = accs[ct % 2]
                    RW, RA = R("wst", ct % 2), R("macc", ct % 2)
                    sc.dma("sp", w[:], w_ada[:, ct * 512:(ct + 1) * 512].rearrange("(kc p) n -> p kc n", p=128),
                           writes=[RW])
                    for kc in range(8):
                        if kc == 0:
                            sc.op("dve", lambda h, w=w, acc=acc: h.tensor_scalar(
                                acc[:], w[:, 0, :], scol[:, 0:1], None, ALU.mult),
                                reads=[RW, R("scol")], writes=[RA])
                        else:
                            sc.op("dve", lambda h, w=w, acc=acc, kc=kc: h.scalar_tensor_tensor(
                                acc[:], w[:, kc, :], scol[:, kc:kc + 1], acc[:], ALU.mult, ALU.add),
                                reads=[RW, R("scol"), RA], writes=[RA])
                    pz = ps[ct % 2]
                    PZ = PS[ct % 2]
                    if v in (2, 5):
                        gi = 0 if v == 2 else 1
                        sc.pe([lambda h, pz=pz, acc=acc: h.matmul(pz[:], ones_f, acc[:], start=True, stop=True)],
                              reads=[RA, RC], writes=[PZ])
                        br, gp = brow[ct % 2], gprow[ct % 2]
                        sc.dma("sp", br[:], bada_rep[:, ct * 512:(ct + 1) * 512], writes=[R("brow", ct % 2)])
                        sc.dma("sp", gp[:], gpost_rep[:, gi * D + half * 512: gi * D + (half + 1) * 512],
                               writes=[R("gprow", ct % 2)])
                        sc.op("dve", lambda h, br=br, pz=pz: h.tensor_tensor(br[:], pz[:], br[:], ALU.add),
                              reads=[PZ, R("brow", ct % 2)], writes=[R("brow", ct % 2)])
                        dst = gtg[:, gi * D + half * 512: gi * D + (half + 1) * 512]
                        sc.op("dve", lambda h, br=br, gp=gp, dst=dst: h.tensor_tensor(dst, br[:], gp[:], ALU.mult),
                              reads=[R("brow", ct % 2), R("gprow", ct % 2)], writes=[R("gtg")])
                    else:
                        fns = []
                        for j in range(4):
                            fns.append(lambda h, pz=pz, acc=acc, j=j: h.matmul(
                                pz[:, j:j + 1], acc[:, j * 128:(j + 1) * 128], ones_f[:, 0:1], start=True, stop=True))
                        sc.pe(fns, reads=[RA, RC], writes=[PZ])
                        blkbase = {1: 0, 0: 8, 4: 16, 3: 24}[v]
                        dst = modc[:, blkbase + half * 4: blkbase + half * 4 + 4]
                        bsl = bcol[:, ct * 4:(ct + 1) * 4]
                        if v in (0, 3):
                            sc.op("dve", lambda h, dst=dst, pz=pz, bsl=bsl: h.tensor_tensor(dst, pz[:, 0:4], bsl, ALU.add),
                                  reads=[PZ, RC], writes=[R("modc")])
                        else:
                            gsl = gpc[:, (0 if v == 1 else 8) + half * 4:(0 if v == 1 else 8) + half * 4 + 4]
                            sc.op("dve", lambda h, dst=dst, pz=pz, bsl=bsl: h.scalar_tensor_tensor(
                                dst, pz[:, 0:4], 1.0, bsl, ALU.add, ALU.add),
                                reads=[PZ, RC], writes=[R("modc")])
                            sc.op("dve", lambda h, dst=dst, gsl=gsl: h.tensor_tensor(dst, dst, gsl, ALU.mult),
                                  reads=[R("modc"), RC], writes=[R("modc")])
                sc.barrier()

            def nt_block(gidx, bi, xt_ap, XR, tag, scratch, banks):
                junk, xs, ssq, rs = scratch
                tps = [ps[b][:].bitcast(BF16) for b in banks]
                TP = [PS[b] for b in banks]
                k = (gidx * 4 + bi) % len(xs)
                kj = (gidx * 4 + bi) % len(junk)
                sc.op("act", lambda h: h.activation(junk[kj][:], xt_ap, AF.Square, accum_out=ssq[:, k:k + 1]),
                      reads=[XR], writes=[R(tag + "junk", kj), R(tag + "ssq", k)])
                rstd_op(rs[:, k:k + 1], ssq[:, k:k + 1], D, R(tag + "rs", k), R(tag + "ssq", k))
                sc.op("dve", lambda h: h.tensor_scalar(xs[k][:], xt_ap, rs[:, k:k + 1], None, ALU.mult),
                      reads=[XR, R(tag + "rs", k)], writes=[R(tag + "xs", k)])
                sc.pe([lambda h, j=j: h.transpose(
                    tps[j // 2][:, (j % 2) * 512 + bi * 128:(j % 2) * 512 + (bi + 1) * 128],
                    xs[k][:, j * 128:(j + 1) * 128], ident_b) for j in range(8)],
                    reads=[R(tag + "xs", k), RC], writes=TP)

            def nt_evac(dst_fn, mbase, banks):
                tps = [ps[b][:].bitcast(BF16) for b in banks]
                TP = [PS[b] for b in banks]
                for j in range(8):
                    dst, DR = dst_fn(j)
                    sc.op("dve", lambda h, j=j, dst=dst: h.tensor_scalar(
                        dst, tps[j // 2][:, (j % 2) * 512:(j % 2) * 512 + 512], modc[:, mbase + j:mbase + j + 1],
                        modc[:, mbase + 8 + j:mbase + 9 + j], ALU.mult, ALU.add),
                        reads=[TP[j // 2], R("modc")], writes=[DR])

            def norm_transpose_group(gidx, blocks, dst_fn, mbase, tag, scratch, banks):
                for bi, (xt_ap, XR) in enumerate(blocks):
                    nt_block(gidx, bi, xt_ap, XR, tag, scratch, banks)
                nt_evac(dst_fn, mbase, banks)

            with ExitStack() as esA:
                xb = [sb("xb%d" % i, [128, D], F32, esA) for i in range(9)]
                junk = [sb("junkA%d" % i, [128, D], BF16, esA) for i in range(2)]
                xs = [sb("xsA%d" % i, [128, D], BF16, esA) for i in range(8)]
                ssq = sb("ssqA", [128, 8], F32, esA)
                rs = sb("rsA", [128, 8], F32, esA)
                for g in range(NT_ALL):
                    blocks = []
                    for bi in range(4):
                        t = g * 4 + bi
                        xt = xb[t % 9]
                        XR = R("xb", t % 9)
                        sc.dma("sp", xt[:], x_p[t * 128:(t + 1) * 128, :], writes=[XR])
                        blocks.append((xt[:], XR))
                    tens, off = hTtile(g)

                    def dst_fn(j, tens=tens, off=off, g=g):
                        return tens[:, j, off:off + 512], R("hT", g)
                    norm_transpose_group(g, blocks, dst_fn, 0, "A", (junk, xs, ssq, rs),
                                         [0, 1, 2, 3] if g % 2 == 0 else [4, 5, 6, 7])
                sc.barrier()

            with ExitStack() as esB:
                KT = bufY[:, 0:2 * S].rearrange("p (a s) -> p a s", a=2)
                QM = bufY[:, 2 * S:2 * S + 4 * SO].rearrange("p (a s) -> p a s", a=4)
                Vb = sb("Vb", [128, NBLK * 260], BF16, esB)
                wsl = sb("wsl", [128, DC, 768], BF16, esB)
                Eb = [sb("Eb%d" % i, [128, 512], BF16, esB) for i in range(2)]
                spb = [sb("spb%d" % i, [128, 512], BF16, esB) for i in range(3)]
                wb = [sb("wb%d" % i, [128, 512], BF16, esB) for i in range(3)]
                Rb = [sb("Rb%d" % i, [128, 512], BF16, esB) for i in range(2)]
                ytok = [sb("ytok%d" % i, [128, 256], BF16, esB) for i in range(2)]
                dfar = sb("dfar", [128, 5120], F32, esB)
                t1b = [dfar[:, i * 512:(i + 1) * 512] for i in range(2)]
                t2b = [dfar[:, 1024 + i * 512:1024 + (i + 1) * 512] for i in range(2)]
                angb = dfar[:, 2048:2560]
                ang2 = dfar[:, 2560:3072]
                sinT = dfar[:, 3072:3584]
                cosT = dfar[:, 3584:4096]
                post = dfar[:, 4096:4608].bitcast(I32)
                raw = [dfar[:, 4608 + i * 256:4608 + (i + 1) * 256].bitcast(BF16) for i in range(2)]
                dfd = [sb("dfd%d" % i, [128, 128], F32, esB) for i in range(2)]
                dfj = sb("dfj", [128, 128], F32, esB)
                dfy = [sb("dfy%d" % i, [128, 128], BF16, esB) for i in range(2)]
                dsm = sb("dsm", [128, 16], F32, esB)
                ocp = [sb("ocp%d" % i, [128, 385], F32, esB) for i in range(2)]
                Vsb = Vb[:, :].rearrange("p (t c) -> p t c", c=260)
                wsd = [dfar[:, i * 1024:(i + 1) * 1024].rearrange("p (a n) -> p a n", a=8) for i in range(2)]
                accd = [dfar[:, 2048 + i * 128:2048 + (i + 1) * 128] for i in range(2)]
                brd = [dfar[:, 2304 + i * 128:2304 + (i + 1) * 128] for i in range(2)]
                gpd = [dfar[:, 2560 + i * 128:2560 + (i + 1) * 128] for i in range(2)]
                mod_piece = [0]
                mod_pend = []

                def emit_mod_piece():
                    while mod_pend:
                        mod_pend.pop(0)()
                    p = mod_piece[0]
                    if p >= 32:
                        return
                    mod_piece[0] += 1
                    mod_pend.append(lambda p=p: mod_part2(p))
                    ct, q = 4 + p // 4, p % 4
                    v, half = ct // 2, ct % 2
                    c0 = ct * 512 + q * 128
                    k = p % 2
                    w, acc = wsd[k], accd[k]
                    RW, RA = R("wsd", k), R("accd", k)
                    pz, PZ = ps[k], PS[k]
                    sc.dma("sp", w[:], w_ada[:, c0:c0 + 128].rearrange("(kc p) n -> p kc n", p=128), writes=[RW])
                    sc.op("dve", lambda h: h.tensor_scalar(acc[:], w[:, 0, :], scol[:, 0:1], None, ALU.mult),
                          reads=[RW, R("scol")], writes=[RA])
                    for kc in range(1, 8):
                        sc.op("dve", lambda h, kc=kc: h.scalar_tensor_tensor(
                            acc[:], w[:, kc, :], scol[:, kc:kc + 1], acc[:], ALU.mult, ALU.add),
                            reads=[RW, R("scol"), RA], writes=[RA])
                    if v in (2, 5):
                        gi = 0 if v == 2 else 1
                        goff = gi * D + half * 512 + q * 128
                        br, gp = brd[k], gpd[k]
                        sc.dma("sp", br[:], bada_rep[:, c0:c0 + 128], writes=[R("brd", k)])
                        sc.dma("sp", gp[:], gpost_rep[:, goff:goff + 128], writes=[R("gpd", k)])

                def mod_part2(p):
                    ct, q = 4 + p // 4, p % 4
                    v, half = ct // 2, ct % 2
                    c0 = ct * 512 + q * 128
                    k = p % 2
                    w, acc = wsd[k], accd[k]
                    RW, RA = R("wsd", k), R("accd", k)
                    pz, PZ = ps[k], PS[k]
                    if v in (2, 5):
                        gi = 0 if v == 2 else 1
                        goff = gi * D + half * 512 + q * 128
                        br, gp = brd[k], gpd[k]
                        sc.pe([lambda h: h.matmul(pz[:, 0:128], ones_f, acc[:], start=True, stop=True)],
                              reads=[RA, RC], writes=[PZ])
                        sc.op("dve", lambda h: h.tensor_tensor(br[:], pz[:, 0:128], br[:], ALU.add),
                              reads=[PZ, R("brd", k)], writes=[R("brd", k)])
                        sc.op("dve", lambda h: h.tensor_tensor(gtg[:, goff:goff + 128], br[:], gp[:], ALU.mult),
                              reads=[R("brd", k), R("gpd", k)], writes=[R("gtg")])
                    else:
                        sc.pe([lambda h: h.matmul(pz[:, 0:1], acc[:, 0:128], ones_f[:, 0:1], start=True, stop=True)],
                              reads=[RA, RC], writes=[PZ])
                        blkbase = {4: 16, 3: 24}[v]
                        dst = modc[:, blkbase + half * 4 + q: blkbase + half * 4 + q + 1]
                        bsl = bcol[:, ct * 4 + q:ct * 4 + q + 1]
                        if v == 3:
                            sc.op("dve", lambda h: h.tensor_tensor(dst, pz[:, 0:1], bsl, ALU.add),
                                  reads=[PZ, RC], writes=[R("modc")])
                        else:
                            gsl = gpc[:, 8 + half * 4 + q:8 + half * 4 + q + 1]
                            sc.op("dve", lambda h: h.scalar_tensor_tensor(dst, pz[:, 0:1], 1.0, bsl, ALU.add, ALU.add),
                                  reads=[PZ, RC], writes=[R("modc")])
                            sc.op("dve", lambda h: h.tensor_tensor(dst, dst, gsl, ALU.mult),
                                  reads=[R("modc"), RC], writes=[R("modc")])
                sc.op("pool", lambda h: h.memset(QM, 0.0), writes=[R("QM", t) for t in range(NT_OWN)])

                def rope_tables(tt):
                    TWO_PI = 2 * math.pi
                    C1 = 6.28125
                    C2 = TWO_PI - C1
                    sc.dma("sp", post[:], pos_rep[:, tt * 512:(tt + 1) * 512], writes=[R("post")])
                    sc.op("dve", lambda h: h.tensor_copy(angb[:], post[:]), reads=[R("post")], writes=[R("angb")])
                    sc.op("dve", lambda h: h.tensor_scalar(angb[:], angb[:], invf, None, ALU.mult),
                          reads=[R("angb"), RC], writes=[R("angb")])
                    sc.op("dve", lambda h: h.tensor_scalar(ang2[:], angb[:], 1.0 / TWO_PI, None, ALU.mult),
                          reads=[R("angb")], writes=[R("ang2")])
                    sc.op("dve", lambda h: h.tensor_copy(post[:], ang2[:]), reads=[R("ang2")], writes=[R("post")])
                    sc.op("dve", lambda h: h.tensor_copy(ang2[:], post[:]), reads=[R("post")], writes=[R("ang2")])
                    sc.op("dve", lambda h: h.scalar_tensor_tensor(angb[:], ang2[:], -C1, angb[:], ALU.mult, ALU.add),
                          reads=[R("ang2"), R("angb")], writes=[R("angb")])
                    sc.op("dve", lambda h: h.scalar_tensor_tensor(angb[:], ang2[:], -C2, angb[:], ALU.mult, ALU.add),
                          reads=[R("ang2"), R("angb")], writes=[R("angb")])

                    def fold(dst_tab, DR):
                        sc.op("dve", lambda h: h.tensor_scalar(ang2[:], angb[:], math.pi, None, ALU.is_gt),
                              reads=[R("angb")], writes=[R("ang2")])
                        sc.op("dve", lambda h: h.scalar_tensor_tensor(angb[:], ang2[:], -TWO_PI, angb[:], ALU.mult, ALU.add),
                              reads=[R("ang2"), R("angb")], writes=[R("angb")])
                        sc.op("dve", lambda h: h.tensor_scalar(ang2[:], angb[:], -math.pi, None, ALU.is_lt),
                              reads=[R("angb")], writes=[R("ang2")])
                        sc.op("dve", lambda h: h.scalar_tensor_tensor(angb[:], ang2[:], TWO_PI, angb[:], ALU.mult, ALU.add),
                              reads=[R("ang2"), R("angb")], writes=[R("angb")])
                        sc.op("dve", lambda h: h.tensor_scalar(ang2[:], angb[:], -math.pi, math.pi, ALU.max, ALU.min),
                              reads=[R("angb")], writes=[R("ang2")])
                        sc.op("act", lambda h: h.activation(dst_tab[:], ang2[:], AF.Sin),
                              reads=[R("ang2")], writes=[DR])
                    fold(sinT, R("sinT"))
                    sc.op("dve", lambda h: h.tensor_scalar(angb[:], angb[:], 0.5 * math.pi, None, ALU.add),
                          reads=[R("angb")], writes=[R("angb")])
                    fold(cosT, R("cosT"))

                pcount = [0]

                def project(col0, tt, is_df, scale, dst_fn):
                    k = pcount[0] % 2
                    pcount[0] += 1
                    pz, PZ = ps[k], PS[k]
                    rz, RZ = ps[7], PS[7]
                    tens, off = hTtile(tt)
                    sc.pe([lambda h, dc=dc: h.matmul(pz[:], wsl[:, dc, col0:col0 + 128], tens[:, dc, off:off + 512],
                                                     start=(dc == 0), stop=(dc == 7)) for dc in range(8)],
                          reads=[R("wsl"), R("hTall")], writes=[PZ])
                    if not is_df:
                        for (dst, DR, lo, hi) in dst_fn():
                            sc.op("dve", lambda h, dst=dst, lo=lo, hi=hi: h.tensor_scalar(
                                dst, pz[lo:hi, :], scale, None, ALU.mult), reads=[PZ], writes=[DR])
                        return
                    rw, RW = raw[k], R("raw", k)
                    sc.op("dve", lambda h: h.tensor_copy(rw[:], pz[:]), reads=[PZ], writes=[RW])
                    sc.pe([lambda h: h.matmul(rz[:], rot, rw[:], start=True, stop=True)],
                          reads=[RW, RC], writes=[RZ])
                    t1, t2 = t1b[k], t2b[k]
                    sc.op("dve", lambda h: h.scalar_tensor_tensor(t1[:], pz[:], scale, cosT[:], ALU.mult, ALU.mult),
                          reads=[PZ, R("cosT")], writes=[R("t1", k)])
                    sc.op("dve", lambda h: h.scalar_tensor_tensor(t2[:], rz[:], scale, sinT[:], ALU.mult, ALU.mult),
                          reads=[RZ, R("sinT")], writes=[R("t2", k)])
                    for (dst, DR, lo, hi) in dst_fn():
                        sc.op("pool", lambda h, dst=dst, lo=lo, hi=hi: h.tensor_tensor(
                            dst, t1[lo:hi, :], t2[lo:hi, :], ALU.add),
                            reads=[R("t1", k), R("t2", k)], writes=[DR])

                wsl_loaded = [False]

                def load_wsl(is_df, g):
                    if is_df:
                        qc, kc_, vc = 1536 + g * 256, 2048 + g * 256, 2560 + g * 256
                    else:
                        qc, kc_, vc = g * 256, 512 + g * 256, 1024 + g * 256
                    for n, c0 in enumerate((qc, kc_, vc)):
                        sc.dma("pool", wsl[:, :, n * 256:(n + 1) * 256],
                               w_in[:, c0:c0 + 256].rearrange("(kc p) n -> p kc n", p=128), writes=[R("wsl")])

                def run_pass(is_df, g, next_pass):
                    if not wsl_loaded[0]:
                        load_wsl(is_df, g)
                    wsl_loaded[0] = False
                    if is_df:
                        allV = [R("V", t) for t in range(NT_ALL)]
                        sc.op("pool", lambda h: h.memset(Vsb[:, :, 128:129], 1.0), writes=allV)
                        sc.op("pool", lambda h: h.memset(Vsb[:, :, 258:259], 1.0), writes=allV)

                    def vproj(t2_):
                        k = pcount[0] % 2
                        pcount[0] += 1
                        pz, PZ = ps[k], PS[k]
                        fns = []
                        for u in range(2):
                            tens, off = hT(2 * t2_ + u)
                            for dc in range(8):
                                fns.append(lambda h, u=u, dc=dc, tens=tens, off=off: h.matmul(
                                    pz[:, u * 256:(u + 1) * 256], tens[:, dc, off:off + 128], wsl[:, dc, 512:768],
                                    start=(dc == 0), stop=(dc == 7)))
                        sc.pe(fns, reads=[R("wsl"), R("hTall")], writes=[PZ])
                        pzv = pz[:, :].rearrange("p (u c) -> p u c", c=256)
                        VR = R("V", (2 * t2_) // 4)
                        if not is_df:
                            sc.op("dve", lambda h: h.tensor_copy(Vsb[:, 2 * t2_:2 * t2_ + 2, 0:256], pzv),
                                  reads=[PZ], writes=[VR])
                        else:
                            for hd in range(2):
                                sc.op("dve", lambda h, hd=hd: h.tensor_copy(
                                    Vsb[:, 2 * t2_:2 * t2_ + 2, hd * 130:hd * 130 + 128],
                                    pzv[:, :, hd * 128:(hd + 1) * 128]), reads=[PZ], writes=[VR])

                    chunks = []
                    for tp_ in range(NT_OWN):
                        lst = []
                        for tt in (tp_, NT_OWN + tp_):
                            if is_df:
                                lst.append(lambda tt=tt: rope_tables(tt))
                            for p in range(2):
                                def kd(p=p, tt=tt):
                                    return [(KT[:, p, tt * 512:(tt + 1) * 512], R("KT", tt), 0, 128)]
                                lst.append(lambda p=p, tt=tt, kd=kd: project(256 + p * 128, tt, is_df, 1.0, kd))
                            if tt < NT_OWN:
                                for p in range(2):
                                    def qd(p=p, tt=tt):
                                        return [(QM[0:64, 2 * p, tt * 512:(tt + 1) * 512], R("QM", tt), 0, 64),
                                                (QM[64:128, 2 * p + 1, tt * 512:(tt + 1) * 512], R("QM", tt), 64, 128)]
                                    lst.append(lambda p=p, tt=tt, qd=qd: project(p * 128, tt, is_df, 0.125, qd))
                            for t2_ in (2 * tt, 2 * tt + 1):
                                lst.append(lambda t2_=t2_: vproj(t2_))
                        chunks.append(lst)
                    chunk_state = [0]

                    def emit_chunk_items(nitems):
                        while nitems > 0 and chunk_state[0] < len(chunks):
                            lst = chunks[chunk_state[0]]
                            if not lst:
                                chunk_state[0] += 1
                                if chunk_state[0] == len(chunks) and next_pass is not None:
                                    load_wsl(*next_pass)
                                    wsl_loaded[0] = True
                                continue
                            lst.pop(0)()
                            nitems -= 1
                        while chunk_state[0] < len(chunks) and not chunks[chunk_state[0]]:
                            chunk_state[0] += 1
                            if chunk_state[0] == len(chunks) and next_pass is not None:
                                load_wsl(*next_pass)
                                wsl_loaded[0] = True

                    def ensure_chunks(gq):
                        while chunk_state[0] <= gq and chunk_state[0] < len(chunks):
                            emit_chunk_items(1)

                    units = []
                    for i in range(NOWN):
                        n = 2 * i + 2
                        for s in range(n):
                            if s % 2 == 0:
                                kb = i - s // 2
                            else:
                                kb = NOWN + i - s // 2
                            units.append((i, s, n, kb))
                    NU = len(units)
                    zb = [3, 4, 5]
                    yb = [6, 2]

                    if is_df:
                        ensure_chunks(len(chunks) - 1)

                    def drip(u):
                        gq = units[u][0] // 4
                        if chunk_state[0] == gq + 1 and chunk_state[0] < len(chunks):
                            emit_chunk_items(1)

                    def stA(u):
                        i, s, n, kb = units[u]
                        z, Z = ps[zb[u % 3]], PS[zb[u % 3]]
                        fns = []
                        plain = is_df and not (s == 0 or s == n - 1)
                        for hh in range(4):
                            fns.append(lambda h, hh=hh: h.matmul(
                                z[:, hh * 128:(hh + 1) * 128], KT[:, hh // 2, kb * 128:(kb + 1) * 128],
                                QM[:, hh, i * 128:(i + 1) * 128], start=(hh == 0), stop=plain, skip_group_check=True))
                        if s == 0:
                            mk = mk_df if is_df else mk_sb
                            fns.append(lambda h: h.matmul(z[:], ident_b, mk, start=False, stop=is_df,
                                                          skip_group_check=True))
                        elif s == n - 1:
                            fns.append(lambda h: h.matmul(z[:], ident_b, mk_dm, start=False, stop=is_df,
                                                          skip_group_check=True))
                        ensure_chunks(i // 4)
                        sc.pe(fns, reads=[R("KT", kb // 4), R("QM", i // 4), RC], writes=[Z])

                    def stB_sb(u):
                        z, Z = ps[zb[u % 3]], PS[zb[u % 3]]
                        E, sp = Eb[u % 2], spb[u % 3]
                        sc.op("act", lambda h: h.activation(E[:], z[:], AF.Exp), reads=[Z], writes=[R("E", u % 2)])
                        sc.op("act", lambda h: h.activation(sp[:], E[:], AF.Ln, bias=1.0),
                              reads=[R("E", u % 2)], writes=[R("sp", u % 3)])

                    def stE_sb(u):
                        i, s, n, kb = units[u]
                        if s == n - 1:
                            return
                        sp = spb[u % 3]
                        if s == 0:
                            sc.op("pool", lambda h: h.tensor_copy(Rb[1][:], sp[:]),
                                  reads=[R("sp", u % 3)], writes=[R("Rb", 1)])
                        else:
                            a, b = Rb[s % 2], Rb[(s + 1) % 2]
                            sc.op("pool", lambda h: h.tensor_tensor(b[:], a[:], sp[:], ALU.add),
                                  reads=[R("sp", u % 3), R("Rb", s % 2)], writes=[R("Rb", (s + 1) % 2)])

                    def stC_sb(u):
                        i, s, n, kb = units[u]
                        z, Z = ps[zb[u % 3]], PS[zb[u % 3]]
                        sp = spb[u % 3]
                        fns = []
                        rd = [R("sp", u % 3), RC, Z]
                        for hh in range(4):
                            fns.append(lambda h, hh=hh: h.matmul(
                                z[:, hh * 128:(hh + 1) * 128], negtri, sp[:, hh * 128:(hh + 1) * 128],
                                start=False, stop=(s == 0), skip_group_check=True))
                        if s > 0:
                            rb = Rb[s % 2]
                            rd.append(R("Rb", s % 2))
                            for hh in range(4):
                                fns.append(lambda h, hh=hh: h.matmul(
                                    z[:, hh * 128:(hh + 1) * 128], negones, rb[:, hh * 128:(hh + 1) * 128],
                                    start=False, stop=True, skip_group_check=True))
                        sc.pe(fns, reads=rd, writes=[Z])

                    def stD(u):
                        z, Z = ps[zb[u % 3]], PS[zb[u % 3]]
                        w_ = wb[u % 3]
                        sc.op("act", lambda h: h.activation(w_[:], z[:], AF.Exp), reads=[Z], writes=[R("w", u % 3)])

                    def stF(u):
                        i, s, n, kb = units[u]
                        w_ = wb[u % 3]
                        fns = []
                        if not is_df:
                            y, Y = ps[yb[i % 2]], PS[yb[i % 2]]
                            for hh in range(4):
                                fns.append(lambda h, hh=hh: h.matmul(
                                    y[:, hh * 64:(hh + 1) * 64], w_[:, hh * 128:(hh + 1) * 128],
                                    Vsb[:, kb, hh * 64:(hh + 1) * 64], start=(s == 0 and hh == 0), stop=(s == n - 1),
                                    skip_group_check=True))
                            sc.pe(fns, reads=[R("w", u % 3), R("V", kb // 4)], writes=[Y])
                        else:
                            for c in range(4):
                                y = ps[yb[c // 2]]
                                fns.append(lambda h, c=c, y=y: h.matmul(
                                    y[:, (c % 2) * 256:(c % 2) * 256 + 129], w_[:, c * 128:(c + 1) * 128],
                                    Vsb[:, kb, (c // 2) * 130:(c // 2) * 130 + 129], start=(s == 0 and c % 2 == 0), stop=(s == n - 1),
                                    skip_group_check=True))
                            sc.pe(fns, reads=[R("w", u % 3), R("V", kb // 4)], writes=[PS[yb[0]], PS[yb[1]]])
                        if s == n - 1:
                            delay = min(8, 2 * i + 3)
                            if is_df:
                                fin_df(i)
                                pending.append((u + delay, lambda i=i: fin_df2(i)))
                            else:
                                fin_sb(i)
                                pending.append((u + delay, lambda i=i: fin_sb2(i)))

                    pending = []

                    def flush(u):
                        while pending and pending[0][0] <= u:
                            pending.pop(0)[1]()

                    def fin_sb(i):
                        y, Y = ps[yb[i % 2]], PS[yb[i % 2]]
                        yt = ytok[i % 2]
                        sc.op("dve", lambda h: h.tensor_copy(yt[:], y[:, 0:256]), reads=[Y], writes=[R("ytok", i % 2)])

                    def fin_sb2(i):
                        yt = ytok[i % 2]
                        tp = ps[7][:].bitcast(BF16)
                        sc.pe([lambda h, p=p: h.transpose(tp[:, p * 128:(p + 1) * 128], yt[:, p * 128:(p + 1) * 128], ident_b)
                               for p in range(2)], reads=[R("ytok", i % 2), RC], writes=[PS[7]])
                        tpv = tp[:, 0:256].rearrange("p (a c) -> p a c", c=128)
                        sc.op("dve", lambda h: h.tensor_copy(ysbT[:, 2 * g:2 * g + 2, i * 128:(i + 1) * 128], tpv),
                              reads=[PS[7]], writes=[R("ysbT")])

                    def fin_df(i):
                        tp = ps[7][:].bitcast(BF16)
                        for hd in range(2):
                            sc.op("dve", lambda h, hd=hd: h.tensor_copy(ocp[hd][:], ps[yb[hd]][:, 0:385]),
                                  reads=[PS[yb[hd]]], writes=[R("ocp", hd)])
                        for hd in range(2):
                            y, Y = ocp[hd], R("ocp", hd)
                            dd, DDR = dfd[hd], R("dfd", hd)
                            b0 = hd * 8
                            sc.op("dve", lambda h, y=y, b0=b0: h.reciprocal(dsm[:, b0:b0 + 1], y[:, 128:129]),
                                  reads=[Y], writes=[R("dsm", hd)])
                            sc.op("dve", lambda h, y=y, b0=b0: h.reciprocal(dsm[:, b0 + 1:b0 + 2], y[:, 256 + 128:256 + 129]),
                                  reads=[Y], writes=[R("dsm", hd)])
                            sc.op("dve", lambda h, b0=b0: h.tensor_tensor(dsm[:, b0 + 1:b0 + 2], dsm[:, b0 + 1:b0 + 2],
                                                                          lamc[:, 3:4], ALU.mult),
                                  reads=[R("dsm", hd), R("lamc")], writes=[R("dsm", hd)])
                            sc.op("dve", lambda h, y=y, dd=dd, b0=b0: h.tensor_scalar(
                                dd[:], y[:, 0:128], dsm[:, b0:b0 + 1], None, ALU.mult),
                                reads=[Y, R("dsm", hd)], writes=[DDR])
                            sc.op("dve", lambda h, y=y, dd=dd, b0=b0: h.scalar_tensor_tensor(
                                dd[:], y[:, 256:384], dsm[:, b0 + 1:b0 + 2], dd[:], ALU.mult, ALU.add),
                                reads=[Y, R("dsm", hd), DDR], writes=[DDR])
                            sc.op("dve", lambda h, dd=dd, b0=b0: h.scalar_tensor_tensor(
                                dfj[:], dd[:], 1.0, dd[:], ALU.mult, ALU.mult, accum_out=dsm[:, b0 + 2:b0 + 3]),
                                reads=[DDR], writes=[R("dfj"), R("dsm", hd)])
                            rstd_op(dsm[:, b0 + 3:b0 + 4], dsm[:, b0 + 2:b0 + 3], 128, R("dsm", hd), R("dsm", hd),
                                    mul=1.0 - LAMBDA_INIT, on_act=False)
                            yy = dfy[hd]
                            sc.op("dve", lambda h, dd=dd, yy=yy, b0=b0: h.scalar_tensor_tensor(
                                yy[:], dd[:], dsm[:, b0 + 3:b0 + 4], gsub[:], ALU.mult, ALU.mult),
                                reads=[DDR, R("dsm", hd), RC], writes=[R("dfy", hd)])

                    def fin_df2(i):
                        tp = ps[7][:].bitcast(BF16)
                        sc.pe([lambda h, hd=hd: h.transpose(tp[:, hd * 128:(hd + 1) * 128], dfy[hd][:], ident_b)
                               for hd in range(2)], reads=[R("dfy", 0), R("dfy", 1), RC], writes=[PS[7]])
                        tpv = tp[:, 0:256].rearrange("p (a c) -> p a c", c=128)
                        sc.op("dve", lambda h: h.tensor_copy(ydfT[:, 2 * g:2 * g + 2, i * 128:(i + 1) * 128], tpv),
                              reads=[PS[7]], writes=[R("ydfT")])

                    if not is_df:
                        stA(0)
                        stA(1)
                        stB_sb(0)
                        for u in range(NU):
                            if g == 0 and u % 8 == 4:
                                emit_mod_piece()
                            drip(u)
                            stC_sb(u)
                            stE_sb(u)
                            if u >= 1:
                                stD(u - 1)
                                stF(u - 1)
                                flush(u - 1)
                            if u + 2 < NU:
                                stA(u + 2)
                            if u + 1 < NU:
                                stB_sb(u + 1)
                        stD(NU - 1)
                        stF(NU - 1)
                        flush(10 ** 9)
                        if g == 0:
                            while mod_piece[0] < 32 or mod_pend:
                                emit_mod_piece()
                    else:
                        stA(0)
                        stA(1)
                        for u in range(NU):
                            drip(u)
                            stD(u)
                            if u + 2 < NU:
                                stA(u + 2)
                            stF(u)
                            flush(u)
                        flush(10 ** 9)

                hall = R("hTall")
                for t in range(NBLK):
                    hall.w = R("hT", t).w if R("hT", t).w is not None else hall.w
                sc.barrier()
                run_pass(False, 0, (False, 1))
                run_pass(False, 1, (True, 0))
                sc.barrier()
                run_pass(True, 0, (True, 1))
                run_pass(True, 1, None)
                sc.barrier()
        sc.barrier()

        with ExitStack() as esCD:
            with ExitStack() as esC:
                with ExitStack() as esC1:
                    wg = sb("wg", [128, DC, 2 * D], BF16, esC1)
                    wbs = sb("wbs", [128, 4, D], BF16, esC1)
                    wbd = sb("wbd", [128, 4, D], BF16, esC1)
                    gsb_ = [sb("gsb%d" % i, [128, 512], F32, esC1) for i in range(2)]
                    gdf_ = [sb("gdf%d" % i, [128, 512], F32, esC1) for i in range(2)]
                    t1c = [sb("t1c%d" % i, [128, 512], F32, esC1) for i in range(2)]
                    t2c = [sb("t2c%d" % i, [128, 512], F32, esC1) for i in range(2)]
                    def load_wg(q):
                        for hh in range(2):
                            c0 = hh * D + q * 256
                            sc.dma("pool", wg[:, :, c0:c0 + 256],
                                   w_gate[:, c0:c0 + 256].rearrange("(kc p) n -> p kc n", p=128),
                                   writes=[R("wg", hh, q)])
                    load_wg(0)
                    sc.dma("pool", wbs[:], w_bsb.rearrange("(c p) n -> p c n", p=128), writes=[R("wbs")])
                    sc.dma("pool", wbd[:], w_bdf.rearrange("(c p) n -> p c n", p=128), writes=[R("wbd")])
                    for q in range(1, 4):
                        load_wg(q)
                    it = 0
                    for tt in range(NT_OWN):
                        tsl = slice(tt * 512, (tt + 1) * 512)
                        for dc in range(8):
                            k = it % 2
                            it += 1
                            pg, pd, pbs, pbd = ps[4 * k], ps[4 * k + 1], ps[4 * k + 2], ps[4 * k + 3]
                            PG, PD, PBS, PBD = PS[4 * k], PS[4 * k + 1], PS[4 * k + 2], PS[4 * k + 3]
                            sc.pe([lambda h, kk=kk, pg=pg, dc=dc, tsl=tsl: h.matmul(
                                pg[:], wg[:, kk, dc * 128:(dc + 1) * 128], hT_own[:, kk, tsl],
                                start=(kk == 0), stop=(kk == 7)) for kk in range(8)],
                                reads=[R("wg", 0, dc // 2)], writes=[PG])
                            sc.pe([lambda h, kk=kk, pd=pd, dc=dc, tsl=tsl: h.matmul(
                                pd[:], wg[:, kk, D + dc * 128:D + (dc + 1) * 128], hT_own[:, kk, tsl],
                                start=(kk == 0), stop=(kk == 7)) for kk in range(8)],
                                reads=[R("wg", 1, dc // 2)], writes=[PD])
                            sc.pe([lambda h, c=c, pbs=pbs, dc=dc, tsl=tsl: h.matmul(
                                pbs[:], wbs[:, c, dc * 128:(dc + 1) * 128], ysbT[:, c, tsl],
                                start=(c == 0), stop=(c == 3)) for c in range(4)],
                                reads=[R("wbs"), R("ysbT")], writes=[PBS])
                            sc.pe([lambda h, c=c, pbd=pbd, dc=dc, tsl=tsl: h.matmul(
                                pbd[:], wbd[:, c, dc * 128:(dc + 1) * 128], ydfT[:, c, tsl],
                                start=(c == 0), stop=(c == 3)) for c in range(4)],
                                reads=[R("wbd"), R("ydfT")], writes=[PBD])
                            sc.op("act", lambda h, k=k, pg=pg, dc=dc: h.activation(
                                gsb_[k][:], pg[:], AF.Sigmoid, bias=bgc[:, dc:dc + 1]),
                                reads=[PG, RC], writes=[R("gsb", k)])
                            sc.op("act", lambda h, k=k, pd=pd, dc=dc: h.activation(
                                gdf_[k][:], pd[:], AF.Sigmoid, bias=bgc[:, 8 + dc:9 + dc]),
                                reads=[PD, RC], writes=[R("gdf", k)])
                            sc.op("dve", lambda h, k=k, pbs=pbs: h.tensor_tensor(t1c[k][:], pbs[:], gsb_[k][:], ALU.mult),
                                  reads=[PBS, R("gsb", k)], writes=[R("t1c", k)])
                            sc.op("dve", lambda h, k=k, pbd=pbd: h.tensor_tensor(t2c[k][:], pbd[:], gdf_[k][:], ALU.mult),
                                  reads=[PBD, R("gdf", k)], writes=[R("t2c", k)])
                            sc.op("pool", lambda h, k=k, dc=dc, tsl=tsl: h.tensor_tensor(
                                mergedT[:, dc, tsl], t1c[k][:], t2c[k][:], ALU.add),
                                reads=[R("t1c", k), R("t2c", k)], writes=[R("mergedT")])
                    sc.barrier()
                esP1.close()

                esW = ExitStack()
                esW.__enter__()
                wf1 = [sb("wf1_%d" % i, [128, DC, 512], BF16, esW) for i in range(2)]
                wf2 = [sb("wf2_%d" % i, [128, 4, D], BF16, esW) for i in range(2)]
                with ExitStack() as esC2:
                    wo = sb("wo", [128, DC, D], BF16, esC2)
                    xb2 = [sb("xb2_%d" % i, [128, D], F32, esC2) for i in range(3)]
                    x1b = [sb("x1b%d" % i, [128, D], F32, esC2) for i in range(4)]
                    junk = [sb("junkC%d" % i, [128, D], BF16, esC2) for i in range(2)]
                    xs = [sb("xsC%d" % i, [128, D], BF16, esC2) for i in range(8)]
                    ssq = sb("ssqC", [128, 8], F32, esC2)
                    rs = sb("rsC", [128, 8], F32, esC2)
                    ssm = sb("ssm", [128, 8], F32, esC2)
                    sc.dma("pool", wo[:], w_out.rearrange("(kc p) n -> p kc n", p=128), writes=[R("wo")])
                    sc.dma("pool", wf1[0][:], w_ff1[:, 0:512].rearrange("(kc p) n -> p kc n", p=128),
                           writes=[R("wf1", 0)])
                    sc.dma("pool", wf2[0][:], w_ff2[0:512, :].rearrange("(c p) n -> p c n", p=128),
                           writes=[R("wf2", 0)])
                    scr = (junk, xs, ssq, rs)

                    def c2_s1(i):
                        k, k3 = i % 2, i % 3
                        mm = ((ps[2 * k], PS[2 * k]), (ps[2 * k + 1], PS[2 * k + 1]))
                        xt, XR = xb2[k3], R("xb2", k3)
                        sc.dma("sp", xt[:], x_p[i * 128:(i + 1) * 128, :], writes=[XR])
                        for half, (m, M) in enumerate(mm):
                            sc.pe([lambda h, dc=dc, m=m, half=half: h.matmul(
                                m[:], mergedT[:, dc, i * 128:(i + 1) * 128], wo[:, dc, half * 512:(half + 1) * 512],
                                start=(dc == 0), stop=(dc == 7)) for dc in range(8)],
                                reads=[R("mergedT"), R("wo")], writes=[M])
                            sc.op("act", lambda h, m=m, half=half: h.activation(
                                junk[k][:, half * 512:(half + 1) * 512], m[:], AF.Square,
                                accum_out=ssm[:, 4 * k + half:4 * k + half + 1]),
                                reads=[M], writes=[R("Cjunk", k), R("ssm", k)])

                    def c2_s2(i):
                        k, k3 = i % 2, i % 3
                        mm = ((ps[2 * k], PS[2 * k]), (ps[2 * k + 1], PS[2 * k + 1]))
                        xt, XR = xb2[k3], R("xb2", k3)
                        sc.op("dve", lambda h: h.tensor_tensor(
                            ssm[:, 4 * k + 2:4 * k + 3], ssm[:, 4 * k:4 * k + 1], ssm[:, 4 * k + 1:4 * k + 2], ALU.add),
                            reads=[R("ssm", k)], writes=[R("ssm", k)])
                        rstd_op(ssm[:, 4 * k + 3:4 * k + 4], ssm[:, 4 * k + 2:4 * k + 3], D, R("ssm", k), R("ssm", k))
                        x1, X1 = x1b[i % 4], R("x1b", i % 4)
                        for half, (m, M) in enumerate(mm):
                            sc.op("dve", lambda h, m=m, half=half: h.scalar_tensor_tensor(
                                x1[:, half * 512:(half + 1) * 512], m[:], ssm[:, 4 * k + 3:4 * k + 4],
                                gtg[:, half * 512:(half + 1) * 512], ALU.mult, ALU.mult),
                                reads=[M, R("ssm", k), R("gtg")], writes=[X1])
                        sc.op("dve", lambda h: h.tensor_tensor(x1[:], x1[:], xt[:], ALU.add),
                              reads=[X1, XR], writes=[X1])
                        sc.dma("sp", out[i * 128:(i + 1) * 128, :], x1[:], reads=[X1], writes=[R("out", i)])

                    def c2_s3(i):
                        g, bi = i // 4, i % 4
                        x1, X1 = x1b[i % 4], R("x1b", i % 4)
                        nt_block(g, bi, x1[:], X1, "C", scr, [4, 5, 6, 7])
                        if bi == 3:
                            def dst_fn(j, g=g):
                                return h2T[:, j, g * 512:(g + 1) * 512], R("h2T")
                            nt_evac(dst_fn, 16, [4, 5, 6, 7])

                    for n in range(NOWN + 3):
                        if n < NOWN:
                            c2_s1(n)
                        if 0 <= n - 1 < NOWN:
                            c2_s2(n - 1)
                        if 0 <= n - 3 < NOWN:
                            c2_s3(n - 3)
                    sc.barrier()
            sc.barrier()

            with ExitStack() as esD:
                facc = sb("facc", [128, NOWN, D], F32, esD)
                f1G = [bufY[:, i * 4 * SO:(i + 1) * 4 * SO].rearrange("p (a s) -> p a s", a=4) for i in range(2)]
                rl = [sb("rl%d" % i, [128, 512], F32, esD) for i in range(2)]
                xr = [sb("xr%d" % i, [128, D], F32, esD) for i in range(4)]
                junkE = sb("junkE", [128, D], BF16, esD)
                sse = sb("sse", [128, 4], F32, esD)
                NG = 8
                cnt = [0, 0]

                def final_block(i):
                    k = i % 2
                    kx = i % 4
                    sc.op("act", lambda h: h.activation(junkE[:], facc[:, i, :], AF.Square,
                                                        accum_out=sse[:, 2 * k:2 * k + 1]),
                          reads=[R("facc", i)], writes=[R("junkE"), R("sse", k)])
                    rstd_op(sse[:, 2 * k + 1:2 * k + 2], sse[:, 2 * k:2 * k + 1], D, R("sse", k), R("sse", k))
                    sc.dma("sp", xr[kx][:], out[i * 128:(i + 1) * 128, :], reads=[R("out", i)], writes=[R("xr", kx)])
                    sc.op("dve", lambda h: h.scalar_tensor_tensor(
                        facc[:, i, :], facc[:, i, :], sse[:, 2 * k + 1:2 * k + 2], gtg[:, D:2 * D], ALU.mult, ALU.mult),
                        reads=[R("facc", i), R("sse", k), R("gtg")], writes=[R("facc", i)])
                    sc.op("dve", lambda h: h.tensor_tensor(xr[kx][:], xr[kx][:], facc[:, i, :], ALU.add),
                          reads=[R("xr", kx), R("facc", i)], writes=[R("xr", kx)])
                    sc.dma("sp", out[i * 128:(i + 1) * 128, :], xr[kx][:], reads=[R("xr", kx)], writes=[R("out", i)])

                def ffn_load(g):
                    k = g % 2
                    sc.dma("pool", wf1[k][:], w_ff1[:, g * 512:(g + 1) * 512].rearrange("(kc p) n -> p kc n", p=128),
                           reads=[], writes=[R("wf1", k)])
                    sc.dma("pool", wf2[k][:], w_ff2[g * 512:(g + 1) * 512, :].rearrange("(c p) n -> p c n", p=128),
                           reads=[], writes=[R("wf2", k)])

                def ffn_f1(g):
                    k = g % 2
                    for c in range(4):
                        for tt in range(NT_OWN):
                            j = cnt[0] % 2
                            cnt[0] += 1
                            pz, PZ = ps[j], PS[j]
                            tsl = slice(tt * 512, (tt + 1) * 512)
                            sc.pe([lambda h, dc=dc, pz=pz, c=c, tsl=tsl, k=k: h.matmul(
                                pz[:], wf1[k][:, dc, c * 128:(c + 1) * 128], h2T[:, dc, tsl],
                                start=(dc == 0), stop=(dc == 7)) for dc in range(8)],
                                reads=[R("wf1", k), R("h2T")], writes=[PZ])
                            sc.op("act", lambda h, j=j, pz=pz: h.activation(rl[j][:], pz[:], AF.Relu),
                                  reads=[PZ], writes=[R("rl", j)])
                            sc.op("dve", lambda h, j=j, k=k, c=c, tsl=tsl: h.tensor_tensor(
                                f1G[k][:, c, tsl], rl[j][:], rl[j][:], ALU.mult),
                                reads=[R("rl", j)], writes=[R("f1G", k)])

                def ffn_f2(g):
                    k = g % 2
                    for i in range(NOWN):
                        for half in range(2):
                            j = 2 + cnt[1] % 4
                            cnt[1] += 1
                            pz, PZ = ps[j], PS[j]
                            sc.pe([lambda h, c=c, pz=pz, i=i, half=half, k=k: h.matmul(
                                pz[:], f1G[k][:, c, i * 128:(i + 1) * 128], wf2[k][:, c, half * 512:(half + 1) * 512],
                                start=(c == 0), stop=(c == 3)) for c in range(4)],
                                reads=[R("f1G", k), R("wf2", k)], writes=[PZ])
                            dst = facc[:, i, half * 512:(half + 1) * 512]
                            if g == 0:
                                sc.op("dve", lambda h, dst=dst, pz=pz: h.tensor_copy(dst, pz[:]),
                                      reads=[PZ], writes=[R("facc", i)])
                            else:
                                sc.op("dve", lambda h, dst=dst, pz=pz: h.tensor_tensor(dst, dst, pz[:], ALU.add),
                                      reads=[PZ, R("facc", i)], writes=[R("facc", i)])
                        if g == NG - 1:
                            final_block(i)

                ffn_f1(0)
                for g in range(NG):
                    if g + 1 < NG:
                        ffn_load(g + 1)
                        ffn_f1(g + 1)
                    ffn_f2(g)

                sc.barrier()
            esW.close()
    return nc


def _consts(hf):
    cbm = np.zeros((128, 5 * 128 + 3 * 512), np.float32)
    j = np.arange(128)[:, None]
    k = np.arange(128)[None, :]
    cbm[:, 0:128] = np.eye(128)
    cbm[:, 128:256] = np.where(j >= k, -1.0, 0.0)
    cbm[:, 256:384] = -1.0
    rot = np.zeros((128, 128), np.float32)
    for b0 in (0, 64):
        for d in range(32):
            rot[b0 + d + 32, b0 + d] = -1.0
            rot[b0 + d, b0 + d + 32] = 1.0
    cbm[:, 384:512] = rot
    m_sb = np.where(j < k, 0.0, NEG)
    m_df = np.where((j // 64) <= (k // 64), 0.0, NEG)
    cbm[:, 640:1152] = np.tile(m_sb, (1, 4))
    cbm[:, 1152:1664] = np.tile(m_df, (1, 4))
    cbm[:, 1664:2176] = NEG if hf == 0 else 0.0
    cfm = np.zeros((128, 260), np.float32)
    cfm[:, 0:128] = 1.0
    inv_freq = (np.float32(10000.0) ** (-(np.arange(0, 64, 2, dtype=np.float32)) / np.float32(64))).astype(np.float32)
    cfm[:, 256] = inv_freq[(np.arange(128) % 64) % 32]
    cfm[:, 257] = -math.pi
    cfm[:, 258] = -0.5
    cfm[:, 259] = EPS
    cfm[:, 128] = math.log(1.0 - LAMBDA_INIT)
    return cbm.astype(ml_dtypes.bfloat16), cfm


_NC_CACHE = {}


def kernel(x, c, positions, w_ada, b_ada, g_pre_mix, w_in, lambda_q1, lambda_k1, lambda_q2, lambda_k2,
           g_subln, w_branch_sb, w_branch_df, w_gate, b_gate, w_out, g_post_mix, g_pre_ffn, w_ff1, w_ff2,
           g_post_ffn):
    f = lambda a: np.ascontiguousarray(np.asarray(a, dtype=np.float32))
    x = f(x)
    B, S, _ = x.shape
    NBLK = S // 128
    NOWN = NBLK // 2
    positions = np.asarray(positions).astype(np.int32)
    if S not in _NC_CACHE:
        _NC_CACHE[S] = build(S)
    nc = _NC_CACHE[S]

    def col(v, n):
        return np.ascontiguousarray(f(v).reshape(n, 128).T)

    def rep(v):
        v = f(v).reshape(1, -1)
        return np.ascontiguousarray(np.broadcast_to(v, (128, v.shape[1])))

    shared = {
        "w_ada": f(w_ada[0]), "bada_col": col(b_ada[0], 48), "bada_rep": rep(b_ada[0]),
        "gpre_col": np.ascontiguousarray(np.concatenate([col(g_pre_mix[0], 8), col(g_pre_ffn[0], 8)], axis=1)),
        "gpost_rep": np.ascontiguousarray(np.concatenate([rep(g_post_mix[0]), rep(g_post_ffn[0])], axis=1)),
        "w_in": f(w_in[0]),
        "lam_rep": np.ascontiguousarray(np.concatenate(
            [rep(lambda_q1[0]), rep(lambda_k1[0]), rep(lambda_q2[0]), rep(lambda_k2[0])], axis=1)),
        "gsub_rep": rep(g_subln[0]), "w_bsb": f(w_branch_sb[0]), "w_bdf": f(w_branch_df[0]),
        "w_gate": f(w_gate[0]), "bgate_col": col(b_gate[0], 16), "w_out": f(w_out[0]),
        "w_ff1": f(w_ff1[0]), "w_ff2": f(w_ff2[0]),
    }
    in_maps = []
    owns = []
    for core in range(8):
        b, hf = core // 2, core % 2
        own = [2 * i + hf for i in range(NOWN)]
        if hf == 0:
            oth = [NBLK - 1] + [2 * o - 1 for o in range(1, NOWN)]
        else:
            oth = [2 * o for o in range(NOWN)]
        owns.append(own)
        blocks = own + oth
        tok = np.concatenate([np.arange(g * 128, (g + 1) * 128) for g in blocks])
        cbm, cfm = _consts(hf)
        m = dict(shared)
        m["x_p"] = np.ascontiguousarray(x[b][tok])
        m["pos_rep"] = np.ascontiguousarray(np.broadcast_to(positions[b][tok][None, :], (128, S))).astype(np.int32)
        m["c_col"] = col(np.asarray(c)[b], 8)
        m["cb"] = cbm
        m["cf"] = cfm
        in_maps.append(m)
    res = run_bass_kernel_spmd(nc, in_maps, core_ids=list(range(8)))
    outp = np.empty((B, S, D), np.float32)
    for core in range(8):
        b = core // 2
        o = np.asarray(res.results[core]["out"], dtype=np.float32)
        for i, gblk in enumerate(owns[core]):
            outp[b, gblk * 128:(gblk + 1) * 128, :] = o[i * 128:(i + 1) * 128, :]
    return outp
```

```python
import math
from contextlib import ExitStack

import numpy as np
import ml_dtypes

import concourse.bass as bass
import concourse.mybir as mybir
from concourse.bass_utils import run_bass_kernel_spmd

F32 = mybir.dt.float32
BF16 = mybir.dt.bfloat16
I32 = mybir.dt.int32
AF = mybir.ActivationFunctionType
ALU = mybir.AluOpType

D = 1024
DC = 8
EPS = 1e-6
NEG = -30000.0
NDS = 24
LAMBDA_INIT = 0.8 - 0.6 * math.exp(0.0)


class Res:
    __slots__ = ("w", "r")

    def __init__(self):
        self.w = None
        self.r = {}


class Sched:
    def __init__(self, nc, es):
        self.nc = nc
        self.eng = {}
        for name, h in (("pe", nc.tensor), ("act", nc.scalar), ("dve", nc.vector),
                        ("pool", nc.gpsimd), ("sp", nc.sync)):
            sem = es.enter_context(nc.semaphore("cnt_" + name))
            self.eng[name] = dict(h=h, sem=sem, n=0, seen={})
        self.es = es
        self.nsw = 0
        self.dsems = [es.enter_context(nc.semaphore("dma%d" % i)) for i in range(NDS)]
        self.dcnt = [0] * NDS
        self.dnext = 0
        self.swd = []
        self.res = {}

    def R(self, *key):
        r = self.res.get(key)
        if r is None:
            r = self.res[key] = Res()
        return r

    def _wait(self, ename, dep):
        key, sem, val = dep
        e = self.eng[ename]
        if e["seen"].get(key, 0) >= val:
            return
        e["h"].wait_ge(sem, val)
        e["seen"][key] = val

    def _sync(self, ename, reads, writes):
        for t in reads:
            if t.w is not None and not (ename == "pe" and t.w[0] == "pe"):
                self._wait(ename, t.w)
        for t in writes:
            if t.w is not None and not (ename == "pe" and t.w[0] == "pe"):
                self._wait(ename, t.w)
            for k, dep in t.r.items():
                if k != ename:
                    self._wait(ename, dep)

    def _mark(self, me, reads, writes):
        for t in reads:
            t.r[me[0]] = me
        for t in writes:
            t.w = me
            t.r = {}

    def op(self, ename, fn, reads=(), writes=()):
        e = self.eng[ename]
        self._sync(ename, reads, writes)
        ins = fn(e["h"])
        e["n"] += 1
        ins.then_inc(e["sem"], 1)
        me = (ename, e["sem"], e["n"])
        self._mark(me, reads, writes)
        return me

    def pe(self, fns, reads=(), writes=()):
        e = self.eng["pe"]
        self._sync("pe", reads, writes)
        ins = None
        for f in fns:
            ins = f(e["h"])
        e["n"] += 1
        ins.then_inc(e["sem"], 1)
        me = ("pe", e["sem"], e["n"])
        self._mark(me, reads, writes)
        return me

    def dma(self, ename, out, in_, reads=(), writes=()):
        e = self.eng[ename]
        self._sync(ename, reads, writes)
        if ename == "pool":
            sem = self.es.enter_context(self.nc.semaphore("swd%d" % self.nsw))
            key = "swd%d" % self.nsw
            self.nsw += 1
            ins = e["h"].dma_start(out=out, in_=in_)
            ins.then_inc(sem, 16)
            me = (key, sem, 16)
            self._mark(me, reads, writes)
            self.swd.append(me)
            return me
        k = self.dnext
        self.dnext = (k + 1) % NDS
        if self.dcnt[k] > 0:
            self._wait(ename, ("dma%d" % k, self.dsems[k], self.dcnt[k]))
        ins = e["h"].dma_start(out=out, in_=in_)
        self.dcnt[k] += 16
        ins.then_inc(self.dsems[k], 16)
        me = ("dma%d" % k, self.dsems[k], self.dcnt[k])
        self._mark(me, reads, writes)
        return me

    def barrier(self):
        names = list(self.eng)
        for a in names:
            for b in names:
                if a != b and self.eng[b]["n"] > 0:
                    self._wait(a, (b, self.eng[b]["sem"], self.eng[b]["n"]))
            for k in range(NDS):
                if self.dcnt[k] > 0:
                    self._wait(a, ("dma%d" % k, self.dsems[k], self.dcnt[k]))
            for dep in self.swd:
                self._wait(a, dep)


def build(S):
    NBLK = S // 128
    NOWN = NBLK // 2
    SO = NOWN * 128
    NT_ALL = S // 512
    NT_OWN = SO // 512
    assert SO % 512 == 0

    nc = bass.Bass("TRN2", target_bir_lowering=False)

    def din(name, shape, dt=F32):
        return nc.dram_tensor(name, list(shape), dt, kind="ExternalInput").ap()

    x_p = din("x_p", [S, D])
    pos_rep = din("pos_rep", [128, S], I32)
    c_col = din("c_col", [128, 8])
    w_ada = din("w_ada", [D, 6 * D])
    bada_col = din("bada_col", [128, 48])
    bada_rep = din("bada_rep", [128, 6 * D])
    gpre_col = din("gpre_col", [128, 16])
    gpost_rep = din("gpost_rep", [128, 2 * D])
    w_in = din("w_in", [D, 3072])
    lam_rep = din("lam_rep", [128, 256])
    gsub_rep = din("gsub_rep", [128, 128])
    w_bsb = din("w_bsb", [512, D])
    w_bdf = din("w_bdf", [512, D])
    w_gate = din("w_gate", [D, 2 * D])
    bgate_col = din("bgate_col", [128, 16])
    w_out = din("w_out", [D, D])
    w_ff1 = din("w_ff1", [D, 4 * D])
    w_ff2 = din("w_ff2", [4 * D, D])
    cb = din("cb", [128, 5 * 128 + 3 * 512], BF16)
    cf = din("cf", [128, 260])
    out = nc.dram_tensor("out", [SO, D], F32, kind="ExternalOutput").ap()

    with ExitStack() as es:
        sc = Sched(nc, es)
        R = sc.R

        def sb(name, shape, dt=F32, stack=es):
            return stack.enter_context(nc.sbuf_tensor(name, list(shape), dt))

        ps = [es.enter_context(nc.psum_tensor("ps%d" % i, [128, 512], F32)) for i in range(8)]
        PS = [R("ps", i) for i in range(8)]

        cbt = sb("cbt", [128, 5 * 128 + 3 * 512], BF16)
        cft = sb("cft", [128, 260])
        ident_b = cbt[:, 0:128]
        negtri = cbt[:, 128:256]
        negones = cbt[:, 256:384]
        rot = cbt[:, 384:512]
        mk_sb = cbt[:, 640:1152]
        mk_df = cbt[:, 1152:1664]
        mk_dm = cbt[:, 1664:2176]
        ones_f = cft[:, 0:128]
        invf = cft[:, 256:257]
        negpi = cft[:, 257:258]
        neghalf = cft[:, 258:259]
        epsc = cft[:, 259:260]
        lnmul = cft[:, 128:129]
        modc = sb("modc", [128, 32])
        gtg = sb("gtg", [128, 2 * D])
        bgc = sb("bgc", [128, 16])
        gsub = sb("gsub", [128, 128])
        lamc = sb("lamc", [128, 4])
        bufX = sb("bufX", [128, DC * SO], BF16)
        bufY = sb("bufY", [128, DC * SO], BF16)
        hT_oth = bufX[:, :].rearrange("p (a s) -> p a s", a=DC)
        h2T = hT_oth
        mergedT = bufY[:, :].rearrange("p (a s) -> p a s", a=DC)
        small = sb("small", [128, 64])
        scol = sb("scol", [128, 8])
        bcol = sb("bcol", [128, 48])
        gpc = sb("gpc", [128, 16])
        esP1 = ExitStack()
        esP1.__enter__()
        hT_own = sb("hT_own", [128, DC, SO], BF16, esP1)
        ysbT = sb("ysbT", [128, 4, SO], BF16, esP1)
        ydfT = sb("ydfT", [128, 4, SO], BF16, esP1)
        RC = R("const")

        sc.dma("sp", cbt[:], cb, writes=[RC])
        sc.dma("sp", cft[:], cf, writes=[RC])
        sc.dma("sp", bgc[:], bgate_col, writes=[RC])
        sc.dma("sp", gsub[:], gsub_rep, writes=[RC])

        def hT(blk):
            if blk < NOWN:
                return hT_own, blk * 128
            return hT_oth, (blk - NOWN) * 128

        def hTtile(tt):
            if tt < NT_OWN:
                return hT_own, tt * 512
            return hT_oth, (tt - NT_OWN) * 512

        def rstd_op(rs_ap, ssq_ap, n, rres, sres, mul=1.0, on_act=True):
            if not on_act:
                sc.op("dve", lambda h: h.tensor_scalar(rs_ap, ssq_ap, 1.0 / n, EPS, ALU.mult, ALU.add),
                      reads=[sres], writes=[rres])
                sc.op("pool", lambda h: h.tensor_tensor(rs_ap, rs_ap, neghalf, ALU.pow),
                      reads=[rres, RC], writes=[rres])
                if mul != 1.0:
                    sc.op("pool", lambda h: h.tensor_scalar(rs_ap, rs_ap, mul, None, ALU.mult),
                          reads=[rres], writes=[rres])
                return
            sc.op("act", lambda h: h.activation(rs_ap, ssq_ap, AF.Ln, bias=epsc, scale=1.0 / n),
                  reads=[sres, RC], writes=[rres])
            if mul == 1.0:
                sc.op("act", lambda h: h.activation(rs_ap, rs_ap, AF.Exp, scale=-0.5),
                      reads=[rres], writes=[rres])
            else:
                sc.op("act", lambda h: h.activation(rs_ap, rs_ap, AF.Exp, bias=lnmul, scale=-0.5),
                      reads=[rres, RC], writes=[rres])

        with ExitStack() as es_att:

            with ExitStack() as es0:
                ccol = sb("ccol", [128, 8], F32, es0)
                wst = [sb("wst%d" % i, [128, 8, 512], F32, es0) for i in range(2)]
                accs = [sb("macc%d" % i, [128, 512], F32, es0) for i in range(2)]
                brow = [sb("brow%d" % i, [128, 512], F32, es0) for i in range(2)]
                gprow = [sb("gprow%d" % i, [128, 512], F32, es0) for i in range(2)]
                lamt = sb("lamt", [128, 256], F32, es0)
                lamj = sb("lamj", [128, 64], F32, es0)
                sc.dma("sp", ccol[:], c_col, writes=[R("ccol")])
                sc.dma("sp", bcol[:], bada_col, writes=[RC])
                sc.dma("sp", gpc[:], gpre_col, writes=[RC])
                sc.dma("sp", lamt[:], lam_rep, writes=[RC])
                sc.barrier()
                sc.op("act", lambda h: h.activation(scol[:], ccol[:], AF.Silu),
                      reads=[R("ccol")], writes=[R("scol")])
                for q in range(2):
                    sc.op("dve", lambda h, q=q: h.scalar_tensor_tensor(
                        lamj[:], lamt[:, q * 128:q * 128 + 64], 1.0, lamt[:, q * 128 + 64:q * 128 + 128],
                        ALU.mult, ALU.mult, accum_out=lamc[:, q:q + 1]),
                        reads=[RC], writes=[R("lamj"), R("lamc")])
                sc.op("act", lambda h: h.activation(lamc[:, 0:2], lamc[:, 0:2], AF.Exp),
                      reads=[R("lamc")], writes=[R("lamc")])
                sc.op("dve", lambda h: h.tensor_scalar(lamc[:, 3:4], lamc[:, 1:2], lamc[:, 0:1], -LAMBDA_INIT,
                                                      ALU.subtract, ALU.add),
                      reads=[R("lamc")], writes=[R("lamc")])

                for ct in range(4):
                    v, half = ct // 2, ct % 2
                    w = wst[ct % 2]
                    acc = accs[ct % 2]
                    RW, RA = R("wst", ct % 2), R("macc", ct % 2)
                    sc.dma("sp", w[:], w_ada[:, ct * 512:(ct + 1) * 512].rearrange("(kc p) n -> p kc n", p=128),
                           writes=[RW])
                    for kc in range(8):
                        if kc == 0:
                            sc.op("dve", lambda h, w=w, acc=acc: h.tensor_scalar(
                                acc[:], w[:, 0, :], scol[:, 0:1], None, ALU.mult),
                                reads=[RW, R("scol")], writes=[RA])
                        else:
                            sc.op("dve", lambda h, w=w, acc=acc, kc=kc: h.scalar_tensor_tensor(
                                acc[:], w[:, kc, :], scol[:, kc:kc + 1], acc[:], ALU.mult, ALU.add),
                                reads=[RW, R("scol"), RA], writes=[RA])
                    pz = ps[ct % 2]
                    PZ = PS[ct % 2]
                    if v in (2, 5):
                        gi = 0 if v == 2 else 1
                        sc.pe([lambda h, pz=pz, acc=acc: h.matmul(pz[:], ones_f, acc[:], start=True, stop=True)],
                              reads=[RA, RC], writes=[PZ])
                        br, gp = brow[ct % 2], gprow[ct % 2]
                        sc.dma("sp", br[:], bada_rep[:, ct * 512:(ct + 1) * 512], writes=[R("brow", ct % 2)])
                        sc.dma("sp", gp[:], gpost_rep[:, gi * D + half * 512: gi * D + (half + 1) * 512],
                               writes=[R("gprow", ct % 2)])
                        sc.op("dve", lambda h, br=br, pz=pz: h.tensor_tensor(br[:], pz[:], br[:], ALU.add),
                              reads=[PZ, R("brow", ct % 2)], writes=[R("brow", ct % 2)])
                        dst = gtg[:, gi * D + half * 512: gi * D + (half + 1) * 512]
                        sc.op("dve", lambda h, br=br, gp=gp, dst=dst: h.tensor_tensor(dst, br[:], gp[:], ALU.mult),
                              reads=[R("brow", ct % 2), R("gprow", ct % 2)], writes=[R("gtg")])
                    else:
                        fns = []
                        for j in range(4):
                            fns.append(lambda h, pz=pz, acc=acc, j=j: h.matmul(
                                pz[:, j:j + 1], acc[:, j * 128:(j + 1) * 128], ones_f[:, 0:1], start=True, stop=True))
                        sc.pe(fns, reads=[RA, RC], writes=[PZ])
                        blkbase = {1: 0, 0: 8, 4: 16, 3: 24}[v]
                        dst = modc[:, blkbase + half * 4: blkbase + half * 4 + 4]
                        bsl = bcol[:, ct * 4:(ct + 1) * 4]
                        if v in (0, 3):
                            sc.op("dve", lambda h, dst=dst, pz=pz, bsl=bsl: h.tensor_tensor(dst, pz[:, 0:4], bsl, ALU.add),
                                  reads=[PZ, RC], writes=[R("modc")])
                        else:
                            gsl = gpc[:, (0 if v == 1 else 8) + half * 4:(0 if v == 1 else 8) + half * 4 + 4]
                            sc.op("dve", lambda h, dst=dst, pz=pz, bsl=bsl: h.scalar_tensor_tensor(
                                dst, pz[:, 0:4], 1.0, bsl, ALU.add, ALU.add),
                                reads=[PZ, RC], writes=[R("modc")])
                            sc.op("dve", lambda h, dst=dst, gsl=gsl: h.tensor_tensor(dst, dst, gsl, ALU.mult),
                                  reads=[R("modc"), RC], writes=[R("modc")])
                sc.barrier()

            def nt_block(gidx, bi, xt_ap, XR, tag, scratch, banks):
                junk, xs, ssq, rs = scratch
                tps = [ps[b][:].bitcast(BF16) for b in banks]
                TP = [PS[b] for b in banks]
                k = (gidx * 4 + bi) % len(xs)
                kj = (gidx * 4 + bi) % len(junk)
                sc.op("act", lambda h: h.activation(junk[kj][:], xt_ap, AF.Square, accum_out=ssq[:, k:k + 1]),
                      reads=[XR], writes=[R(tag + "junk", kj), R(tag + "ssq", k)])
                rstd_op(rs[:, k:k + 1], ssq[:, k:k + 1], D, R(tag + "rs", k), R(tag + "ssq", k))
                sc.op("dve", lambda h: h.tensor_scalar(xs[k][:], xt_ap, rs[:, k:k + 1], None, ALU.mult),
                      reads=[XR, R(tag + "rs", k)], writes=[R(tag + "xs", k)])
                sc.pe([lambda h, j=j: h.transpose(
                    tps[j // 2][:, (j % 2) * 512 + bi * 128:(j % 2) * 512 + (bi + 1) * 128],
                    xs[k][:, j * 128:(j + 1) * 128], ident_b) for j in range(8)],
                    reads=[R(tag + "xs", k), RC], writes=TP)

            def nt_evac(dst_fn, mbase, banks):
                tps = [ps[b][:].bitcast(BF16) for b in banks]
                TP = [PS[b] for b in banks]
                for j in range(8):
                    dst, DR = dst_fn(j)
                    sc.op("dve", lambda h, j=j, dst=dst: h.tensor_scalar(
                        dst, tps[j // 2][:, (j % 2) * 512:(j % 2) * 512 + 512], modc[:, mbase + j:mbase + j + 1],
                        modc[:, mbase + 8 + j:mbase + 9 + j], ALU.mult, ALU.add),
                        reads=[TP[j // 2], R("modc")], writes=[DR])

            def norm_transpose_group(gidx, blocks, dst_fn, mbase, tag, scratch, banks):
                for bi, (xt_ap, XR) in enumerate(blocks):
                    nt_block(gidx, bi, xt_ap, XR, tag, scratch, banks)
                nt_evac(dst_fn, mbase, banks)

            with ExitStack() as esA:
                xb = [sb("xb%d" % i, [128, D], F32, esA) for i in range(9)]
                junk = [sb("junkA%d" % i, [128, D], BF16, esA) for i in range(2)]
                xs = [sb("xsA%d" % i, [128, D], BF16, esA) for i in range(8)]
                ssq = sb("ssqA", [128, 8], F32, esA)
                rs = sb("rsA", [128, 8], F32, esA)
                for g in range(NT_ALL):
                    blocks = []
                    for bi in range(4):
                        t = g * 4 + bi
                        xt = xb[t % 9]
                        XR = R("xb", t % 9)
                        sc.dma("sp", xt[:], x_p[t * 128:(t + 1) * 128, :], writes=[XR])
                        blocks.append((xt[:], XR))
                    tens, off = hTtile(g)

                    def dst_fn(j, tens=tens, off=off, g=g):
                        return tens[:, j, off:off + 512], R("hT", g)
                    norm_transpose_group(g, blocks, dst_fn, 0, "A", (junk, xs, ssq, rs),
                                         [0, 1, 2, 3] if g % 2 == 0 else [4, 5, 6, 7])
                sc.barrier()

            with ExitStack() as esB:
                KT = bufY[:, 0:2 * S].rearrange("p (a s) -> p a s", a=2)
                QM = bufY[:, 2 * S:2 * S + 4 * SO].rearrange("p (a s) -> p a s", a=4)
                Vb = sb("Vb", [128, NBLK * 260], BF16, esB)
                wsl = sb("wsl", [128, DC, 768], BF16, esB)
                Eb = [sb("Eb%d" % i, [128, 512], BF16, esB) for i in range(2)]
                spb = [sb("spb%d" % i, [128, 512], BF16, esB) for i in range(3)]
                wb = [sb("wb%d" % i, [128, 512], BF16, esB) for i in range(3)]
                Rb = [sb("Rb%d" % i, [128, 512], BF16, esB) for i in range(2)]
                ytok = [sb("ytok%d" % i, [128, 256], BF16, esB) for i in range(2)]
                dfar = sb("dfar", [128, 5120], F32, esB)
                t1b = [dfar[:, i * 512:(i + 1) * 512] for i in range(2)]
                t2b = [dfar[:, 1024 + i * 512:1024 + (i + 1) * 512] for i in range(2)]
                angb = dfar[:, 2048:2560]
                ang2 = dfar[:, 2560:3072]
                sinT = dfar[:, 3072:3584]
                cosT = dfar[:, 3584:4096]
                post = dfar[:, 4096:4608].bitcast(I32)
                raw = [dfar[:, 4608 + i * 256:4608 + (i + 1) * 256].bitcast(BF16) for i in range(2)]
                dfd = [sb("dfd%d" % i, [128, 128], F32, esB) for i in range(2)]
                dfj = sb("dfj", [128, 128], F32, esB)
                dfy = [sb("dfy%d" % i, [128, 128], BF16, esB) for i in range(2)]
                dsm = sb("dsm", [128, 16], F32, esB)
                ocp = [sb("ocp%d" % i, [128, 385], F32, esB) for i in range(2)]
                Vsb = Vb[:, :].rearrange("p (t c) -> p t c", c=260)
                wsd = [dfar[:, i * 1024:(i + 1) * 1024].rearrange("p (a n) -> p a n", a=8) for i in range(2)]
                accd = [dfar[:, 2048 + i * 128:2048 + (i + 1) * 128] for i in range(2)]
                brd = [dfar[:, 2304 + i * 128:2304 + (i + 1) * 128] for i in range(2)]
                gpd = [dfar[:, 2560 + i * 128:2560 + (i + 1) * 128] for i in range(2)]
                mod_piece = [0]
                mod_pend = []

                def emit_mod_piece():
                    while mod_pend:
                        mod_pend.pop(0)()
                    p = mod_piece[0]
                    if p >= 32:
                        return
                    mod_piece[0] += 1
                    mod_pend.append(lambda p=p: mod_part2(p))
                    ct, q = 4 + p // 4, p % 4
                    v, half = ct // 2, ct % 2
                    c0 = ct * 512 + q * 128
                    k = p % 2
                    w, acc = wsd[k], accd[k]
                    RW, RA = R("wsd", k), R("accd", k)
                    pz, PZ = ps[k], PS[k]
                    sc.dma("sp", w[:], w_ada[:, c0:c0 + 128].rearrange("(kc p) n -> p kc n", p=128), writes=[RW])
                    sc.op("dve", lambda h: h.tensor_scalar(acc[:], w[:, 0, :], scol[:, 0:1], None, ALU.mult),
                          reads=[RW, R("scol")], writes=[RA])
                    for kc in range(1, 8):
                        sc.op("dve", lambda h, kc=kc: h.scalar_tensor_tensor(
                            acc[:], w[:, kc, :], scol[:, kc:kc + 1], acc[:], ALU.mult, ALU.add),
                            reads=[RW, R("scol"), RA], writes=[RA])
                    if v in (2, 5):
                        gi = 0 if v == 2 else 1
                        goff = gi * D + half * 512 + q * 128
                        br, gp = brd[k], gpd[k]
                        sc.dma("sp", br[:], bada_rep[:, c0:c0 + 128], writes=[R("brd", k)])
                        sc.dma("sp", gp[:], gpost_rep[:, goff:goff + 128], writes=[R("gpd", k)])

                def mod_part2(p):
                    ct, q = 4 + p // 4, p % 4
                    v, half = ct // 2, ct % 2
                    c0 = ct * 512 + q * 128
                    k = p % 2
                    w, acc = wsd[k], accd[k]
                    RW, RA = R("wsd", k), R("accd", k)
                    pz, PZ = ps[k], PS[k]
                    if v in (2, 5):
                        gi = 0 if v == 2 else 1
                        goff = gi * D + half * 512 + q * 128
                        br, gp = brd[k], gpd[k]
                        sc.pe([lambda h: h.matmul(pz[:, 0:128], ones_f, acc[:], start=True, stop=True)],
                              reads=[RA, RC], writes=[PZ])
                        sc.op("dve", lambda h: h.tensor_tensor(br[:], pz[:, 0:128], br[:], ALU.add),
                              reads=[PZ, R("brd", k)], writes=[R("brd", k)])
                        sc.op("dve", lambda h: h.tensor_tensor(gtg[:, goff:goff + 128], br[:], gp[:], ALU.mult),
                              reads=[R("brd", k), R("gpd", k)], writes=[R("gtg")])
                    else:
                        sc.pe([lambda h: h.matmul(pz[:, 0:1], acc[:, 0:128], ones_f[:, 0:1], start=True, stop=True)],
                              reads=[RA, RC], writes=[PZ])
                        blkbase = {4: 16, 3: 24}[v]
                        dst = modc[:, blkbase + half * 4 + q: blkbase + half * 4 + q + 1]
                        bsl = bcol[:, ct * 4 + q:ct * 4 + q + 1]
                        if v == 3:
                            sc.op("dve", lambda h: h.tensor_tensor(dst, pz[:, 0:1], bsl, ALU.add),
                                  reads=[PZ, RC], writes=[R("modc")])
                        else:
                            gsl = gpc[:, 8 + half * 4 + q:8 + half * 4 + q + 1]
                            sc.op("dve", lambda h: h.scalar_tensor_tensor(dst, pz[:, 0:1], 1.0, bsl, ALU.add, ALU.add),
                                  reads=[PZ, RC], writes=[R("modc")])
                            sc.op("dve", lambda h: h.tensor_tensor(dst, dst, gsl, ALU.mult),
                                  reads=[R("modc"), RC], writes=[R("modc")])
                sc.op("pool", lambda h: h.memset(QM, 0.0), writes=[R("QM", t) for t in range(NT_OWN)])

                def rope_tables(tt):
                    TWO_PI = 2 * math.pi
                    C1 = 6.28125
                    C2 = TWO_PI - C1
                    sc.dma("sp", post[:], pos_rep[:, tt * 512:(tt + 1) * 512], writes=[R("post")])
                    sc.op("dve", lambda h: h.tensor_copy(angb[:], post[:]), reads=[R("post")], writes=[R("angb")])
                    sc.op("dve", lambda h: h.tensor_scalar(angb[:], angb[:], invf, None, ALU.mult),
                          reads=[R("angb"), RC], writes=[R("angb")])
                    sc.op("dve", lambda h: h.tensor_scalar(ang2[:], angb[:], 1.0 / TWO_PI, None, ALU.mult),
                          reads=[R("angb")], writes=[R("ang2")])
                    sc.op("dve", lambda h: h.tensor_copy(post[:], ang2[:]), reads=[R("ang2")], writes=[R("post")])
                    sc.op("dve", lambda h: h.tensor_copy(ang2[:], post[:]), reads=[R("post")], writes=[R("ang2")])
                    sc.op("dve", lambda h: h.scalar_tensor_tensor(angb[:], ang2[:], -C1, angb[:], ALU.mult, ALU.add),
                          reads=[R("ang2"), R("angb")], writes=[R("angb")])
                    sc.op("dve", lambda h: h.scalar_tensor_tensor(angb[:], ang2[:], -C2, angb[:], ALU.mult, ALU.add),
                          reads=[R("ang2"), R("angb")], writes=[R("angb")])

                    def fold(dst_tab, DR):
                        sc.op("dve", lambda h: h.tensor_scalar(ang2[:], angb[:], math.pi, None, ALU.is_gt),
                              reads=[R("angb")], writes=[R("ang2")])
                        sc.op("dve", lambda h: h.scalar_tensor_tensor(angb[:], ang2[:], -TWO_PI, angb[:], ALU.mult, ALU.add),
                              reads=[R("ang2"), R("angb")], writes=[R("angb")])
                        sc.op("dve", lambda h: h.tensor_scalar(ang2[:], angb[:], -math.pi, None, ALU.is_lt),
                              reads=[R("angb")], writes=[R("ang2")])
                        sc.op("dve", lambda h: h.scalar_tensor_tensor(angb[:], ang2[:], TWO_PI, angb[:], ALU.mult, ALU.add),
                              reads=[R("ang2"), R("angb")], writes=[R("angb")])
                        sc.op("dve", lambda h: h.tensor_scalar(ang2[:], angb[:], -math.pi, math.pi, ALU.max, ALU.min),
                              reads=[R("angb")], writes=[R("ang2")])
                        sc.op("act", lambda h: h.activation(dst_tab[:], ang2[:], AF.Sin),
                              reads=[R("ang2")], writes=[DR])
                    fold(sinT, R("sinT"))
                    sc.op("dve", lambda h: h.tensor_scalar(angb[:], angb[:], 0.5 * math.pi, None, ALU.add),
                          reads=[R("angb")], writes=[R("angb")])
                    fold(cosT, R("cosT"))

                pcount = [0]

                def project(col0, tt, is_df, scale, dst_fn):
                    k = pcount[0] % 2
                    pcount[0] += 1
                    pz, PZ = ps[k], PS[k]
                    rz, RZ = ps[7], PS[7]
                    tens, off = hTtile(tt)
                    sc.pe([lambda h, dc=dc: h.matmul(pz[:], wsl[:, dc, col0:col0 + 128], tens[:, dc, off:off + 512],
                                                     start=(dc == 0), stop=(dc == 7)) for dc in range(8)],
                          reads=[R("wsl"), R("hTall")], writes=[PZ])
                    if not is_df:
                        for (dst, DR, lo, hi) in dst_fn():
                            sc.op("dve", lambda h, dst=dst, lo=lo, hi=hi: h.tensor_scalar(
                                dst, pz[lo:hi, :], scale, None, ALU.mult), reads=[PZ], writes=[DR])
                        return
                    rw, RW = raw[k], R("raw", k)
                    sc.op("dve", lambda h: h.tensor_copy(rw[:], pz[:]), reads=[PZ], writes=[RW])
                    sc.pe([lambda h: h.matmul(rz[:], rot, rw[:], start=True, stop=True)],
                          reads=[RW, RC], writes=[RZ])
                    t1, t2 = t1b[k], t2b[k]
                    sc.op("dve", lambda h: h.scalar_tensor_tensor(t1[:], pz[:], scale, cosT[:], ALU.mult, ALU.mult),
                          reads=[PZ, R("cosT")], writes=[R("t1", k)])
                    sc.op("dve", lambda h: h.scalar_tensor_tensor(t2[:], rz[:], scale, sinT[:], ALU.mult, ALU.mult),
                          reads=[RZ, R("sinT")], writes=[R("t2", k)])
                    for (dst, DR, lo, hi) in dst_fn():
                        sc.op("pool", lambda h, dst=dst, lo=lo, hi=hi: h.tensor_tensor(
                            dst, t1[lo:hi, :], t2[lo:hi, :], ALU.add),
                            reads=[R("t1", k), R("t2", k)], writes=[DR])

                wsl_loaded = [False]

                def load_wsl(is_df, g):
                    if is_df:
                        qc, kc_, vc = 1536 + g * 256, 2048 + g * 256, 2560 + g * 256
                    else:
                        qc, kc_, vc = g * 256, 512 + g * 256, 1024 + g * 256
                    for n, c0 in enumerate((qc, kc_, vc)):
                        sc.dma("pool", wsl[:, :, n * 256:(n + 1) * 256],
                               w_in[:, c0:c0 + 256].rearrange("(kc p) n -> p kc n", p=128), writes=[R("wsl")])

                def run_pass(is_df, g, next_pass):
                    if not wsl_loaded[0]:
                        load_wsl(is_df, g)
                    wsl_loaded[0] = False
                    if is_df:
                        allV = [R("V", t) for t in range(NT_ALL)]
                        sc.op("pool", lambda h: h.memset(Vsb[:, :, 128:129], 1.0), writes=allV)
                        sc.op("pool", lambda h: h.memset(Vsb[:, :, 258:259], 1.0), writes=allV)

                    def vproj(t2_):
                        k = pcount[0] % 2
                        pcount[0] += 1
                        pz, PZ = ps[k], PS[k]
                        fns = []
                        for u in range(2):
                            tens, off = hT(2 * t2_ + u)
                            for dc in range(8):
                                fns.append(lambda h, u=u, dc=dc, tens=tens, off=off: h.matmul(
                                    pz[:, u * 256:(u + 1) * 256], tens[:, dc, off:off + 128], wsl[:, dc, 512:768],
                                    start=(dc == 0), stop=(dc == 7)))
                        sc.pe(fns, reads=[R("wsl"), R("hTall")], writes=[PZ])
                        pzv = pz[:, :].rearrange("p (u c) -> p u c", c=256)
                        VR = R("V", (2 * t2_) // 4)
                        if not is_df:
                            sc.op("dve", lambda h: h.tensor_copy(Vsb[:, 2 * t2_:2 * t2_ + 2, 0:256], pzv),
                                  reads=[PZ], writes=[VR])
                        else:
                            for hd in range(2):
                                sc.op("dve", lambda h, hd=hd: h.tensor_copy(
                                    Vsb[:, 2 * t2_:2 * t2_ + 2, hd * 130:hd * 130 + 128],
                                    pzv[:, :, hd * 128:(hd + 1) * 128]), reads=[PZ], writes=[VR])

                    chunks = []
                    for tp_ in range(NT_OWN):
                        lst = []
                        for tt in (tp_, NT_OWN + tp_):
                            if is_df:
                                lst.append(lambda tt=tt: rope_tables(tt))
                            for p in range(2):
                                def kd(p=p, tt=tt):
                                    return [(KT[:, p, tt * 512:(tt + 1) * 512], R("KT", tt), 0, 128)]
                                lst.append(lambda p=p, tt=tt, kd=kd: project(256 + p * 128, tt, is_df, 1.0, kd))
                            if tt < NT_OWN:
                                for p in range(2):
                                    def qd(p=p, tt=tt):
                                        return [(QM[0:64, 2 * p, tt * 512:(tt + 1) * 512], R("QM", tt), 0, 64),
                                                (QM[64:128, 2 * p + 1, tt * 512:(tt + 1) * 512], R("QM", tt), 64, 128)]
                                    lst.append(lambda p=p, tt=tt, qd=qd: project(p * 128, tt, is_df, 0.125, qd))
                            for t2_ in (2 * tt, 2 * tt + 1):
                                lst.append(lambda t2_=t2_: vproj(t2_))
                        chunks.append(lst)
                    chunk_state = [0]

                    def emit_chunk_items(nitems):
                        while nitems > 0 and chunk_state[0] < len(chunks):
                            lst = chunks[chunk_state[0]]
                            if not lst:
                                chunk_state[0] += 1
                                if chunk_state[0] == len(chunks) and next_pass is not None:
                                    load_wsl(*next_pass)
                                    wsl_loaded[0] = True
                                continue
                            lst.pop(0)()
                            nitems -= 1
                        while chunk_state[0] < len(chunks) and not chunks[chunk_state[0]]:
                            chunk_state[0] += 1
                            if chunk_state[0] == len(chunks) and next_pass is not None:
                                load_wsl(*next_pass)
                                wsl_loaded[0] = True

                    def ensure_chunks(gq):
                        while chunk_state[0] <= gq and chunk_state[0] < len(chunks):
                            emit_chunk_items(1)

                    units = []
                    for i in range(NOWN):
                        n = 2 * i + 2
                        for s in range(n):
                            if s % 2 == 0:
                                kb = i - s // 2
                            else:
                                kb = NOWN + i - s // 2
                            units.append((i, s, n, kb))
                    NU = len(units)
                    zb = [3, 4, 5]
                    yb = [6, 2]

                    if is_df:
                        ensure_chunks(len(chunks) - 1)

                    def drip(u):
                        gq = units[u][0] // 4
                        if chunk_state[0] == gq + 1 and chunk_state[0] < len(chunks):
                            emit_chunk_items(1)

                    def stA(u):
                        i, s, n, kb = units[u]
                        z, Z = ps[zb[u % 3]], PS[zb[u % 3]]
                        fns = []
                        plain = is_df and not (s == 0 or s == n - 1)
                        for hh in range(4):
                            fns.append(lambda h, hh=hh: h.matmul(
                                z[:, hh * 128:(hh + 1) * 128], KT[:, hh // 2, kb * 128:(kb + 1) * 128],
                                QM[:, hh, i * 128:(i + 1) * 128], start=(hh == 0), stop=plain, skip_group_check=True))
                        if s == 0:
                            mk = mk_df if is_df else mk_sb
                            fns.append(lambda h: h.matmul(z[:], ident_b, mk, start=False, stop=is_df,
                                                          skip_group_check=True))
                        elif s == n - 1:
                            fns.append(lambda h: h.matmul(z[:], ident_b, mk_dm, start=False, stop=is_df,
                                                          skip_group_check=True))
                        ensure_chunks(i // 4)
                        sc.pe(fns, reads=[R("KT", kb // 4), R("QM", i // 4), RC], writes=[Z])

                    def stB_sb(u):
                        z, Z = ps[zb[u % 3]], PS[zb[u % 3]]
                        E, sp = Eb[u % 2], spb[u % 3]
                        sc.op("act", lambda h: h.activation(E[:], z[:], AF.Exp), reads=[Z], writes=[R("E", u % 2)])
                        sc.op("act", lambda h: h.activation(sp[:], E[:], AF.Ln, bias=1.0),
                              reads=[R("E", u % 2)], writes=[R("sp", u % 3)])

                    def stE_sb(u):
                        i, s, n, kb = units[u]
                        if s == n - 1:
                            return
                        sp = spb[u % 3]
                        if s == 0:
                            sc.op("pool", lambda h: h.tensor_copy(Rb[1][:], sp[:]),
                                  reads=[R("sp", u % 3)], writes=[R("Rb", 1)])
                        else:
                            a, b = Rb[s % 2], Rb[(s + 1) % 2]
                            sc.op("pool", lambda h: h.tensor_tensor(b[:], a[:], sp[:], ALU.add),
                                  reads=[R("sp", u % 3), R("Rb", s % 2)], writes=[R("Rb", (s + 1) % 2)])

                    def stC_sb(u):
                        i, s, n, kb = units[u]
                        z, Z = ps[zb[u % 3]], PS[zb[u % 3]]
                        sp = spb[u % 3]
                        fns = []
                        rd = [R("sp", u % 3), RC, Z]
                        for hh in range(4):
                            fns.append(lambda h, hh=hh: h.matmul(
                                z[:, hh * 128:(hh + 1) * 128], negtri, sp[:, hh * 128:(hh + 1) * 128],
                                start=False, stop=(s == 0), skip_group_check=True))
                        if s > 0:
                            rb = Rb[s % 2]
                            rd.append(R("Rb", s % 2))
                            for hh in range(4):
                                fns.append(lambda h, hh=hh: h.matmul(
                                    z[:, hh * 128:(hh + 1) * 128], negones, rb[:, hh * 128:(hh + 1) * 128],
                                    start=False, stop=True, skip_group_check=True))
                        sc.pe(fns, reads=rd, writes=[Z])

                    def stD(u):
                        z, Z = ps[zb[u % 3]], PS[zb[u % 3]]
                        w_ = wb[u % 3]
                        sc.op("act", lambda h: h.activation(w_[:], z[:], AF.Exp), reads=[Z], writes=[R("w", u % 3)])

                    def stF(u):
                        i, s, n, kb = units[u]
                        w_ = wb[u % 3]
                        fns = []
                        if not is_df:
                            y, Y = ps[yb[i % 2]], PS[yb[i % 2]]
                            for hh in range(4):
                                fns.append(lambda h, hh=hh: h.matmul(
                                    y[:, hh * 64:(hh + 1) * 64], w_[:, hh * 128:(hh + 1) * 128],
                                    Vsb[:, kb, hh * 64:(hh + 1) * 64], start=(s == 0 and hh == 0), stop=(s == n - 1),
                                    skip_group_check=True))
                            sc.pe(fns, reads=[R("w", u % 3), R("V", kb // 4)], writes=[Y])
                        else:
                            for c in range(4):
                                y = ps[yb[c // 2]]
                                fns.append(lambda h, c=c, y=y: h.matmul(
                                    y[:, (c % 2) * 256:(c % 2) * 256 + 129], w_[:, c * 128:(c + 1) * 128],
                                    Vsb[:, kb, (c // 2) * 130:(c // 2) * 130 + 129], start=(s == 0 and c % 2 == 0), stop=(s == n - 1),
                                    skip_group_check=True))
                            sc.pe(fns, reads=[R("w", u % 3), R("V", kb // 4)], writes=[PS[yb[0]], PS[yb[1]]])
                        if s == n - 1:
                            delay = min(8, 2 * i + 3)
                            if is_df:
                                fin_df(i)
                                pending.append((u + delay, lambda i=i: fin_df2(i)))
                            else:
                                fin_sb(i)
                                pending.append((u + delay, lambda i=i: fin_sb2(i)))

                    pending = []

                    def flush(u):
                        while pending and pending[0][0] <= u:
                            pending.pop(0)[1]()

                    def fin_sb(i):
                        y, Y = ps[yb[i % 2]], PS[yb[i % 2]]
                        yt = ytok[i % 2]
                        sc.op("dve", lambda h: h.tensor_copy(yt[:], y[:, 0:256]), reads=[Y], writes=[R("ytok", i % 2)])

                    def fin_sb2(i):
                        yt = ytok[i % 2]
                        tp = ps[7][:].bitcast(BF16)
                        sc.pe([lambda h, p=p: h.transpose(tp[:, p * 128:(p + 1) * 128], yt[:, p * 128:(p + 1) * 128], ident_b)
                               for p in range(2)], reads=[R("ytok", i % 2), RC], writes=[PS[7]])
                        tpv = tp[:, 0:256].rearrange("p (a c) -> p a c", c=128)
                        sc.op("dve", lambda h: h.tensor_copy(ysbT[:, 2 * g:2 * g + 2, i * 128:(i + 1) * 128], tpv),
                              reads=[PS[7]], writes=[R("ysbT")])

                    def fin_df(i):
                        tp = ps[7][:].bitcast(BF16)
                        for hd in range(2):
                            sc.op("dve", lambda h, hd=hd: h.tensor_copy(ocp[hd][:], ps[yb[hd]][:, 0:385]),
                                  reads=[PS[yb[hd]]], writes=[R("ocp", hd)])
                        for hd in range(2):
                            y, Y = ocp[hd], R("ocp", hd)
                            dd, DDR = dfd[hd], R("dfd", hd)
                            b0 = hd * 8
                            sc.op("dve", lambda h, y=y, b0=b0: h.reciprocal(dsm[:, b0:b0 + 1], y[:, 128:129]),
                                  reads=[Y], writes=[R("dsm", hd)])
                            sc.op("dve", lambda h, y=y, b0=b0: h.reciprocal(dsm[:, b0 + 1:b0 + 2], y[:, 256 + 128:256 + 129]),
                                  reads=[Y], writes=[R("dsm", hd)])
                            sc.op("dve", lambda h, b0=b0: h.tensor_tensor(dsm[:, b0 + 1:b0 + 2], dsm[:, b0 + 1:b0 + 2],
                                                                          lamc[:, 3:4], ALU.mult),
                                  reads=[R("dsm", hd), R("lamc")], writes=[R("dsm", hd)])
                            sc.op("dve", lambda h, y=y, dd=dd, b0=b0: h.tensor_scalar(
                                dd[:], y[:, 0:128], dsm[:, b0:b0 + 1], None, ALU.mult),
                                reads=[Y, R("dsm", hd)], writes=[DDR])
                            sc.op("dve", lambda h, y=y, dd=dd, b0=b0: h.scalar_tensor_tensor(
                                dd[:], y[:, 256:384], dsm[:, b0 + 1:b0 + 2], dd[:], ALU.mult, ALU.add),
                                reads=[Y, R("dsm", hd), DDR], writes=[DDR])
                            sc.op("dve", lambda h, dd=dd, b0=b0: h.scalar_tensor_tensor(
                                dfj[:], dd[:], 1.0, dd[:], ALU.mult, ALU.mult, accum_out=dsm[:, b0 + 2:b0 + 3]),
                                reads=[DDR], writes=[R("dfj"), R("dsm", hd)])
                            rstd_op(dsm[:, b0 + 3:b0 + 4], dsm[:, b0 + 2:b0 + 3], 128, R("dsm", hd), R("dsm", hd),
                                    mul=1.0 - LAMBDA_INIT, on_act=False)
                            yy = dfy[hd]
                            sc.op("dve", lambda h, dd=dd, yy=yy, b0=b0: h.scalar_tensor_tensor(
                                yy[:], dd[:], dsm[:, b0 + 3:b0 + 4], gsub[:], ALU.mult, ALU.mult),
                                reads=[DDR, R("dsm", hd), RC], writes=[R("dfy", hd)])

                    def fin_df2(i):
                        tp = ps[7][:].bitcast(BF16)
                        sc.pe([lambda h, hd=hd: h.transpose(tp[:, hd * 128:(hd + 1) * 128], dfy[hd][:], ident_b)
                               for hd in range(2)], reads=[R("dfy", 0), R("dfy", 1), RC], writes=[PS[7]])
                        tpv = tp[:, 0:256].rearrange("p (a c) -> p a c", c=128)
                        sc.op("dve", lambda h: h.tensor_copy(ydfT[:, 2 * g:2 * g + 2, i * 128:(i + 1) * 128], tpv),
                              reads=[PS[7]], writes=[R("ydfT")])

                    if not is_df:
                        stA(0)
                        stA(1)
                        stB_sb(0)
                        for u in range(NU):
                            if g == 0 and u % 8 == 4:
                                emit_mod_piece()
                            drip(u)
                            stC_sb(u)
                            stE_sb(u)
                            if u >= 1:
                                stD(u - 1)
                                stF(u - 1)
                                flush(u - 1)
                            if u + 2 < NU:
                                stA(u + 2)
                            if u + 1 < NU:
                                stB_sb(u + 1)
                        stD(NU - 1)
                        stF(NU - 1)
                        flush(10 ** 9)
                        if g == 0:
                            while mod_piece[0] < 32 or mod_pend:
                                emit_mod_piece()
                    else:
                        stA(0)
                        stA(1)
                        for u in range(NU):
                            drip(u)
                            stD(u)
                            if u + 2 < NU:
                                stA(u + 2)
                            stF(u)
                            flush(u)
                        flush(10 ** 9)

                hall = R("hTall")
                for t in range(NBLK):
                    hall.w = R("hT", t).w if R("hT", t).w is not None else hall.w
                sc.barrier()
                run_pass(False, 0, (False, 1))
                run_pass(False, 1, (True, 0))
                sc.barrier()
                run_pass(True, 0, (True, 1))
                run_pass(True, 1, None)
                sc.barrier()
        sc.barrier()

        with ExitStack() as esCD:
            with ExitStack() as esC:
                with ExitStack() as esC1:
                    wg = sb("wg", [128, DC, 2 * D], BF16, esC1)
                    wbs = sb("wbs", [128, 4, D], BF16, esC1)
                    wbd = sb("wbd", [128, 4, D], BF16, esC1)
                    gsb_ = [sb("gsb%d" % i, [128, 512], F32, esC1) for i in range(2)]
                    gdf_ = [sb("gdf%d" % i, [128, 512], F32, esC1) for i in range(2)]
                    t1c = [sb("t1c%d" % i, [128, 512], F32, esC1) for i in range(2)]
                    t2c = [sb("t2c%d" % i, [128, 512], F32, esC1) for i in range(2)]
                    def load_wg(q):
                        for hh in range(2):
                            c0 = hh * D + q * 256
                            sc.dma("pool", wg[:, :, c0:c0 + 256],
                                   w_gate[:, c0:c0 + 256].rearrange("(kc p) n -> p kc n", p=128),
                                   writes=[R("wg", hh, q)])
                    load_wg(0)
                    sc.dma("pool", wbs[:], w_bsb.rearrange("(c p) n -> p c n", p=128), writes=[R("wbs")])
                    sc.dma("pool", wbd[:], w_bdf.rearrange("(c p) n -> p c n", p=128), writes=[R("wbd")])
                    for q in range(1, 4):
                        load_wg(q)
                    it = 0
                    for tt in range(NT_OWN):
                        tsl = slice(tt * 512, (tt + 1) * 512)
                        for dc in range(8):
                            k = it % 2
                            it += 1
                            pg, pd, pbs, pbd = ps[4 * k], ps[4 * k + 1], ps[4 * k + 2], ps[4 * k + 3]
                            PG, PD, PBS, PBD = PS[4 * k], PS[4 * k + 1], PS[4 * k + 2], PS[4 * k + 3]
                            sc.pe([lambda h, kk=kk, pg=pg, dc=dc, tsl=tsl: h.matmul(
                                pg[:], wg[:, kk, dc * 128:(dc + 1) * 128], hT_own[:, kk, tsl],
                                start=(kk == 0), stop=(kk == 7)) for kk in range(8)],
                                reads=[R("wg", 0, dc // 2)], writes=[PG])
                            sc.pe([lambda h, kk=kk, pd=pd, dc=dc, tsl=tsl: h.matmul(
                                pd[:], wg[:, kk, D + dc * 128:D + (dc + 1) * 128], hT_own[:, kk, tsl],
                                start=(kk == 0), stop=(kk == 7)) for kk in range(8)],
                                reads=[R("wg", 1, dc // 2)], writes=[PD])
                            sc.pe([lambda h, c=c, pbs=pbs, dc=dc, tsl=tsl: h.matmul(
                                pbs[:], wbs[:, c, dc * 128:(dc + 1) * 128], ysbT[:, c, tsl],
                                start=(c == 0), stop=(c == 3)) for c in range(4)],
                                reads=[R("wbs"), R("ysbT")], writes=[PBS])
                            sc.pe([lambda h, c=c, pbd=pbd, dc=dc, tsl=tsl: h.matmul(
                                pbd[:], wbd[:, c, dc * 128:(dc + 1) * 128], ydfT[:, c, tsl],
                                start=(c == 0), stop=(c == 3)) for c in range(4)],
                                reads=[R("wbd"), R("ydfT")], writes=[PBD])
                            sc.op("act", lambda h, k=k, pg=pg, dc=dc: h.activation(
                                gsb_[k][:], pg[:], AF.Sigmoid, bias=bgc[:, dc:dc + 1]),
                                reads=[PG, RC], writes=[R("gsb", k)])
                            sc.op("act", lambda h, k=k, pd=pd, dc=dc: h.activation(
                                gdf_[k][:], pd[:], AF.Sigmoid, bias=bgc[:, 8 + dc:9 + dc]),
                                reads=[PD, RC], writes=[R("gdf", k)])
                            sc.op("dve", lambda h, k=k, pbs=pbs: h.tensor_tensor(t1c[k][:], pbs[:], gsb_[k][:], ALU.mult),
                                  reads=[PBS, R("gsb", k)], writes=[R("t1c", k)])
                            sc.op("dve", lambda h, k=k, pbd=pbd: h.tensor_tensor(t2c[k][:], pbd[:], gdf_[k][:], ALU.mult),
                                  reads=[PBD, R("gdf", k)], writes=[R("t2c", k)])
                            sc.op("pool", lambda h, k=k, dc=dc, tsl=tsl: h.tensor_tensor(
                                mergedT[:, dc, tsl], t1c[k][:], t2c[k][:], ALU.add),
                                reads=[R("t1c", k), R("t2c", k)], writes=[R("mergedT")])
                    sc.barrier()
                esP1.close()

                esW = ExitStack()
                esW.__enter__()
                wf1 = [sb("wf1_%d" % i, [128, DC, 512], BF16, esW) for i in range(2)]
                wf2 = [sb("wf2_%d" % i, [128, 4, D], BF16, esW) for i in range(2)]
                with ExitStack() as esC2:
                    wo = sb("wo", [128, DC, D], BF16, esC2)
                    xb2 = [sb("xb2_%d" % i, [128, D], F32, esC2) for i in range(6)]
                    x1b = [sb("x1b%d" % i, [128, D], F32, esC2) for i in range(4)]
                    junk = [sb("junkC%d" % i, [128, D], BF16, esC2) for i in range(2)]
                    xs = [sb("xsC%d" % i, [128, D], BF16, esC2) for i in range(8)]
                    ssq = sb("ssqC", [128, 8], F32, esC2)
                    rs = sb("rsC", [128, 8], F32, esC2)
                    ssm = sb("ssm", [128, 8], F32, esC2)
                    sc.dma("pool", wo[:], w_out.rearrange("(kc p) n -> p kc n", p=128), writes=[R("wo")])
                    sc.dma("pool", wf1[0][:], w_ff1[:, 0:512].rearrange("(kc p) n -> p kc n", p=128),
                           writes=[R("wf1", 0)])
                    sc.dma("pool", wf2[0][:], w_ff2[0:512, :].rearrange("(c p) n -> p c n", p=128),
                           writes=[R("wf2", 0)])
                    scr = (junk, xs, ssq, rs)

                    def c2_s1(i):
                        k, k3 = i % 2, i % 6
                        mm = ((ps[2 * k], PS[2 * k]), (ps[2 * k + 1], PS[2 * k + 1]))
                        xt, XR = xb2[k3], R("xb2", k3)
                        sc.dma("sp", xt[:], x_p[i * 128:(i + 1) * 128, :], writes=[XR])
                        for half, (m, M) in enumerate(mm):
                            sc.pe([lambda h, dc=dc, m=m, half=half: h.matmul(
                                m[:], mergedT[:, dc, i * 128:(i + 1) * 128], wo[:, dc, half * 512:(half + 1) * 512],
                                start=(dc == 0), stop=(dc == 7)) for dc in range(8)],
                                reads=[R("mergedT"), R("wo")], writes=[M])
                            sc.op("act", lambda h, m=m, half=half: h.activation(
                                junk[k][:, half * 512:(half + 1) * 512], m[:], AF.Square,
                                accum_out=ssm[:, 4 * k + half:4 * k + half + 1]),
                                reads=[M], writes=[R("Cjunk", k), R("ssm", k)])

                    def c2_s2(i):
                        k, k3 = i % 2, i % 6
                        mm = ((ps[2 * k], PS[2 * k]), (ps[2 * k + 1], PS[2 * k + 1]))
                        xt, XR = xb2[k3], R("xb2", k3)
                        sc.op("dve", lambda h: h.tensor_tensor(
                            ssm[:, 4 * k + 2:4 * k + 3], ssm[:, 4 * k:4 * k + 1], ssm[:, 4 * k + 1:4 * k + 2], ALU.add),
                            reads=[R("ssm", k)], writes=[R("ssm", k)])
                        rstd_op(ssm[:, 4 * k + 3:4 * k + 4], ssm[:, 4 * k + 2:4 * k + 3], D, R("ssm", k), R("ssm", k))
                        x1, X1 = x1b[i % 4], R("x1b", i % 4)
                        for half, (m, M) in enumerate(mm):
                            sc.op("dve", lambda h, m=m, half=half: h.scalar_tensor_tensor(
                                x1[:, half * 512:(half + 1) * 512], m[:], ssm[:, 4 * k + 3:4 * k + 4],
                                gtg[:, half * 512:(half + 1) * 512], ALU.mult, ALU.mult),
                                reads=[M, R("ssm", k), R("gtg")], writes=[X1])
                        sc.op("dve", lambda h: h.tensor_tensor(x1[:], x1[:], xt[:], ALU.add),
                              reads=[X1, XR], writes=[X1])
                        sc.dma("sp", out[i * 128:(i + 1) * 128, :], x1[:], reads=[X1], writes=[R("out", i)])

                    def c2_s3(i):
                        g, bi = i // 4, i % 4
                        x1, X1 = x1b[i % 4], R("x1b", i % 4)
                        nt_block(g, bi, x1[:], X1, "C", scr, [4, 5, 6, 7])
                        if bi == 3:
                            def dst_fn(j, g=g):
                                return h2T[:, j, g * 512:(g + 1) * 512], R("h2T")
                            nt_evac(dst_fn, 16, [4, 5, 6, 7])

                    for n in range(NOWN + 3):
                        if n < NOWN:
                            c2_s1(n)
                        if 0 <= n - 1 < NOWN:
                            c2_s2(n - 1)
                        if 0 <= n - 3 < NOWN:
                            c2_s3(n - 3)
                    sc.barrier()
            sc.barrier()

            with ExitStack() as esD:
                facc = sb("facc", [128, NOWN, D], F32, esD)
                f1G = [bufY[:, i * 4 * SO:(i + 1) * 4 * SO].rearrange("p (a s) -> p a s", a=4) for i in range(2)]
                rl = [sb("rl%d" % i, [128, 512], F32, esD) for i in range(2)]
                xr = [sb("xr%d" % i, [128, D], F32, esD) for i in range(2)]
                junkE = sb("junkE", [128, D], BF16, esD)
                sse = sb("sse", [128, 4], F32, esD)
                NG = 8
                cnt = [0, 0]

                def final_block(i):
                    k = i % 2
                    sc.op("act", lambda h: h.activation(junkE[:], facc[:, i, :], AF.Square,
                                                        accum_out=sse[:, 2 * k:2 * k + 1]),
                          reads=[R("facc", i)], writes=[R("junkE"), R("sse", k)])
                    rstd_op(sse[:, 2 * k + 1:2 * k + 2], sse[:, 2 * k:2 * k + 1], D, R("sse", k), R("sse", k))
                    sc.dma("sp", xr[k][:], out[i * 128:(i + 1) * 128, :], reads=[R("out", i)], writes=[R("xr", k)])
                    sc.op("dve", lambda h: h.scalar_tensor_tensor(
                        facc[:, i, :], facc[:, i, :], sse[:, 2 * k + 1:2 * k + 2], gtg[:, D:2 * D], ALU.mult, ALU.mult),
                        reads=[R("facc", i), R("sse", k), R("gtg")], writes=[R("facc", i)])
                    sc.op("dve", lambda h: h.tensor_tensor(xr[k][:], xr[k][:], facc[:, i, :], ALU.add),
                          reads=[R("xr", k), R("facc", i)], writes=[R("xr", k)])
                    sc.dma("sp", out[i * 128:(i + 1) * 128, :], xr[k][:], reads=[R("xr", k)], writes=[R("out", i)])

                def ffn_load(g):
                    k = g % 2
                    sc.dma("pool", wf1[k][:], w_ff1[:, g * 512:(g + 1) * 512].rearrange("(kc p) n -> p kc n", p=128),
                           reads=[], writes=[R("wf1", k)])
                    sc.dma("pool", wf2[k][:], w_ff2[g * 512:(g + 1) * 512, :].rearrange("(c p) n -> p c n", p=128),
                           reads=[], writes=[R("wf2", k)])

                def ffn_f1(g):
                    k = g % 2
                    for c in range(4):
                        for tt in range(NT_OWN):
                            j = cnt[0] % 2
                            cnt[0] += 1
                            pz, PZ = ps[j], PS[j]
                            tsl = slice(tt * 512, (tt + 1) * 512)
                            sc.pe([lambda h, dc=dc, pz=pz, c=c, tsl=tsl, k=k: h.matmul(
                                pz[:], wf1[k][:, dc, c * 128:(c + 1) * 128], h2T[:, dc, tsl],
                                start=(dc == 0), stop=(dc == 7)) for dc in range(8)],
                                reads=[R("wf1", k), R("h2T")], writes=[PZ])
                            sc.op("act", lambda h, j=j, pz=pz: h.activation(rl[j][:], pz[:], AF.Relu),
                                  reads=[PZ], writes=[R("rl", j)])
                            sc.op("dve", lambda h, j=j, k=k, c=c, tsl=tsl: h.tensor_tensor(
                                f1G[k][:, c, tsl], rl[j][:], rl[j][:], ALU.mult),
                                reads=[R("rl", j)], writes=[R("f1G", k)])

                def ffn_f2(g):
                    k = g % 2
                    for i in range(NOWN):
                        for half in range(2):
                            j = 2 + cnt[1] % 4
                            cnt[1] += 1
                            pz, PZ = ps[j], PS[j]
                            sc.pe([lambda h, c=c, pz=pz, i=i, half=half, k=k: h.matmul(
                                pz[:], f1G[k][:, c, i * 128:(i + 1) * 128], wf2[k][:, c, half * 512:(half + 1) * 512],
                                start=(c == 0), stop=(c == 3)) for c in range(4)],
                                reads=[R("f1G", k), R("wf2", k)], writes=[PZ])
                            dst = facc[:, i, half * 512:(half + 1) * 512]
                            if g == 0:
                                sc.op("dve", lambda h, dst=dst, pz=pz: h.tensor_copy(dst, pz[:]),
                                      reads=[PZ], writes=[R("facc", i)])
                            else:
                                sc.op("dve", lambda h, dst=dst, pz=pz: h.tensor_tensor(dst, dst, pz[:], ALU.add),
                                      reads=[PZ, R("facc", i)], writes=[R("facc", i)])
                        if g == NG - 1:
                            final_block(i)

                ffn_f1(0)
                for g in range(NG):
                    if g + 1 < NG:
                        ffn_load(g + 1)
                        ffn_f1(g + 1)
                    ffn_f2(g)

                sc.barrier()
            esW.close()
    return nc


def _consts(hf):
    cbm = np.zeros((128, 5 * 128 + 3 * 512), np.float32)
    j = np.arange(128)[:, None]
    k = np.arange(128)[None, :]
    cbm[:, 0:128] = np.eye(128)
    cbm[:, 128:256] = np.where(j >= k, -1.0, 0.0)
    cbm[:, 256:384] = -1.0
    rot = np.zeros((128, 128), np.float32)
    for b0 in (0, 64):
        for d in range(32):
            rot[b0 + d + 32, b0 + d] = -1.0
            rot[b0 + d, b0 + d + 32] = 1.0
    cbm[:, 384:512] = rot
    m_sb = np.where(j < k, 0.0, NEG)
    m_df = np.where((j // 64) <= (k // 64), 0.0, NEG)
    cbm[:, 640:1152] = np.tile(m_sb, (1, 4))
    cbm[:, 1152:1664] = np.tile(m_df, (1, 4))
    cbm[:, 1664:2176] = NEG if hf == 0 else 0.0
    cfm = np.zeros((128, 260), np.float32)
    cfm[:, 0:128] = 1.0
    inv_freq = (np.float32(10000.0) ** (-(np.arange(0, 64, 2, dtype=np.float32)) / np.float32(64))).astype(np.float32)
    cfm[:, 256] = inv_freq[(np.arange(128) % 64) % 32]
    cfm[:, 257] = -math.pi
    cfm[:, 258] = -0.5
    cfm[:, 259] = EPS
    cfm[:, 128] = math.log(1.0 - LAMBDA_INIT)
    return cbm.astype(ml_dtypes.bfloat16), cfm


_NC_CACHE = {}


def kernel(x, c, positions, w_ada, b_ada, g_pre_mix, w_in, lambda_q1, lambda_k1, lambda_q2, lambda_k2,
           g_subln, w_branch_sb, w_branch_df, w_gate, b_gate, w_out, g_post_mix, g_pre_ffn, w_ff1, w_ff2,
           g_post_ffn):
    f = lambda a: np.ascontiguousarray(np.asarray(a, dtype=np.float32))
    x = f(x)
    B, S, _ = x.shape
    NBLK = S // 128
    NOWN = NBLK // 2
    positions = np.asarray(positions).astype(np.int32)
    if S not in _NC_CACHE:
        _NC_CACHE[S] = build(S)
    nc = _NC_CACHE[S]

    def col(v, n):
        return np.ascontiguousarray(f(v).reshape(n, 128).T)

    def rep(v):
        v = f(v).reshape(1, -1)
        return np.ascontiguousarray(np.broadcast_to(v, (128, v.shape[1])))

    shared = {
        "w_ada": f(w_ada[0]), "bada_col": col(b_ada[0], 48), "bada_rep": rep(b_ada[0]),
        "gpre_col": np.ascontiguousarray(np.concatenate([col(g_pre_mix[0], 8), col(g_pre_ffn[0], 8)], axis=1)),
        "gpost_rep": np.ascontiguousarray(np.concatenate([rep(g_post_mix[0]), rep(g_post_ffn[0])], axis=1)),
        "w_in": f(w_in[0]),
        "lam_rep": np.ascontiguousarray(np.concatenate(
            [rep(lambda_q1[0]), rep(lambda_k1[0]), rep(lambda_q2[0]), rep(lambda_k2[0])], axis=1)),
        "gsub_rep": rep(g_subln[0]), "w_bsb": f(w_branch_sb[0]), "w_bdf": f(w_branch_df[0]),
        "w_gate": f(w_gate[0]), "bgate_col": col(b_gate[0], 16), "w_out": f(w_out[0]),
        "w_ff1": f(w_ff1[0]), "w_ff2": f(w_ff2[0]),
    }
    in_maps = []
    owns = []
    for core in range(8):
        b, hf = core // 2, core % 2
        own = [2 * i + hf for i in range(NOWN)]
        if hf == 0:
            oth = [NBLK - 1] + [2 * o - 1 for o in range(1, NOWN)]
        else:
            oth = [2 * o for o in range(NOWN)]
        owns.append(own)
        blocks = own + oth
        tok = np.concatenate([np.arange(g * 128, (g + 1) * 128) for g in blocks])
        cbm, cfm = _consts(hf)
        m = dict(shared)
        m["x_p"] = np.ascontiguousarray(x[b][tok])
        m["pos_rep"] = np.ascontiguousarray(np.broadcast_to(positions[b][tok][None, :], (128, S))).astype(np.int32)
        m["c_col"] = col(np.asarray(c)[b], 8)
        m["cb"] = cbm
        m["cf"] = cfm
        in_maps.append(m)
    res = run_bass_kernel_spmd(nc, in_maps, core_ids=list(range(8)))
    outp = np.empty((B, S, D), np.float32)
    for core in range(8):
        b = core // 2
        o = np.asarray(res.results[core]["out"], dtype=np.float32)
        for i, gblk in enumerate(owns[core]):
            outp[b, gblk * 128:(gblk + 1) * 128, :] = o[i * 128:(i + 1) * 128, :]
    return outp
```
